# Optimizing a Trainium2 kernel written in Bass

```python
import math
import jax, jax.numpy as jnp
from jax import lax
import numpy as np

D_MODEL = 1024
BATCH = 16
SEQ = 2048
DEPTH = 4

GRID_W = 64
CTX_LEN = 256
N_EVEN = (DEPTH + 1) // 2
N_ODD = DEPTH // 2
EPS = 1e-6
CHUNK = 128
D_A = D_MODEL
A_GROUPS = 8
A_GROUP_DIM = D_A // A_GROUPS
D_B = D_MODEL
HYENA_ORDER = 2
SHORT_CONV = 3
FILTER_EMB = 33
FILTER_BANDS = (FILTER_EMB - 1) // 2
FILTER_HIDDEN = 64
DECAY_TARGET = 1e-2
FAST_DECAY_PCT = 0.3
SLOW_DECAY_PCT = 1.5
MIN_DECAY = math.log(DECAY_TARGET) / SLOW_DECAY_PCT
MAX_DECAY = math.log(DECAY_TARGET) / FAST_DECAY_PCT
D_IN_EVEN = 2 * D_A + (HYENA_ORDER + 1) * D_B
HEAD_DIM = 128
N_HEADS = D_MODEL // HEAD_DIM
N_KV_HEADS = 2
GQA_GROUP = N_HEADS // N_KV_HEADS
Q_BLOCK = 128
ROPE_THETA = 10000.0
D_QKV = (N_HEADS + 2 * N_KV_HEADS) * HEAD_DIM
D_FF = ((8 * D_MODEL + 3 * 256 - 1) // (3 * 256)) * 256

kernel_name = 'hybrid_gmlp_hyena_gqa_prefix_dit'


def rmsnorm(x, g):
    xf = x.astype(jnp.float32)
    y = xf * lax.rsqrt(jnp.mean(xf * xf, axis=-1, keepdims=True) + EPS)
    return (y * g.astype(jnp.float32)).astype(x.dtype)


def layernorm(x, g, b):
    xf = x.astype(jnp.float32)
    mu = jnp.mean(xf, axis=-1, keepdims=True)
    var = jnp.mean(jnp.square(xf - mu), axis=-1, keepdims=True)
    y = (xf - mu) * lax.rsqrt(var + 1e-5)
    return (y * g.astype(jnp.float32) + b.astype(jnp.float32)).astype(x.dtype)


def adaln(cond, w_mod, b_mod):
    return jnp.split(jax.nn.silu(cond) @ w_mod + b_mod, 6, axis=-1)


def modulate(x, g, shift, scale):
    return rmsnorm(x, g) * (1 + scale) + shift


def swiglu(h, w_gu, w_down):
    gate, up = jnp.split(h @ w_gu, 2, axis=-1)
    return (jax.nn.silu(gate) * up) @ w_down


def axial_rope_tables(n):
    rows = n // GRID_W
    row = jnp.repeat(jnp.arange(rows), GRID_W).astype(jnp.float32)
    col = jnp.tile(jnp.arange(GRID_W), rows).astype(jnp.float32)
    half = HEAD_DIM // 2
    inv = ROPE_THETA ** (-jnp.arange(0, half, 2, dtype=jnp.float32) / half)
    ang = jnp.concatenate([row[:, None] * inv, col[:, None] * inv], axis=-1)
    return jnp.cos(ang), jnp.sin(ang)


def apply_rope(x, cos, sin):
    shape = (x.shape[1],) + (1,) * (x.ndim - 3) + (cos.shape[-1],)
    cs, sn = cos.reshape(shape), sin.reshape(shape)
    x1, x2 = jnp.split(x, 2, axis=-1)
    return jnp.concatenate([x1 * cs - x2 * sn, x2 * cs + x1 * sn], axis=-1)


def hyena_filter_freqs(n, w1, b1, w2, b2, w3, freq):
    f32 = jnp.float32
    t = jnp.linspace(0.0, 1.0, n, dtype=f32)[:, None]
    w = (2.0 * math.pi / n) * jnp.arange(n, dtype=f32)[:, None]
    bands = jnp.linspace(1e-4, FILTER_BANDS - 1, FILTER_BANDS, dtype=f32)[None, :]
    ang = bands * w
    z = jnp.concatenate([t, jnp.cos(ang), -jnp.sin(ang)], axis=-1)
    freq = freq.astype(f32)
    hid = jnp.sin(freq[0] * (z @ w1.astype(f32) + b1.astype(f32)))
    hid = jnp.sin(freq[1] * (hid @ w2.astype(f32) + b2.astype(f32)))
    h = (hid @ w3.astype(f32)).reshape(n, 2, HYENA_ORDER, D_B)
    deltas = jnp.abs(jnp.linspace(MIN_DECAY, MAX_DECAY, D_B, dtype=f32))
    h = h * jnp.exp(-t[:, :, None, None] * deltas)
    h_circ = jnp.concatenate(
        [h[:, 0], jnp.zeros((1, HYENA_ORDER, D_B), f32), h[:0:-1, 1]], axis=0)
    h_circ = h_circ / jnp.sum(jnp.abs(h_circ), axis=0, keepdims=True)
    return jnp.fft.rfft(h_circ, axis=0)


def bidir_long_conv(z, h_freq, skip):
    n = z.shape[1]
    zf = z.astype(jnp.float32)
    y = jnp.fft.irfft(jnp.fft.rfft(zf, n=2 * n, axis=1) * h_freq, n=2 * n, axis=1)[:, :n]
    return (y + zf * skip.astype(jnp.float32)).astype(z.dtype)


def centred_short_conv(z, w, b):
    n = z.shape[1]
    pad = SHORT_CONV // 2
    zp = jnp.pad(z, ((0, 0), (pad, pad), (0, 0)))
    return sum(zp[:, k:k + n] * w[k] for k in range(SHORT_CONV)) + b


def gmlp_hyena_mixer(h, h_freq, w_in, ln_g, ln_b, w_s, b_s, conv_w, conv_b, skip, w_out):
    bsz, n, _ = h.shape
    proj = h @ w_in
    u, v = jnp.split(jax.nn.gelu(proj[..., :2 * D_A], approximate=False), 2, axis=-1)
    v = layernorm(v, ln_g, ln_b).reshape(bsz, n // CHUNK, CHUNK, A_GROUPS, A_GROUP_DIM)
    sv = jnp.einsum('gpq,bkqgc->bkpgc', w_s, v) + b_s.T[None, None, :, :, None]
    y_a = u * sv.reshape(bsz, n, D_A)
    hb = centred_short_conv(proj[..., 2 * D_A:], conv_w, conv_b)
    vb, x1, x2 = jnp.split(hb, 3, axis=-1)
    z = x1 * bidir_long_conv(vb, h_freq[:, 0], skip[0])
    z = x2 * bidir_long_conv(z, h_freq[:, 1], skip[1])
    return jnp.concatenate([y_a, z], axis=-1) @ w_out


def gqa_block(q, k, v):
    s = jnp.einsum('bqkgd,bskd->bkgqs', q, k) * (HEAD_DIM ** -0.5)
    p = jax.nn.softmax(s, axis=-1)
    return jnp.einsum('bkgqs,bskd->bqkgd', p, v)


def attention_mixer(h_lat, h_ctx, w_qkv, q_g, k_g, w_o, cos, sin, need_ctx_out):
    def project(h, with_q):
        bsz, n, _ = h.shape
        nq = N_HEADS * HEAD_DIM
        nk = N_KV_HEADS * HEAD_DIM
        w = w_qkv if with_q else w_qkv[:, nq:]
        qkv = h @ w
        off = nq if with_q else 0
        k = rmsnorm(qkv[..., off:off + nk].reshape(bsz, n, N_KV_HEADS, HEAD_DIM), k_g).astype(jnp.float32)
        v = qkv[..., off + nk:].reshape(bsz, n, N_KV_HEADS, HEAD_DIM).astype(jnp.float32)
        q = None
        if with_q:
            q = rmsnorm(qkv[..., :nq].reshape(bsz, n, N_KV_HEADS, GQA_GROUP, HEAD_DIM), q_g).astype(jnp.float32)
        return q, k, v

    bsz, n, _ = h_lat.shape
    q_l, k_l, v_l = project(h_lat, True)
    q_l, k_l = apply_rope(q_l, cos, sin), apply_rope(k_l, cos, sin)
    q_c, k_c, v_c = project(h_ctx, need_ctx_out)
    k_all = jnp.concatenate([k_c, k_l], axis=1)
    v_all = jnp.concatenate([v_c, v_l], axis=1)
    n_blk = n // Q_BLOCK
    q_blocks = q_l.reshape(bsz, n_blk, Q_BLOCK, N_KV_HEADS, GQA_GROUP, HEAD_DIM).swapaxes(0, 1)
    o_l = lax.map(lambda qb: gqa_block(qb, k_all, v_all), q_blocks)
    o_l = o_l.swapaxes(0, 1).reshape(bsz, n, N_HEADS * HEAD_DIM).astype(h_lat.dtype) @ w_o
    o_c = None
    if need_ctx_out:
        n_c = h_ctx.shape[1]
        o_c = gqa_block(q_c, k_c, v_c).reshape(bsz, n_c, N_HEADS * HEAD_DIM).astype(h_ctx.dtype) @ w_o
    return o_l, o_c


def setup_inputs(seed: int = 0) -> dict:
    key = jax.random.key(seed)
    ks = iter(jax.random.split(key, 40))
    f32 = jnp.float32

    def nrm(shape, scale):
        return jax.random.normal(next(ks), shape, f32) * scale

    def gain(shape):
        return 1.0 + nrm(shape, 0.01)

    return {
        'x': nrm((BATCH, SEQ, D_MODEL), 1.0),
        'c': nrm((BATCH, D_MODEL), 1.0),
        'ctx': nrm((BATCH, CTX_LEN, D_MODEL), 1.0),
        'c_ctx': nrm((D_MODEL,), 1.0),
        'mod_w': nrm((DEPTH, D_MODEL, 6 * D_MODEL), 0.5 * D_MODEL ** -0.5),
        'mod_b': nrm((DEPTH, 6 * D_MODEL), 0.01),
        'norm1_g': gain((DEPTH, D_MODEL)),
        'norm2_g': gain((DEPTH, D_MODEL)),
        'ffn_w_gu': nrm((DEPTH, D_MODEL, 2 * D_FF), D_MODEL ** -0.5),
        'ffn_w_down': nrm((DEPTH, D_FF, D_MODEL), D_FF ** -0.5),
        'even_w_in': nrm((N_EVEN, D_MODEL, D_IN_EVEN), D_MODEL ** -0.5),
        'gmlp_ln_g': gain((N_EVEN, D_A)),
        'gmlp_ln_b': nrm((N_EVEN, D_A), 0.01),
        'gmlp_w_s': nrm((N_EVEN, A_GROUPS, CHUNK, CHUNK), CHUNK ** -0.5),
        'gmlp_b_s': gain((N_EVEN, A_GROUPS, CHUNK)),
        'hyena_conv_w': nrm((N_EVEN, SHORT_CONV, (HYENA_ORDER + 1) * D_B), SHORT_CONV ** -0.5),
        'hyena_conv_b': nrm((N_EVEN, (HYENA_ORDER + 1) * D_B), 0.01),
        'hyena_f_w1': nrm((N_EVEN, FILTER_EMB, FILTER_HIDDEN), FILTER_EMB ** -0.5),
        'hyena_f_b1': nrm((N_EVEN, FILTER_HIDDEN), 0.1),
        'hyena_f_w2': nrm((N_EVEN, FILTER_HIDDEN, FILTER_HIDDEN), FILTER_HIDDEN ** -0.5),
        'hyena_f_b2': nrm((N_EVEN, FILTER_HIDDEN), 0.1),
        'hyena_f_w3': nrm((N_EVEN, FILTER_HIDDEN, 2 * HYENA_ORDER * D_B), FILTER_HIDDEN ** -0.5),
        'hyena_freq': gain((N_EVEN, 2, FILTER_HIDDEN)),
        'hyena_skip': nrm((N_EVEN, HYENA_ORDER, D_B), 0.1),
        'even_w_out': nrm((N_EVEN, D_A + D_B, D_MODEL), (D_A + D_B) ** -0.5),
        'attn_w_qkv': nrm((N_ODD, D_MODEL, D_QKV), D_MODEL ** -0.5),
        'attn_q_g': gain((N_ODD, HEAD_DIM)),
        'attn_k_g': gain((N_ODD, HEAD_DIM)),
        'attn_w_o': nrm((N_ODD, N_HEADS * HEAD_DIM, D_MODEL), (N_HEADS * HEAD_DIM) ** -0.5),
        'final_g': gain((D_MODEL,)),
    }


def reference(x, c, ctx, c_ctx, mod_w, mod_b, norm1_g, norm2_g, ffn_w_gu, ffn_w_down,
              even_w_in, gmlp_ln_g, gmlp_ln_b, gmlp_w_s, gmlp_b_s, hyena_conv_w, hyena_conv_b,
              hyena_f_w1, hyena_f_b1, hyena_f_w2, hyena_f_b2, hyena_f_w3, hyena_freq, hyena_skip,
              even_w_out, attn_w_qkv, attn_q_g, attn_k_g, attn_w_o, final_g):
    n = x.shape[1]
    n_ctx = ctx.shape[1]
    cos, sin = axial_rope_tables(n)
    for layer in range(DEPTH):
        last = layer == DEPTH - 1
        is_even = layer % 2 == 0
        ctx_needed = (not last) or (not is_even)
        sh1, sc1, g1, sh2, sc2, g2 = adaln(c[:, None, :], mod_w[layer], mod_b[layer])
        h = modulate(x, norm1_g[layer], sh1, sc1)
        if ctx_needed:
            csh1, csc1, cg1, csh2, csc2, cg2 = adaln(c_ctx, mod_w[layer], mod_b[layer])
            hc = modulate(ctx, norm1_g[layer], csh1, csc1)
        if is_even:
            i = layer // 2
            filt = (hyena_f_w1[i], hyena_f_b1[i], hyena_f_w2[i], hyena_f_b2[i], hyena_f_w3[i], hyena_freq[i])
            prm = (even_w_in[i], gmlp_ln_g[i], gmlp_ln_b[i], gmlp_w_s[i], gmlp_b_s[i],
                   hyena_conv_w[i], hyena_conv_b[i], hyena_skip[i], even_w_out[i])
            y = gmlp_hyena_mixer(h, hyena_filter_freqs(n, *filt), *prm)
            yc = gmlp_hyena_mixer(hc, hyena_filter_freqs(n_ctx, *filt), *prm) if not last else None
        else:
            j = layer // 2
            y, yc = attention_mixer(h, hc, attn_w_qkv[j], attn_q_g[j], attn_k_g[j], attn_w_o[j],
                                    cos, sin, not last)
        x = x + g1 * y
        x = x + g2 * swiglu(modulate(x, norm2_g[layer], sh2, sc2), ffn_w_gu[layer], ffn_w_down[layer])
        if not last:
            ctx = ctx + cg1 * yc
            ctx = ctx + cg2 * swiglu(modulate(ctx, norm2_g[layer], csh2, csc2),
                                     ffn_w_gu[layer], ffn_w_down[layer])
    return rmsnorm(x, final_g)
```

```python
import math
import os
import numpy as np
import ml_dtypes
import concourse.bass as bass
import concourse.mybir as mybir
from concourse.bass_utils import run_bass_kernel_spmd

F32 = mybir.dt.float32
BF16 = mybir.dt.bfloat16
AF = mybir.ActivationFunctionType
ALU = mybir.AluOpType

EPOCH = 30000


class Buf:
    __slots__ = ("name", "w", "rs", "dsem", "dcnt")

    def __init__(self, name):
        self.name = name
        self.w = {}
        self.rs = {}
        self.dsem = None
        self.dcnt = 0


class Prog:
    ENGS = ("pe", "act", "dve", "pool", "sp")

    def __init__(self, nc):
        self.nc = nc
        self.ops = {e: [] for e in self.ENGS}
        self.cnt = {e: 0 for e in self.ENGS}
        self.esems = {e: [] for e in self.ENGS}
        self.seen = {e: {} for e in self.ENGS}
        self.semobjs = {}
        self.dma_bufs = []
        self.named = {}

    def dbuf(self, name):
        b = self.named.get(name)
        if b is None:
            b = Buf(name)
            self.named[name] = b
        return b

    def _newsem(self, name):
        cm = self.nc.semaphore(name)
        s = cm.__enter__()
        self.semobjs[name] = (s, cm)
        return name

    def _esem(self, eng, epoch):
        while len(self.esems[eng]) <= epoch:
            self.esems[eng].append(self._newsem(f"e_{eng}_{len(self.esems[eng])}"))
        return self.esems[eng][epoch]

    def _deps(self, eng, reads, writes):
        out = {}

        def add(key, val):
            if out.get(key, 0) < val:
                out[key] = val

        for b in reads:
            for key, (val, src) in b.w.items():
                if src != eng or eng in ("act", "dve", "pool", "__dma__"):
                    add(key, val)
        strict = eng in ("act", "dve", "pool", "__dma__")
        for b in writes:
            for key, (val, src) in b.w.items():
                if src != eng or strict:
                    add(key, val)
            for key, (val, src) in b.rs.items():
                if src != eng or strict:
                    add(key, val)
        return out

    def _filter_waits(self, eng, deps):
        seen = self.seen[eng]
        waits = []
        for key, val in deps.items():
            if seen.get(key, 0) >= val:
                continue
            seen[key] = val
            waits.append((key, val))
        return waits

    def _record(self, event, reads, writes):
        key, val, src = event
        for b in reads:
            b.rs[key] = (val, src)
        for b in writes:
            b.w[key] = (val, src)
            b.rs = {}

    def op(self, eng, fn, reads=(), writes=()):
        deps = self._deps(eng, reads, writes)
        waits = self._filter_waits(eng, deps)
        n = self.cnt[eng]
        epoch, idx = divmod(n, EPOCH)
        key = self._esem(eng, epoch)
        self.cnt[eng] = n + 1
        self.ops[eng].append((waits, fn, key, 1))
        self._record((key, idx + 1, eng), reads, writes)

    def dma(self, out_ap, in_ap, reads=(), writes=(), q="sp", **kw):
        deps = self._deps("__dma__", reads, writes)
        waits = self._filter_waits(q, deps)
        dst = writes[0]
        if dst.dsem is None:
            dst.dsem = self._newsem(f"d{len(self.dma_bufs)}_{dst.name}")
            self.dma_bufs.append(dst)
        dst.dcnt += 16

        def fn(e, out_ap=out_ap, in_ap=in_ap, kw=kw):
            return e.dma_start(out=out_ap, in_=in_ap, **kw)

        self.ops[q].append((waits, fn, dst.dsem, 16))
        self._record((dst.dsem, dst.dcnt, "__dma__"), reads, writes)

    def barrier(self):
        for eng in self.ENGS:
            deps = {}
            for e2 in self.ENGS:
                if e2 == eng or self.cnt[e2] == 0:
                    continue
                epoch, idx = divmod(self.cnt[e2] - 1, EPOCH)
                deps[self.esems[e2][epoch]] = idx + 1
            for b in self.dma_bufs:
                deps[b.dsem] = b.dcnt
            waits = self._filter_waits(eng, deps)
            if waits:
                self.ops[eng].append((waits, None, None, 0))

    def emit(self):
        nc = self.nc
        sem = {k: v[0] for k, v in self.semobjs.items()}

        def run(e, lst):
            for waits, fn, key, inc in lst:
                for (k, v) in waits:
                    e.wait_ge(sem[k], v)
                if fn is not None:
                    fn(e).then_inc(sem[key], inc)

        with nc.Block() as block:
            @block.sync
            def _(e):
                run(e, self.ops["sp"])

            @block.tensor
            def _(e):
                run(e, self.ops["pe"])

            @block.scalar
            def _(e):
                run(e, self.ops["act"])

            @block.vector
            def _(e):
                run(e, self.ops["dve"])

            @block.gpsimd
            def _(e):
                run(e, self.ops["pool"])


class Arena:
    def __init__(self, ap_f32, nbytes):
        self.ap = ap_f32
        self.nbytes = nbytes
        self.off = 0
        self.marks = []
        self.peak = 0

    def alloc(self, shape_free, dtype):
        esz = 2 if dtype == BF16 else 4
        n = 1
        for s in shape_free:
            n *= s
        nb = (n * esz + 63) // 64 * 64
        assert self.off + nb <= self.nbytes, f"arena overflow: {self.off}+{nb}>{self.nbytes}"
        a = self.ap[:, self.off // 4:(self.off + nb) // 4]
        self.off += nb
        self.peak = max(self.peak, self.off)
        if dtype == BF16:
            a = a.bitcast(BF16)
        a = a[:, 0:n]
        if len(shape_free) == 2:
            a = a.rearrange("p (a b) -> p a b", a=shape_free[0])
        elif len(shape_free) == 3:
            a = a.rearrange("p (a b c) -> p a b c", a=shape_free[0], b=shape_free[1])
        elif len(shape_free) == 4:
            a = a.rearrange("p (a b c d) -> p a b c d", a=shape_free[0], b=shape_free[1], c=shape_free[2])
        return a

    def mark(self):
        self.marks.append(self.off)

    def release(self):
        self.off = self.marks.pop()


D = 1024
KC = 8
NL = 2048
NCX = 256
T = NL + NCX
DFF = 2816
FCH = 22
DEPTH = 4
EPS = 1e-6
HL = 1
HC = NL + 3
HW = NL + NCX + 4
GRID_W = 64
HEAD_DIM = 128
MAGIC = 12582912.0
TWO_PI = 2.0 * math.pi
CB = 256
NCB = D // CB

_CONST_CACHE = {}


def _bf16(a):
    return np.ascontiguousarray(a.astype(ml_dtypes.bfloat16))


def _dft_tables(n):
    N = 2 * n
    nf = n // 128 + 1
    idx = np.arange(nf * 128, dtype=np.int64)
    prod = (idx[:, None] * idx[None, :]) % N
    ang = prod.astype(np.float64) * (2.0 * np.pi / N)
    valid = (idx[:, None] <= n) & (idx[None, :] <= n)
    c = np.where(valid, np.cos(ang), 0.0)
    s = np.where(valid, np.sin(ang), 0.0)
    c4 = c.reshape(nf, 128, nf, 128).transpose(2, 1, 0, 3)
    s4 = s.reshape(nf, 128, nf, 128).transpose(2, 1, 0, 3)
    return _bf16(c4), _bf16(s4)


def _consts():
    if _CONST_CACHE:
        return _CONST_CACHE
    C = {}
    for n, tag in ((NL, "L"), (NCX, "C")):
        tc, ts = _dft_tables(n)
        C["tcos" + tag] = tc
        C["tsin" + tag] = ts
        nt = n // 128
        nf = nt + 1
        t = np.linspace(0.0, 1.0, n, dtype=np.float32)
        w = (np.float32(2.0 * math.pi / n) * np.arange(n, dtype=np.float32))
        bands = np.linspace(1e-4, 15.0, 16, dtype=np.float32)
        ang = bands[None, :] * w[:, None]
        z = np.concatenate([t[:, None], np.cos(ang), -np.sin(ang)], axis=-1).astype(np.float32)
        C["zT" + tag] = np.ascontiguousarray(z.T)
        C["negt" + tag] = np.ascontiguousarray((-t).reshape(nt, 128).T)
        wf = np.full(nf * 128, 2.0, dtype=np.float64)
        wf[0] = 1.0
        wf[n] = 1.0
        wf[n + 1:] = 0.0
        C["wfn" + tag] = np.ascontiguousarray((wf / (2 * n)).astype(np.float32).reshape(nf, 128).T)
    min_decay = math.log(1e-2) / 1.5
    max_decay = math.log(1e-2) / 0.3
    C["delta"] = np.abs(np.linspace(min_decay, max_decay, D, dtype=np.float32)).astype(np.float32)
    rows = NL // GRID_W
    row = np.repeat(np.arange(rows), GRID_W).astype(np.float32)
    col = np.tile(np.arange(GRID_W), rows).astype(np.float32)
    half = HEAD_DIM // 2
    inv = (10000.0 ** (-np.arange(0, half, 2, dtype=np.float32) / half)).astype(np.float32)
    ang = np.concatenate([row[:, None] * inv, col[:, None] * inv], axis=-1)
    cos = np.cos(ang).astype(np.float32)
    sin = np.sin(ang).astype(np.float32)
    C["ropec"] = np.ascontiguousarray(np.concatenate([cos, cos], axis=1).T)
    C["ropes"] = np.ascontiguousarray(np.concatenate([sin, sin], axis=1).T)
    rp = np.zeros((128, 128), dtype=np.float32)
    for m in range(64):
        rp[m + 64, m] = -1.0
        rp[m, m + 64] = 1.0
    C["rperm"] = rp
    C["ident"] = np.eye(128, dtype=np.float32)
    sh = np.zeros((4, 128, 128), dtype=np.float32)
    for t in range(1, 128):
        sh[0, t - 1, t] = 1.0
    sh[1, 127, 0] = 1.0
    for t in range(0, 127):
        sh[2, t + 1, t] = 1.0
    sh[3, 0, 127] = 1.0
    C["shiftm"] = np.ascontiguousarray(sh.transpose(1, 0, 2))
    _CONST_CACHE.update(C)
    return C


def build(nlayers=DEPTH, nbatch=2, final_norm=True, layers=None):
    layers = list(range(nlayers)) if layers is None else list(layers)
    nc = bass.Bass("TRN2", target_bir_lowering=False)
    dr = {}

    def din(name, shape, dt=F32):
        dr[name] = nc.dram_tensor(name, list(shape), dt, kind="ExternalInput").ap()
        return dr[name]

    din("xT", [2, D, NL])
    din("ctxT", [2, D, NCX])
    din("cvec", [128, KC, 3])
    din("mod_w", [DEPTH, D, 6 * D])
    din("modb", [128, DEPTH, 6, KC])
    din("n1g", [128, DEPTH, KC])
    din("n2g", [128, DEPTH, KC])
    din("fing", [128, KC])
    din("w_gu", [DEPTH, D, 2 * DFF])
    din("w_down", [DEPTH, DFF, D])
    din("w_in", [2, D, 5 * D])
    din("ln_g", [2, D])
    din("ln_b", [2, D])
    din("w_sT", [2, 128, 8, 128])
    din("b_s", [2, 8 * 128])
    din("conv_w", [2, 3, 3 * D])
    din("conv_b", [2, 3 * D])
    din("f_w1", [2, 33, 64])
    din("f_pp", [2, 64, 4])
    din("f_w2", [2, 64, 64])
    din("f_w3", [2, 64, 4 * D])
    din("skip", [2, 2, D])
    din("w_out", [2, 2 * D, D])
    din("w_qkv", [2, D, 1536])
    din("qkg", [2, 128, 2])
    din("w_o", [2, D, D])
    for tag, n in (("L", NL), ("C", NCX)):
        nf = n // 128 + 1
        din("tcos" + tag, [nf, 128, nf, 128], BF16)
        din("tsin" + tag, [nf, 128, nf, 128], BF16)
        din("zT" + tag, [33, n])
        din("negt" + tag, [128, n // 128])
        din("wfn" + tag, [128, nf])
    din("delta", [D])
    din("ropec", [128, NL])
    din("ropes", [128, NL])
    din("rperm", [128, 128])
    din("ident", [128, 128])
    din("shiftm", [128, 4, 128])
    outT = nc.dram_tensor("outT", [2, D, NL], F32, kind="ExternalOutput").ap()
    Hs = {}
    for i in range(2):
        for tag, n in (("L", NL), ("C", NCX)):
            nf = n // 128 + 1
            Hs[(i, tag)] = nc.dram_tensor(f"Hs{i}{tag}", [2, nf, NCB, 128, 2, CB], F32, kind="Internal").ap()

    ARENA_BYTES = 211968
    cm_a = nc.sbuf_tensor("arena", [128, ARENA_BYTES // 4], F32)
    arena_t = cm_a.__enter__()
    cm_p = nc.psum_tensor("psum", [128, 8, 512], F32)
    ps = cm_p.__enter__()
    A = Arena(arena_t[:, :], ARENA_BYTES)
    P = Prog(nc)
    PB = [Buf(f"psb{i}") for i in range(8)]

    def MM(out, lhsT, rhs, start, stop, R, W):
        P.op("pe", lambda e: e.matmul(out, lhsT, rhs, start=start, stop=stop), R, W)

    def TR(out, in_, ident_ap, R, W):
        P.op("pe", lambda e: e.transpose(out, in_, ident_ap), R, W)

    def ACT(out, in_, func, R, W, bias=None, scale=None, accum=None):
        kw = {}
        if bias is not None:
            kw["bias"] = bias
        if scale is not None:
            kw["scale"] = scale
        if accum is not None:
            kw["accum_out"] = accum
        P.op("act", lambda e: e.activation(out=out, in_=in_, func=func, **kw), R, W)

    def TT(eng, out, a, b, op, R, W):
        P.op(eng, lambda e: e.tensor_tensor(out=out, in0=a, in1=b, op=op), R, W)

    def TS(out, a, s1, s2, op0, op1, R, W, eng="dve"):
        P.op(eng, lambda e: e.tensor_scalar(out, a, s1, s2, op0, op1), R, W)

    def STT(out, in0, scalar, in1, op0, op1, R, W):
        P.op("dve", lambda e: e.scalar_tensor_tensor(out=out, in0=in0, scalar=scalar, in1=in1, op0=op0, op1=op1), R, W)

    def CP(eng, out, in_, R, W):
        P.op(eng, lambda e: e.tensor_copy(out=out, in_=in_), R, W)

    def MSET(eng, out, val, W):
        P.op(eng, lambda e: e.memset(out, val), (), W)

    def RECIP(out, in_, R, W):
        P.op("dve", lambda e: e.reciprocal(out=out, in_=in_), R, W)

    def wview(w2d, k0, kc, c0, ncol):
        return w2d[k0:k0 + kc * 128, c0:c0 + ncol].rearrange("(kc p) c -> p kc c", p=128)

    ident = A.alloc([128], BF16); b_ident = Buf("ident")
    ones_bf = A.alloc([128], BF16); b_ones = Buf("ones")
    ones_f = A.alloc([128], F32)
    rperm = A.alloc([128], BF16); b_rperm = Buf("rperm")
    cbias = A.alloc([4], F32); b_cbias = Buf("cbias")
    modv = A.alloc([DEPTH, 3, 6, KC], F32); b_modv = Buf("modv")
    fing = A.alloc([KC], F32); b_fing = Buf("fing")
    n1g = A.alloc([DEPTH, KC], F32); n2g = A.alloc([DEPTH, KC], F32); b_n1g = Buf("n1g"); b_n2g = Buf("n2g")
    shiftm = A.alloc([4, 128], BF16); b_shiftm = Buf("shiftm")
    P.dma(shiftm, dr["shiftm"], writes=[b_shiftm], q="pool")
    P.dma(ident, dr["ident"], writes=[b_ident], q="pool")
    P.dma(rperm, dr["rperm"], writes=[b_rperm], q="pool")
    P.dma(n1g, dr["n1g"], writes=[b_n1g])
    P.dma(n2g, dr["n2g"], writes=[b_n2g])
    P.dma(fing, dr["fing"], writes=[b_fing])
    MSET("pool", ones_f, 1.0, [b_ones])
    MSET("pool", ones_bf, 1.0, [b_ones])
    MSET("pool", cbias[:, 0:1], D * EPS, [b_cbias])
    MSET("pool", cbias[:, 1:2], 128 * EPS, [b_cbias])
    MSET("pool", cbias[:, 2:3], 1e-5, [b_cbias])
    MSET("pool", cbias[:, 3:4], 0.0, [b_cbias])

    A.mark()
    scT = A.alloc([KC, 3], F32); b_sc = Buf("scT")
    modb = A.alloc([DEPTH, 6, KC], F32); b_modb = Buf("modb")
    P.dma(scT, dr["cvec"], writes=[b_sc])
    P.dma(modb, dr["modb"], writes=[b_modb])
    ACT(scT, scT, AF.Silu, [b_sc], [b_sc])
    scTb = A.alloc([KC, 3], BF16); b_scb = Buf("scTb")
    CP("dve", scTb, scT, [b_sc], [b_scb])
    mwt = [A.alloc([KC, 512], BF16) for _ in range(3)]
    b_mw = [P.dbuf(f"mw{i}") for i in range(3)]
    it = 0
    for l in layers:
        for j in range(12):
            s = it % 3
            P.dma(mwt[s], wview(dr["mod_w"][l], 0, KC, j * 512, 512), writes=[b_mw[s]], q="pool")
            six = j // 2
            for cc in range(4):
                chunk = (j % 2) * 4 + cc
                pb = PB[(it * 4 + cc) % 8]
                for k in range(KC):
                    MM(ps[:, (it * 4 + cc) % 8, 0:3], mwt[s][:, k, cc * 128:(cc + 1) * 128], scTb[:, k, :],
                       k == 0, k == KC - 1, [b_mw[s], b_scb], [pb])
                TS(modv[:, l, :, six, chunk], ps[:, (it * 4 + cc) % 8, 0:3], modb[:, l, six, chunk:chunk + 1], 0.0,
                   ALU.add, ALU.add, [pb, b_modb], [b_modv])
            it += 1
        for r in range(3):
            TS(modv[:, l, r, 1, :], modv[:, l, r, 1, :], 1.0, 32.0, ALU.add, ALU.mult, [b_modv], [b_modv])
            TT("dve", modv[:, l, r, 1, :], modv[:, l, r, 1, :], n1g[:, l, :], ALU.mult, [b_modv, b_n1g], [b_modv])
            TS(modv[:, l, r, 4, :], modv[:, l, r, 4, :], 1.0, 32.0, ALU.add, ALU.mult, [b_modv], [b_modv])
            TT("dve", modv[:, l, r, 4, :], modv[:, l, r, 4, :], n2g[:, l, :], ALU.mult, [b_modv, b_n2g], [b_modv])
    TS(fing, fing, 32.0, 0.0, ALU.mult, ALU.add, [b_fing], [b_fing])
    P.barrier()
    A.release()

    b_Hs = {k: Buf(f"Hs{k}") for k in Hs}
    even_idx = sorted({l // 2 for l in layers if l % 2 == 0})

    def filter_precompute(i, tag, n):
        nt = n // 128
        nf = nt + 1
        A.mark()
        zT = A.alloc([n], F32); b_z = P.dbuf("f_z")
        w1 = A.alloc([64], F32); b_w1 = P.dbuf("f_w1")
        w2 = A.alloc([64], F32); b_w2 = P.dbuf("f_w2")
        w3 = A.alloc([4 * D], F32); b_w3 = P.dbuf("f_w3")
        fpp = A.alloc([6], F32); b_fpp = P.dbuf("f_pp")
        negt = A.alloc([nt], F32); b_negt = P.dbuf("f_negt")
        wfn = A.alloc([nf], F32); b_wfn = P.dbuf("f_wfn")
        dl = A.alloc([D], F32); b_dl = P.dbuf("f_dl")
        hid1 = A.alloc([n], F32); b_h1 = Buf("hid1")
        hid2 = A.alloc([n], F32); b_h2 = Buf("hid2")
        P.dma(zT[0:33, :], dr["zT" + tag], writes=[b_z])
        P.dma(w1[0:33, :], dr["f_w1"][i], writes=[b_w1])
        P.dma(w2[0:64, :], dr["f_w2"][i], writes=[b_w2])
        P.dma(w3[0:64, :], dr["f_w3"][i], writes=[b_w3])
        P.dma(fpp[0:64, 0:4], dr["f_pp"][i], writes=[b_fpp])
        P.dma(negt, dr["negt" + tag], writes=[b_negt])
        P.dma(wfn, dr["wfn" + tag], writes=[b_wfn])
        P.dma(dl, dr["delta"].partition_broadcast(128), writes=[b_dl])
        TT("dve", fpp[0:64, 4:5], fpp[0:64, 0:1], fpp[0:64, 1:2], ALU.mult, [b_fpp], [b_fpp])
        TT("dve", fpp[0:64, 5:6], fpp[0:64, 2:3], fpp[0:64, 3:4], ALU.mult, [b_fpp], [b_fpp])
        tmpa = A.alloc([512], F32); b_ta = Buf("tmpa")
        tmpb = A.alloc([512], F32); b_tb = Buf("tmpb")
        bw = min(512, n)
        for stage in range(2):
            src, b_src, kk = (zT, b_z, 33) if stage == 0 else (hid1, b_h1, 64)
            wt, b_wt = (w1, b_w1) if stage == 0 else (w2, b_w2)
            dst, b_dst = (hid1, b_h1) if stage == 0 else (hid2, b_h2)
            fcol = 0 if stage == 0 else 2
            for blk in range(n // bw):
                cs = slice(blk * bw, (blk + 1) * bw)
                pb = PB[blk % 2]
                MM(ps[0:64, blk % 2, 0:bw], wt[0:kk, 0:64], src[0:kk, cs], True, True, [b_wt, b_src], [pb])
                TS(tmpa[0:64, 0:bw], ps[0:64, blk % 2, 0:bw], fpp[0:64, fcol:fcol + 1], fpp[0:64, 4 + stage:5 + stage],
                   ALU.mult, ALU.add, [pb, b_fpp], [b_ta])
                TS(tmpb[0:64, 0:bw], tmpa[0:64, 0:bw], 1.0 / TWO_PI, MAGIC, ALU.mult, ALU.add, [b_ta], [b_tb])
                TS(tmpb[0:64, 0:bw], tmpb[0:64, 0:bw], MAGIC, -TWO_PI, ALU.subtract, ALU.mult, [b_tb], [b_tb])
                TT("dve", tmpa[0:64, 0:bw], tmpa[0:64, 0:bw], tmpb[0:64, 0:bw], ALU.add, [b_ta, b_tb], [b_ta])
                ACT(dst[0:64, cs], tmpa[0:64, 0:bw], AF.Sin, [b_ta], [b_dst])
        hsum = A.alloc([nt, D], BF16); b_hs = Buf("hsum")
        hdif = A.alloc([nt, D], BF16); b_hd = Buf("hdif")
        hw_ = [A.alloc([2, D], F32) for _ in range(2)]; b_hw_ = [Buf("hw0"), Buf("hw1")]
        habs_ = [A.alloc([2, D], BF16) for _ in range(2)]; b_ha_ = [Buf("habs0"), Buf("habs1")]
        win_ = [A.alloc([D], F32) for _ in range(2)]; b_win_ = [Buf("win0"), Buf("win1")]
        rl1 = A.alloc([D], F32); b_rl1 = Buf("rl1")
        tct = [A.alloc([nf, 128], BF16) for _ in range(2)]
        tst = [A.alloc([nf, 128], BF16) for _ in range(2)]
        b_tc = [P.dbuf(f"tc{s}") for s in range(2)]
        b_ts = [P.dbuf(f"ts{s}") for s in range(2)]
        hst = [A.alloc([2, D], F32) for _ in range(2)]
        b_hst = [Buf(f"hst{s}") for s in range(2)]
        hsc = Hs[(i, tag)]
        git = 0
        for o in range(2):
            for tc in range(nt):
                hw, b_hw = hw_[tc % 2], b_hw_[tc % 2]
                habs, b_ha = habs_[tc % 2], b_ha_[tc % 2]
                win, b_win = win_[tc % 2], b_win_[tc % 2]
                ACT(win, dl, AF.Exp, [b_dl, b_negt], [b_win], scale=negt[:, tc:tc + 1])
                for dirn in range(2):
                    for ch in range(2):
                        col0 = dirn * 2 * D + o * D + ch * 512
                        bank = (2 if tc % 2 == 0 else 6) + (dirn * 2 + ch) % 2
                        MM(ps[:, bank, :], hid2[0:64, tc * 128:(tc + 1) * 128], w3[0:64, col0:col0 + 512], True, True,
                           [b_h2, b_w3], [PB[bank]])
                        TT("dve", hw[:, dirn, ch * 512:(ch + 1) * 512], ps[:, bank, :], win[:, ch * 512:(ch + 1) * 512],
                           ALU.mult, [PB[bank], b_win], [b_hw])
                if tc == 0:
                    MSET("dve", hw[0:1, 1, :], 0.0, [b_hw])
                TT("pool", hsum[:, tc, :], hw[:, 0, :], hw[:, 1, :], ALU.add, [b_hw], [b_hs])
                TT("pool", hdif[:, tc, :], hw[:, 1, :], hw[:, 0, :], ALU.subtract, [b_hw], [b_hd])
                ACT(habs, hw, AF.Abs, [b_hw], [b_ha])
                for ch in range(2):
                    for dirn in range(2):
                        MM(ps[:, ch, :], ones_bf, habs[:, dirn, ch * 512:(ch + 1) * 512],
                           tc == 0 and dirn == 0, tc == nt - 1 and dirn == 1, [b_ones, b_ha], [PB[ch]])
            for ch in range(2):
                RECIP(rl1[:, ch * 512:(ch + 1) * 512], ps[:, ch, :], [PB[ch]], [b_rl1])
            def load_tables(fc_, s_):
                P.dma(tct[s_], dr["tcos" + tag][fc_], writes=[b_tc[s_]])
                if fc_ != nf - 1:
                    P.dma(tst[s_], dr["tsin" + tag][fc_], writes=[b_ts[s_]])

            load_tables(0, git % 2)
            for fc in range(nf):
                s = git % 2
                git += 1
                nyq = fc == nf - 1
                if fc + 1 < nf:
                    load_tables(fc + 1, git % 2)
                mrow = 1 if nyq else 128
                mcol = 1 if nyq else 128
                if nyq:
                    MSET("dve", hst[s], 0.0, [b_hst[s]])
                for ch in range(2):
                    cs = slice(ch * 512, (ch + 1) * 512)
                    bank = 4 + ch * 2
                    for tc in range(nt):
                        MM(ps[0:mrow, bank, :], tct[s][:, tc, 0:mcol], hsum[:, tc, cs], tc == 0, tc == nt - 1,
                           [b_tc[s], b_hs], [PB[bank]])
                    STT(hst[s][0:mrow, 0, cs], ps[0:mrow, bank, :], wfn[0:mrow, fc:fc + 1], rl1[0:mrow, cs], ALU.mult, ALU.mult,
                        [PB[bank], b_wfn, b_rl1], [b_hst[s]])
                    if not nyq:
                        for tc in range(nt):
                            MM(ps[:, bank + 1, :], tst[s][:, tc, :], hdif[:, tc, cs], tc == 0, tc == nt - 1,
                               [b_ts[s], b_hd], [PB[bank + 1]])
                        STT(hst[s][:, 1, cs], ps[:, bank + 1, :], wfn[:, fc:fc + 1], rl1[:, cs], ALU.mult, ALU.mult,
                            [PB[bank + 1], b_wfn, b_rl1], [b_hst[s]])
                P.dma(hsc[o, fc].rearrange("cb p ri c -> p ri cb c"),
                      hst[s].rearrange("p ri (cb c) -> p ri cb c", cb=NCB),
                      reads=[b_hst[s]], writes=[P.dbuf(f"hsw{s}"), b_Hs[(i, tag)]])
        P.barrier()
        A.release()

    for i in even_idx:
        filter_precompute(i, "L", NL)
        if 2 * i < DEPTH - 1:
            filter_precompute(i, "C", NCX)

    xT = A.alloc([KC, T], F32)
    hT = A.alloc([KC, HW], BF16)
    TBLK = [(0, 512), (512, 512), (1024, 512), (1536, 512), (2048, 256)]
    b_x = [[Buf(f"x{c}_{j}") for j in range(5)] for c in range(KC)]
    b_h = [Buf(f"h{j}") for j in range(5)]

    def xbufs(c, x0, n):
        return [b_x[c][j] for j, (s, ln) in enumerate(TBLK) if s < x0 + n and x0 < s + ln]

    def hbufs(x0, n):
        return [b_h[j] for j, (s, ln) in enumerate(TBLK) if s < x0 + n and x0 < s + ln]

    def hcol(x0):
        return HL + x0 if x0 < NL else HC + (x0 - NL)

    MSET("pool", hT, 0.0, b_h)

    def norm_alloc():
        nb = {}
        nb["sq"] = [A.alloc([512], BF16) for _ in range(2)]; nb["b_sq"] = [Buf("sq0"), Buf("sq1")]
        nb["lnv"] = [A.alloc([512], F32) for _ in range(2)]; nb["b_ln"] = [Buf("lnv0"), Buf("lnv1")]
        nb["rstd"] = [A.alloc([512], F32) for _ in range(2)]; nb["b_rs"] = [Buf("rstd0"), Buf("rstd1")]
        nb["tmp"] = [A.alloc([512], F32) for _ in range(2)]; nb["b_tmp"] = [Buf("nt0"), Buf("nt1")]
        return nb

    def norm_stats(nb, l, which, b, j, bank):
        x0, n = TBLK[j]
        sq, b_sq = nb["sq"], nb["b_sq"]
        p_ = j % 2
        for c in range(KC):
            s = c % 2
            if c % 2 == 0:
                ACT(sq[s][:, 0:n], xT[:, c, x0:x0 + n], AF.Square, [b_x[c][j]], [b_sq[s]])
            else:
                TT("dve", sq[s][:, 0:n], xT[:, c, x0:x0 + n], xT[:, c, x0:x0 + n], ALU.mult, [b_x[c][j]], [b_sq[s]])
            MM(ps[:, bank, 0:n], ones_bf, sq[s][:, 0:n], c == 0, c == KC - 1, [b_ones, b_sq[s]], [PB[bank]])
        ACT(nb["lnv"][p_][:, 0:n], ps[:, bank, 0:n], AF.Ln, [PB[bank], b_cbias], [nb["b_ln"][p_]], bias=cbias[:, 0:1])
        ACT(nb["rstd"][p_][:, 0:n], nb["lnv"][p_][:, 0:n], AF.Exp, [nb["b_ln"][p_]], [nb["b_rs"][p_]], scale=-0.5)

    def norm_apply(nb, l, which, b, j):
        so = 0 if which == 1 else 3
        x0, n = TBLK[j]
        row = b if x0 < NL else 2
        p_ = j % 2
        tmp, b_tmp = nb["tmp"], nb["b_tmp"]
        h0 = hcol(x0)
        for c in range(KC):
            s = c % 2
            STT(tmp[s][:, 0:n], xT[:, c, x0:x0 + n], modv[:, l, row, so + 1, c:c + 1], nb["rstd"][p_][:, 0:n], ALU.mult, ALU.mult,
                [b_x[c][j], b_modv, nb["b_rs"][p_]], [b_tmp[s]])
            ACT(hT[:, c, h0:h0 + n], tmp[s][:, 0:n], AF.Identity, [b_tmp[s], b_modv], [b_h[j]],
                bias=modv[:, l, row, so, c:c + 1])

    def norm_block(nb, l, which, b, j, bank):
        norm_stats(nb, l, which, b, j, bank)
        norm_apply(nb, l, which, b, j)

    def norm_blocks(nb, l, which, b, js, banks=(0, 1)):
        js = list(js)
        pipeline(js, [lambda i_, j: norm_stats(nb, l, which, b, j, banks[j % 2]),
                      lambda i_, j: norm_apply(nb, l, which, b, j)])

    def norm_phase(l, which, b, with_ctx):
        A.mark()
        nb = norm_alloc()
        norm_blocks(nb, l, which, b, range(5 if with_ctx else 4))
        A.release()

    def ffn_phase(l, b, with_ctx):
        A.mark()
        nb = norm_alloc()
        norm_blocks(nb, l, 2, b, range(3))
        late_norms = [3, 4] if with_ctx else [3]
        subs = [(0, 384), (384, 384), (768, 384), (1152, 384), (1536, 384), (1920, 128)]
        if with_ctx:
            subs.append((2048, 256))
        blocks = [subs[0:3], subs[3:]]
        aT = A.alloc([FCH, 1152], BF16)
        b_a = [Buf(f"a{s}") for s in range(4)]
        wgu = [A.alloc([KC, 256], BF16) for _ in range(3)]
        b_wgu = [P.dbuf(f"wgu{s}") for s in range(3)]
        wdn = [A.alloc([FCH, 128], BF16) for _ in range(2)]
        b_wdn = [P.dbuf(f"wdn{s}") for s in range(2)]
        sg = [A.alloc([384], F32) for _ in range(2)]; b_sg = [Buf("sg0"), Buf("sg1")]
        wi = 0; di = 0; si = 0; pi = 0
        for blk in blocks:
            offs = []
            o = 0
            for (x0, n) in blk:
                offs.append(o); o += n
            for f in range(FCH):
                s = wi % 3; wi += 1
                P.dma(wgu[s], wview(dr["w_gu"][l], 0, KC, f * 256, 256), writes=[b_wgu[s]], q="pool")
                if f in (2, 5) and late_norms:
                    norm_block(nb, l, 2, b, late_norms.pop(0), 3)
                for sb, (x0, n) in enumerate(blk):
                    h0 = hcol(x0)
                    bg = pi % 2; bu = 2 + pi % 2; pi += 1
                    for k in range(KC):
                        MM(ps[:, bg, 0:n], wgu[s][:, k, 0:128], hT[:, k, h0:h0 + n], k == 0, k == KC - 1,
                           [b_wgu[s]] + hbufs(x0, n), [PB[bg]])
                    for k in range(KC):
                        MM(ps[:, bu, 0:n], wgu[s][:, k, 128:256], hT[:, k, h0:h0 + n], k == 0, k == KC - 1,
                           [b_wgu[s]] + hbufs(x0, n), [PB[bu]])
                    ss = si % 2; si += 1
                    ACT(sg[ss][:, 0:n], ps[:, bg, 0:n], AF.Silu, [PB[bg]], [b_sg[ss]])
                    TT("dve", aT[:, f, offs[sb]:offs[sb] + n], sg[ss][:, 0:n], ps[:, bu, 0:n], ALU.mult,
                       [b_sg[ss], PB[bu]], [b_a[sb]])
            for dc in range(KC):
                s = di % 2; di += 1
                P.dma(wdn[s], wview(dr["w_down"][l], 0, FCH, dc * 128, 128), writes=[b_wdn[s]], q="pool")
                for sb, (x0, n) in enumerate(blk):
                    row = b if x0 < NL else 2
                    bank = 4 + pi % 4; pi += 1
                    for f in range(FCH):
                        MM(ps[:, bank, 0:n], wdn[s][:, f, :], aT[:, f, offs[sb]:offs[sb] + n], f == 0, f == FCH - 1,
                           [b_wdn[s], b_a[sb]], [PB[bank]])
                    xb = xbufs(dc, x0, n)
                    STT(xT[:, dc, x0:x0 + n], ps[:, bank, 0:n], modv[:, l, row, 5, dc:dc + 1], xT[:, dc, x0:x0 + n],
                        ALU.mult, ALU.add, [PB[bank], b_modv] + xb, xb)
        A.release()

    def pipeline(items, stages):
        n = len(items)
        S = len(stages)
        for step in range(n + S - 1):
            for s_ in range(S):
                i = step - s_
                if 0 <= i < n and stages[s_] is not None:
                    stages[s_](i, items[i])

    def attn_phase(l, b, last):
        j = l // 2
        A.mark()
        ropec = A.alloc([NL], F32); ropes = A.alloc([NL], F32)
        b_ropec = P.dbuf("ropec"); b_ropes = P.dbuf("ropes")
        P.dma(ropec, dr["ropec"], writes=[b_ropec])
        P.dma(ropes, dr["ropes"], writes=[b_ropes])
        qkg = A.alloc([2], F32); b_qkg = P.dbuf("qkg")
        P.dma(qkg, dr["qkg"][j], writes=[b_qkg])
        TS(qkg, qkg, math.sqrt(128.0), 0.0, ALU.mult, ALU.add, [b_qkg], [b_qkg])
        qT = A.alloc([4, T], BF16); b_q = [[Buf(f"q{h}_{jb}") for jb in range(5)] for h in range(4)]
        kT = A.alloc([T], BF16); b_k = [Buf(f"k{jb}") for jb in range(5)]
        V = A.alloc([18, 128], BF16); b_v = [Buf(f"v{jb}") for jb in range(5)]
        oT = A.alloc([4, 512], BF16); b_o = [Buf(f"o{h}") for h in range(4)]
        wq = [A.alloc([KC, 128], BF16) for _ in range(2)]; b_wq = [P.dbuf(f"wq{s}") for s in range(2)]
        wv = A.alloc([KC, 128], BF16); b_wv = P.dbuf("wv")
        wo = A.alloc([4, D], BF16); b_wo = P.dbuf("wo")
        sqb = [A.alloc([512], BF16) for _ in range(3)]; b_sqb = [Buf(f"asq{s}") for s in range(3)]
        lnv = [A.alloc([512], F32) for _ in range(2)]; b_ln = [Buf(f"alnv{s}") for s in range(2)]
        rstd = [A.alloc([512], F32) for _ in range(2)]; b_rs = [Buf(f"arstd{s}") for s in range(2)]
        qn = [A.alloc([512], BF16) for _ in range(3)]; b_qn = [Buf(f"qn{s}") for s in range(3)]
        t1 = [A.alloc([512], F32) for _ in range(2)]; b_t1 = [Buf(f"t1{s}") for s in range(2)]
        t2 = [A.alloc([512], F32) for _ in range(2)]; b_t2 = [Buf(f"t2{s}") for s in range(2)]
        pT = [A.alloc([512], BF16) for _ in range(3)]; b_pT = [Buf(f"pT{s}") for s in range(3)]
        rden = [A.alloc([512], F32) for _ in range(2)]; b_rd = [Buf(f"rden{s}") for s in range(2)]
        blks = TBLK
        hcount = [0]

        for g in range(2):
            P.dma(wo, wview(dr["w_o"][j], g * 512, 4, 0, D), writes=[b_wo], q="pool")
            P.dma(wv, wview(dr["w_qkv"][j], 0, KC, 1280 + g * 128, 128), writes=[b_wv], q="pool")
            for tt in range(18):
                x0 = tt * 128
                jb = min(x0 // 512, 4)
                h0 = hcol(x0)
                bank = 6 + tt % 2
                for k in range(KC):
                    MM(ps[:, bank, 0:128], hT[:, k, h0:h0 + 128], wv[:, k, :], k == 0, k == KC - 1, [b_wv, b_h[jb]], [PB[bank]])
                ACT(V[:, tt, :], ps[:, bank, 0:128], AF.Copy, [PB[bank]], [b_v[jb]])
            qblist = list(range(4)) if last else list(range(5))
            heads = [(1024 + g * 128, 1, kT, b_k, list(range(5)))]
            for h in range(4):
                heads.append(((g * 4 + h) * 128, 0, qT[:, h, :], b_q[h], qblist))
            items = []
            for hi, hd in enumerate(heads):
                for bi, jb in enumerate(hd[4]):
                    items.append((hi, jb, bi == 0))

            def stA(i, it):
                hi, jb, firstblk = it
                col0 = heads[hi][0]
                s = (hcount[0] + hi) % 2
                if firstblk:
                    P.dma(wq[s], wview(dr["w_qkv"][j], 0, KC, col0, 128), writes=[b_wq[s]], q="pool")
                x0, n = blks[jb]
                h0 = hcol(x0)
                bank = i % 3
                for k in range(KC):
                    MM(ps[:, bank, 0:n], wq[s][:, k, :], hT[:, k, h0:h0 + n], k == 0, k == KC - 1, [b_wq[s], b_h[jb]], [PB[bank]])
                ACT(sqb[i % 3][:, 0:n], ps[:, bank, 0:n], AF.Square, [PB[bank]], [b_sqb[i % 3]])

            def stB(i, it):
                hi, jb, firstblk = it
                col0, gcol, dst, dstbufs, _ = heads[hi]
                x0, n = blks[jb]
                bank = i % 3
                sb = 3 if i % 2 == 0 else 6
                MM(ps[:, sb, 0:n], ones_bf, sqb[i % 3][:, 0:n], True, True, [b_ones, b_sqb[i % 3]], [PB[sb]])
                ACT(lnv[i % 2][:, 0:n], ps[:, sb, 0:n], AF.Ln, [PB[sb], b_cbias], [b_ln[i % 2]], bias=cbias[:, 1:2])
                ACT(rstd[i % 2][:, 0:n], lnv[i % 2][:, 0:n], AF.Exp, [b_ln[i % 2]], [b_rs[i % 2]], scale=-0.5)
                if x0 >= NL:
                    STT(dst[:, x0:x0 + n], ps[:, bank, 0:n], qkg[:, gcol:gcol + 1], rstd[i % 2][:, 0:n], ALU.mult, ALU.mult,
                        [PB[bank], b_qkg, b_rs[i % 2]], [dstbufs[jb]])
                else:
                    STT(qn[i % 3][:, 0:n], ps[:, bank, 0:n], qkg[:, gcol:gcol + 1], rstd[i % 2][:, 0:n], ALU.mult, ALU.mult,
                        [PB[bank], b_qkg, b_rs[i % 2]], [b_qn[i % 3]])

            def stC(i, it):
                hi, jb, firstblk = it
                col0, gcol, dst, dstbufs, _ = heads[hi]
                x0, n = blks[jb]
                if x0 >= NL:
                    return
                rb = 4 + i % 2
                MM(ps[:, rb, 0:n], rperm, qn[i % 3][:, 0:n], True, True, [b_rperm, b_qn[i % 3]], [PB[rb]])
                TT("pool", t1[i % 2][:, 0:n], qn[i % 3][:, 0:n], ropec[:, x0:x0 + n], ALU.mult, [b_qn[i % 3], b_ropec], [b_t1[i % 2]])
                TT("dve", t2[i % 2][:, 0:n], ps[:, rb, 0:n], ropes[:, x0:x0 + n], ALU.mult, [PB[rb], b_ropes], [b_t2[i % 2]])
                TT("pool", dst[:, x0:x0 + n], t1[i % 2][:, 0:n], t2[i % 2][:, 0:n], ALU.add, [b_t1[i % 2], b_t2[i % 2]], [dstbufs[jb]])

            pipeline(items, [stA, stB, stC])
            hcount[0] += len(heads)

            for jb in qblist:
                x0, n = blks[jb]
                row = b if x0 < NL else 2
                ktiles = list(range(18)) if x0 < NL else [16, 17]
                nk = len(ktiles)
                aitems = [(h, ki, kt) for h in range(4) for ki, kt in enumerate(ktiles)]

                def stS(i, it):
                    h, ki, kt = it
                    kjb = min(kt // 4, 4)
                    MM(ps[:, i % 2, 0:n], kT[:, kt * 128:(kt + 1) * 128], qT[:, h, x0:x0 + n], True, True,
                       [b_k[kjb], b_q[h][jb]], [PB[i % 2]])
                    ACT(pT[i % 3][:, 0:n], ps[:, i % 2, 0:n], AF.Exp, [PB[i % 2]], [b_pT[i % 3]], scale=1.0 / math.sqrt(128.0))

                def stPV(i, it):
                    h, ki, kt = it
                    kjb = min(kt // 4, 4)
                    ob = 4 + h % 2
                    db = 6 + h % 2
                    MM(ps[:, ob, 0:n], V[:, kt, :], pT[i % 3][:, 0:n], ki == 0, ki == nk - 1, [b_v[kjb], b_pT[i % 3]], [PB[ob]])
                    MM(ps[:, db, 0:n], ones_bf, pT[i % 3][:, 0:n], ki == 0, ki == nk - 1, [b_ones, b_pT[i % 3]], [PB[db]])
                    if ki == nk - 1:
                        RECIP(rden[h % 2][:, 0:n], ps[:, db, 0:n], [PB[db]], [b_rd[h % 2]])
                        TT("dve", oT[:, h, 0:n], ps[:, ob, 0:n], rden[h % 2][:, 0:n], ALU.mult, [PB[ob], b_rd[h % 2]], [b_o[h]])

                pipeline(aitems, [stS, None, stPV])
                for dc in range(KC):
                    bank = 2 + dc % 2
                    for h in range(4):
                        MM(ps[:, bank, 0:n], wo[:, h, dc * 128:(dc + 1) * 128], oT[:, h, 0:n], h == 0, h == 3, [b_wo, b_o[h]], [PB[bank]])
                    STT(xT[:, dc, x0:x0 + n], ps[:, bank, 0:n], modv[:, l, row, 2, dc:dc + 1], xT[:, dc, x0:x0 + n],
                        ALU.mult, ALU.add, [PB[bank], b_modv, b_x[dc][jb]], [b_x[dc][jb]])
        A.release()

    def gmlp_phase(l, b, with_ctx):
        i = l // 2
        A.mark()
        wvv = [A.alloc([KC, 512], BF16) for _ in range(2)]; b_wvv = [P.dbuf(f"gwv{s}") for s in range(2)]
        wuo = [A.alloc([KC, 256], BF16) for _ in range(4)]; b_wuo = [P.dbuf(f"gwuo{s}") for s in range(4)]
        lng = A.alloc([D], F32); lnb = A.alloc([D], F32); b_lng = P.dbuf("lng"); b_lnb = P.dbuf("lnb")
        wsT = A.alloc([8, 128], BF16); b_ws = P.dbuf("wsT")
        bsb = A.alloc([8, 128], F32); b_bs = P.dbuf("bsb")
        P.dma(wvv[0], wview(dr["w_in"][i], 0, KC, 1024, 512), writes=[b_wvv[0]], q="pool")
        P.dma(wvv[1], wview(dr["w_in"][i], 0, KC, 1536, 512), writes=[b_wvv[1]], q="pool")
        P.dma(lng, dr["ln_g"][i].partition_broadcast(128), writes=[b_lng])
        P.dma(lnb, dr["ln_b"][i].partition_broadcast(128), writes=[b_lnb])
        P.dma(wsT, dr["w_sT"][i], writes=[b_ws], q="pool")
        P.dma(bsb.rearrange("p g q -> p (g q)"), dr["b_s"][i].partition_broadcast(128), writes=[b_bs])
        vg = [A.alloc([D], F32) for _ in range(4)]; b_vg = [Buf(f"vg{c}") for c in range(4)]
        junk = A.alloc([D], BF16); b_junk = Buf("junk")
        st = A.alloc([8, 4], F32); b_st = Buf("st")
        vnb = [A.alloc([D], BF16) for _ in range(2)]; b_vnb = [Buf("vnb0"), Buf("vnb1")]
        svt = [A.alloc([4, 128], F32) for _ in range(2)]; b_svt = [Buf("svt0"), Buf("svt1")]
        ug = A.alloc([8, 512], BF16); b_ug = Buf("ug")
        ya = A.alloc([8, 512], BF16); b_ya = [Buf(f"ya{c}") for c in range(4)]
        blks = TBLK if with_ctx else TBLK[:4]
        sched = []
        for jb in range(len(blks)):
            for t_ in range(4):
                sched.append((dr["w_in"][i], t_ * 256))
            for t_ in range(4):
                sched.append((dr["w_out"][i], t_ * 256))
        issued = [0]

        def need(idx):
            while issued[0] <= min(idx + 2, len(sched) - 1):
                src, c0 = sched[issued[0]]
                s_ = issued[0] % 4
                P.dma(wuo[s_], wview(src, 0, KC, c0, 256), writes=[b_wuo[s_]], q="pool")
                issued[0] += 1
            return idx % 4

        ti = 0
        for jb, (x0, n) in enumerate(blks):
            row = b if x0 < NL else 2
            nch = n // 128
            h0 = hcol(x0)
            MSET("dve", st[:, 0:2, :], 0.0, [b_st])
            for ci in range(nch):
                p_ = ci % 2
                hh = hcol(x0 + ci * 128)
                for hf in range(2):
                    for k in range(KC):
                        MM(ps[:, 2 * p_ + hf, :], hT[:, k, hh:hh + 128], wvv[hf][:, k, :], k == 0, k == KC - 1,
                           [b_wvv[hf], b_h[jb]], [PB[2 * p_ + hf]])
                ACT(vg[ci].rearrange("p (a b) -> p a b", a=2), ps[:, 2 * p_:2 * p_ + 2, :], AF.Gelu,
                    [PB[2 * p_], PB[2 * p_ + 1]], [b_vg[ci], b_st], accum=st[:, 0, ci:ci + 1])
                ACT(junk, vg[ci], AF.Square, [b_vg[ci]], [b_junk, b_st], accum=st[:, 1, ci:ci + 1])
            for gp in range(4):
                s_ = need(ti); ti += 1
                for gg in range(2):
                    g = gp * 2 + gg
                    bank = 6 + g % 2
                    for k in range(KC):
                        MM(ps[:, bank, 0:n], wuo[s_][:, k, gg * 128:(gg + 1) * 128], hT[:, k, h0:h0 + n], k == 0, k == KC - 1,
                           [b_wuo[s_], b_h[jb]], [PB[bank]])
                    ACT(ug[:, g, 0:n], ps[:, bank, 0:n], AF.Gelu, [PB[bank]], [b_ug])
            TS(st[:, 2, :], st[:, 0, :], 1.0 / D, 0.0, ALU.mult, ALU.add, [b_st], [b_st])
            TT("dve", st[:, 3, :], st[:, 2, :], st[:, 2, :], ALU.mult, [b_st], [b_st])
            STT(st[:, 4, :], st[:, 1, :], 1.0 / D, st[:, 3, :], ALU.mult, ALU.subtract, [b_st], [b_st])
            ACT(st[:, 5, :], st[:, 4, :], AF.Ln, [b_st, b_cbias], [b_st], bias=cbias[:, 2:3])
            ACT(st[:, 6, :], st[:, 5, :], AF.Exp, [b_st], [b_st], scale=-0.5)
            STT(st[:, 7, :], st[:, 2, :], -1.0, st[:, 6, :], ALU.mult, ALU.mult, [b_st], [b_st])
            for ci in range(nch):
                p_ = ci % 2
                ACT(vg[ci], vg[ci], AF.Identity, [b_vg[ci], b_st], [b_vg[ci]], bias=st[:, 7, ci:ci + 1], scale=st[:, 6, ci:ci + 1])
                TT("dve", vg[ci], vg[ci], lng, ALU.mult, [b_vg[ci], b_lng], [b_vg[ci]])
                TT("dve", vnb[p_], vg[ci], lnb, ALU.add, [b_vg[ci], b_lnb], [b_vnb[p_]])
                for g in range(8):
                    bank = 4 + (g // 4)
                    MM(ps[:, bank, (g % 4) * 128:(g % 4 + 1) * 128], vnb[p_][:, g * 128:(g + 1) * 128], wsT[:, g, :], True, True,
                       [b_vnb[p_], b_ws], [PB[bank]])
                for hf in range(2):
                    TT("dve", svt[hf], ps[:, 4 + hf, :].rearrange("p (g q) -> p g q", g=4), bsb[:, hf * 4:(hf + 1) * 4, :], ALU.add,
                       [PB[4 + hf], b_bs], [b_svt[hf]])
                    TT("pool", ya[:, hf * 4:(hf + 1) * 4, ci * 128:(ci + 1) * 128], ug[:, hf * 4:(hf + 1) * 4, ci * 128:(ci + 1) * 128],
                       svt[hf], ALU.mult, [b_ug, b_svt[hf]], [b_ya[ci]])
            for dp in range(4):
                s_ = need(ti); ti += 1
                for dd in range(2):
                    dc = dp * 2 + dd
                    bank = 6 + dc % 2
                    for g in range(8):
                        MM(ps[:, bank, 0:n], wuo[s_][:, g, dd * 128:(dd + 1) * 128], ya[:, g, 0:n], g == 0, g == 7,
                           [b_wuo[s_]] + b_ya[0:nch], [PB[bank]])
                    STT(xT[:, dc, x0:x0 + n], ps[:, bank, 0:n], modv[:, l, row, 2, dc:dc + 1], xT[:, dc, x0:x0 + n],
                        ALU.mult, ALU.add, [PB[bank], b_modv, b_x[dc][jb]], [b_x[dc][jb]])
        A.release()

    def hyena_phase(l, b, tag):
        i = l // 2
        n = NL if tag == "L" else NCX
        xbase = 0 if tag == "L" else NL
        row = b if tag == "L" else 2
        nt = n // 128
        nf = nt + 1
        hsc = Hs[(i, tag)]
        b_hsc = b_Hs[(i, tag)]
        A.mark()
        wB = [A.alloc([KC, CB], BF16) for _ in range(2)]; b_wB = [P.dbuf(f"hwB{s}") for s in range(2)]
        Q0 = [A.alloc([CB], BF16) for _ in range(3)]; b_Q0 = [Buf(f"Q0_{s}") for s in range(3)]
        Q2 = [A.alloc([CB], BF16) for _ in range(3)]; b_Q2 = [Buf(f"Q2_{s}") for s in range(3)]
        C1 = [A.alloc([CB], F32) for _ in range(3)]; b_C1 = [Buf(f"C1_{s}") for s in range(3)]
        taps = [A.alloc([4, CB], F32) for _ in range(2)]; b_taps = [P.dbuf(f"taps{s}") for s in range(2)]
        b_tapsb = [P.dbuf(f"tapsb{s}") for s in range(2)]
        skp = A.alloc([2, CB], F32); b_skp = P.dbuf("skp")
        zb = A.alloc([nt, CB], BF16); b_zb = [Buf(f"zb{t}") for t in range(nt)]
        Yr = A.alloc([nf, CB], BF16); Yi = A.alloc([nf, CB], BF16)
        b_Y = [Buf(f"Y{f}") for f in range(nf)]
        zT = A.alloc([CB // 128, n], BF16); b_zT = [Buf(f"zT{j}") for j in range((n + 511) // 512)]
        tct = [A.alloc([nf, 128], BF16) for _ in range(2)]; b_tc = [P.dbuf(f"tc{s}") for s in range(2)]
        tst = [A.alloc([nf, 128], BF16) for _ in range(2)]; b_ts = [P.dbuf(f"ts{s}") for s in range(2)]
        Ht = [A.alloc([2, CB], F32) for _ in range(2)]; b_Ht = [P.dbuf(f"Ht{s}") for s in range(2)]
        wo = A.alloc([CB // 128, D], BF16); b_wo = P.dbuf("hwo")
        e1 = A.alloc([CB], F32); b_e1 = Buf("e1")
        e2 = A.alloc([CB], F32); b_e2 = Buf("e2")
        e3 = A.alloc([CB], F32); b_e3 = Buf("e3")
        e4 = A.alloc([CB], F32); b_e4 = Buf("e4")
        z2 = [A.alloc([CB], BF16) for _ in range(2)]; b_z2 = [Buf("z2a"), Buf("z2b")]
        cnt = {"w": 0, "t": 0, "h": 0, "o": 0, "z": 0}

        def prep_weights(cb, third):
            s = cnt["w"] % 2; cnt["w"] += 1
            col0 = third * D + cb * CB
            P.dma(wB[s], wview(dr["w_in"][i], 0, KC, 2 * D + col0, CB), writes=[b_wB[s]], q="pool")
            P.dma(taps[s][:, 0:3, :], dr["conv_w"][i][:, col0:col0 + CB].partition_broadcast(128), writes=[b_taps[s]])
            P.dma(taps[s][:, 3, :], dr["conv_b"][i][col0:col0 + CB].partition_broadcast(128), writes=[b_tapsb[s]])
            return s

        def proj_stage(s, tc):
            x0 = xbase + tc * 128
            h0 = hcol(x0)
            jb = min(x0 // 512, 4)
            pbank = tc % 2
            sl = tc % 3
            for k in range(KC):
                MM(ps[:, pbank, 0:CB], hT[:, k, h0:h0 + 128], wB[s][:, k, :], k == 0, k == KC - 1, [b_wB[s], b_h[jb]], [PB[pbank]])
            TT("dve", Q0[sl], ps[:, pbank, 0:CB], taps[s][:, 0, :], ALU.mult, [PB[pbank], b_taps[s]], [b_Q0[sl]])
            TT("dve", Q2[sl], ps[:, pbank, 0:CB], taps[s][:, 2, :], ALU.mult, [PB[pbank], b_taps[s]], [b_Q2[sl]])
            TT("dve", C1[sl], ps[:, pbank, 0:CB], taps[s][:, 1, :], ALU.mult, [PB[pbank], b_taps[s]], [b_C1[sl]])
            TT("pool", C1[sl], C1[sl], taps[s][:, 3, :], ALU.add, [b_C1[sl], b_tapsb[s]], [b_C1[sl]])

        def shift_stage(tc, bank):
            sl = tc % 3
            ops = [(0, Q0[sl], b_Q0[sl]), (2, Q2[sl], b_Q2[sl])]
            if tc > 0:
                ops.append((1, Q0[(tc - 1) % 3], b_Q0[(tc - 1) % 3]))
            if tc < nt - 1:
                ops.append((3, Q2[(tc + 1) % 3], b_Q2[(tc + 1) % 3]))
            for oi, (m, q, bq) in enumerate(ops):
                MM(ps[:, bank, 0:CB], shiftm[:, m, :], q, oi == 0, oi == len(ops) - 1, [b_shiftm, bq], [PB[bank]])

        def fwd_dft(o, cb, deferred=()):
            deferred = list(deferred)
            for fc in range(nf):
                for _ in range(3):
                    if deferred:
                        deferred.pop(0)()
                nyq = fc == nf - 1
                s = cnt["t"] % 2; cnt["t"] += 1
                P.dma(tct[s], dr["tcos" + tag][fc], writes=[b_tc[s]])
                if not nyq:
                    P.dma(tst[s], dr["tsin" + tag][fc], writes=[b_ts[s]])
                hs_ = cnt["h"] % 2; cnt["h"] += 1
                P.dma(Ht[hs_], hsc[o, fc, cb], reads=[b_hsc], writes=[b_Ht[hs_]])
                m = 1 if nyq else 128
                bank = 2 * (fc % 2)
                for tc in range(nt):
                    MM(ps[0:m, bank, 0:CB], tct[s][:, tc, 0:m], zb[:, tc, :], tc == 0, tc == nt - 1, [b_tc[s], b_zb[tc]], [PB[bank]])
                if nyq:
                    MSET("pool", Yr[:, fc, :], 0.0, [b_Y[fc]])
                    MSET("pool", Yi[:, fc, :], 0.0, [b_Y[fc]])
                    TT("dve", Yr[0:1, fc, :], ps[0:1, bank, 0:CB], Ht[hs_][0:1, 0, :], ALU.mult, [PB[bank], b_Ht[hs_]], [b_Y[fc]])
                    continue
                for tc in range(nt):
                    MM(ps[:, bank + 1, 0:CB], tst[s][:, tc, :], zb[:, tc, :], tc == 0, tc == nt - 1, [b_ts[s], b_zb[tc]], [PB[bank + 1]])
                TT("dve", e1, ps[:, bank, 0:CB], Ht[hs_][:, 0, :], ALU.mult, [PB[bank], b_Ht[hs_]], [b_e1])
                TT("dve", e2, ps[:, bank + 1, 0:CB], Ht[hs_][:, 1, :], ALU.mult, [PB[bank + 1], b_Ht[hs_]], [b_e2])
                TT("pool", Yr[:, fc, :], e1, e2, ALU.add, [b_e1, b_e2], [b_Y[fc]])
                TT("dve", e3, ps[:, bank + 1, 0:CB], Ht[hs_][:, 0, :], ALU.mult, [PB[bank + 1], b_Ht[hs_]], [b_e3])
                TT("dve", e4, ps[:, bank, 0:CB], Ht[hs_][:, 1, :], ALU.mult, [PB[bank], b_Ht[hs_]], [b_e4])
                TT("pool", Yi[:, fc, :], e3, e4, ALU.subtract, [b_e3, b_e4], [b_Y[fc]])
            while deferred:
                deferred.pop(0)()

        def inv_dft_gate(o, cb, sx, final):
            pending = []
            proj_stage(sx, 0)
            for tc in range(nt):
                if tc + 1 < nt:
                    proj_stage(sx, tc + 1)
                s = cnt["t"] % 2; cnt["t"] += 1
                P.dma(tct[s], dr["tcos" + tag][tc], writes=[b_tc[s]])
                P.dma(tst[s], dr["tsin" + tag][tc], writes=[b_ts[s]])
                bank = 4 + 2 * (tc % 2)
                for fc in range(nf):
                    kk = 1 if fc == nf - 1 else 128
                    MM(ps[:, bank, 0:CB], tct[s][0:kk, fc, :], Yr[0:kk, fc, :], fc == 0, False, [b_tc[s], b_Y[fc]], [PB[bank]])
                for fc in range(nf - 1):
                    MM(ps[:, bank, 0:CB], tst[s][:, fc, :], Yi[:, fc, :], False, fc == nf - 2, [b_ts[s], b_Y[fc]], [PB[bank]])
                shift_stage(tc, bank + 1)
                TT("pool", e1, zb[:, tc, :], skp[:, o, :], ALU.mult, [b_zb[tc], b_skp], [b_e1])
                TT("dve", e2, ps[:, bank, 0:CB], e1, ALU.add, [PB[bank], b_e1], [b_e2])
                TT("dve", e3, ps[:, bank + 1, 0:CB], C1[tc % 3], ALU.add, [PB[bank + 1], b_C1[tc % 3]], [b_e3])
                if not final:
                    TT("pool", zb[:, tc, :], e2, e3, ALU.mult, [b_e2, b_e3], [b_zb[tc]])
                else:
                    zs = cnt["z"] % 2; cnt["z"] += 1
                    TT("pool", z2[zs], e2, e3, ALU.mult, [b_e2, b_e3], [b_z2[zs]])

                    def trans(zs=zs, tc=tc):
                        psb = ps[:, 2 + zs, :].bitcast(BF16)
                        for cc in range(CB // 128):
                            TR(psb[:, cc * 128:(cc + 1) * 128], z2[zs][:, cc * 128:(cc + 1) * 128], ident, [b_z2[zs], b_ident], [PB[2 + zs]])
                        ACT(zT[:, :, tc * 128:(tc + 1) * 128], psb[:, 0:CB].rearrange("p (c t) -> p c t", c=CB // 128), AF.Copy,
                            [PB[2 + zs]], [b_zT[tc // 4]])
                    pending.append(trans)
                if final and len(pending) > 1:
                    pending.pop(0)()
            while pending:
                pending.pop(0)()

        for cb in range(NCB):
            P.dma(skp, dr["skip"][i][:, cb * CB:(cb + 1) * CB].partition_broadcast(128), writes=[b_skp])
            P.dma(wo, wview(dr["w_out"][i], D + cb * CB, CB // 128, 0, D), writes=[b_wo], q="pool")
            sv_ = prep_weights(cb, 0)
            sx1 = prep_weights(cb, 1)
            proj_stage(sv_, 0)
            for tc in range(nt):
                if tc + 1 < nt:
                    proj_stage(sv_, tc + 1)
                bank = 6 + tc % 2
                shift_stage(tc, bank)
                TT("dve", zb[:, tc, :], ps[:, bank, 0:CB], C1[tc % 3], ALU.add, [PB[bank], b_C1[tc % 3]], [b_zb[tc]])
            fwd_dft(0, cb)
            inv_dft_gate(0, cb, sx1, False)
            sx2 = prep_weights(cb, 2)
            fwd_dft(1, cb)
            inv_dft_gate(1, cb, sx2, True)
            for dc in range(KC):
                for jb in range((n + 511) // 512):
                    t0 = jb * 512
                    nn = min(512, n - t0)
                    x0 = xbase + t0
                    xjb = min(x0 // 512, 4)
                    bank = (dc * 4 + jb) % 2
                    for cc in range(CB // 128):
                        MM(ps[:, bank, 0:nn], wo[:, cc, dc * 128:(dc + 1) * 128], zT[:, cc, t0:t0 + nn], cc == 0, cc == CB // 128 - 1,
                           [b_wo, b_zT[jb]], [PB[bank]])
                    STT(xT[:, dc, x0:x0 + nn], ps[:, bank, 0:nn], modv[:, l, row, 2, dc:dc + 1], xT[:, dc, x0:x0 + nn],
                        ALU.mult, ALU.add, [PB[bank], b_modv, b_x[dc][xjb]], [b_x[dc][xjb]])
        A.release()

    def final_phase(b):
        A.mark()
        sq = [A.alloc([512], BF16) for _ in range(2)]; b_sq = [Buf("fsq0"), Buf("fsq1")]
        lnv = A.alloc([512], F32); b_ln = Buf("flnv")
        rstd = A.alloc([512], F32); b_rs = Buf("frstd")
        ot = [A.alloc([512], F32) for _ in range(4)]; b_ot = [Buf(f"fo{s}") for s in range(4)]
        it = 0
        for j, (x0, n) in enumerate(TBLK[:4]):
            bank = j % 2
            for c in range(KC):
                s = it % 2; it += 1
                if final_norm:
                    ACT(sq[s][:, 0:n], xT[:, c, x0:x0 + n], AF.Square, [b_x[c][j]], [b_sq[s]])
                    MM(ps[:, bank, 0:n], ones_bf, sq[s][:, 0:n], c == 0, c == KC - 1, [b_ones, b_sq[s]], [PB[bank]])
            if final_norm:
                ACT(lnv[:, 0:n], ps[:, bank, 0:n], AF.Ln, [PB[bank], b_cbias], [b_ln], bias=cbias[:, 0:1])
                ACT(rstd[:, 0:n], lnv[:, 0:n], AF.Exp, [b_ln], [b_rs], scale=-0.5)
            for c in range(KC):
                s = it % 4; it += 1
                if final_norm:
                    STT(ot[s][:, 0:n], xT[:, c, x0:x0 + n], fing[:, c:c + 1], rstd[:, 0:n], ALU.mult, ALU.mult,
                        [b_x[c][j], b_fing, b_rs], [b_ot[s]])
                    P.dma(outT[b, c * 128:(c + 1) * 128, x0:x0 + n], ot[s][:, 0:n], reads=[b_ot[s]], writes=[P.dbuf(f"out{s}")])
                else:
                    P.dma(outT[b, c * 128:(c + 1) * 128, x0:x0 + n], xT[:, c, x0:x0 + n], reads=[b_x[c][j]], writes=[P.dbuf(f"out{s}")])
        A.release()

    for b in range(nbatch):
        for c in range(KC):
            P.dma(xT[:, c, 0:NL], dr["xT"][b, c * 128:(c + 1) * 128, :], writes=[P.dbuf("xld")] + b_x[c][0:4])
            P.dma(xT[:, c, NL:T], dr["ctxT"][b, c * 128:(c + 1) * 128, :], writes=[P.dbuf("xld2"), b_x[c][4]])
        for l in layers:
            last = l == DEPTH - 1
            even = l % 2 == 0
            with_ctx = not last
            norm_phase(l, 1, b, True if not even else with_ctx)
            P.barrier()
            if even:
                gmlp_phase(l, b, with_ctx)
                P.barrier()
                hyena_phase(l, b, "L")
                P.barrier()
                if with_ctx:
                    hyena_phase(l, b, "C")
                    P.barrier()
            else:
                attn_phase(l, b, last)
                P.barrier()
            ffn_phase(l, b, with_ctx)
            P.barrier()
        final_phase(b)
        P.barrier()
    P.emit()
    return nc, A.peak


_PROG_CACHE = {}


def _pp(v):
    v = np.asarray(v, dtype=np.float32)
    lead = v.shape[:-1]
    r = v.reshape(lead + (KC, 128))
    r = np.moveaxis(r, -1, 0)
    return np.ascontiguousarray(r)


def _shared_inputs(inp):
    C = _consts()
    m = dict(C)
    f32 = lambda a: np.ascontiguousarray(np.asarray(a, dtype=np.float32))
    m["mod_w"] = f32(inp["mod_w"])
    mb = np.asarray(inp["mod_b"], dtype=np.float32).reshape(DEPTH, 6, KC, 128)
    m["modb"] = np.ascontiguousarray(mb.transpose(3, 0, 1, 2))
    m["n1g"] = _pp(inp["norm1_g"])
    m["n2g"] = _pp(inp["norm2_g"])
    m["fing"] = _pp(inp["final_g"])
    wgu = np.asarray(inp["ffn_w_gu"], dtype=np.float32)
    g = wgu[:, :, :DFF].reshape(DEPTH, D, FCH, 1, 128)
    u = wgu[:, :, DFF:].reshape(DEPTH, D, FCH, 1, 128)
    m["w_gu"] = np.ascontiguousarray(np.concatenate([g, u], axis=3).reshape(DEPTH, D, 2 * DFF))
    m["w_down"] = f32(inp["ffn_w_down"])
    m["w_in"] = f32(inp["even_w_in"])
    m["ln_g"] = f32(inp["gmlp_ln_g"])
    m["ln_b"] = f32(inp["gmlp_ln_b"])
    ws = np.asarray(inp["gmlp_w_s"], dtype=np.float32)
    m["w_sT"] = np.ascontiguousarray(ws.transpose(0, 3, 1, 2))
    m["b_s"] = f32(np.asarray(inp["gmlp_b_s"], dtype=np.float32).reshape(2, 8 * 128))
    m["conv_w"] = f32(inp["hyena_conv_w"])
    m["conv_b"] = f32(inp["hyena_conv_b"])
    m["f_w1"] = f32(inp["hyena_f_w1"])
    fr = np.asarray(inp["hyena_freq"], dtype=np.float32)
    m["f_pp"] = np.ascontiguousarray(np.stack([fr[:, 0], np.asarray(inp["hyena_f_b1"], np.float32),
                                               fr[:, 1], np.asarray(inp["hyena_f_b2"], np.float32)], axis=-1))
    m["f_w2"] = f32(inp["hyena_f_w2"])
    m["f_w3"] = f32(inp["hyena_f_w3"])
    m["skip"] = f32(inp["hyena_skip"])
    m["w_out"] = f32(inp["even_w_out"])
    m["w_qkv"] = f32(inp["attn_w_qkv"])
    m["qkg"] = np.ascontiguousarray(np.stack([np.asarray(inp["attn_q_g"], np.float32),
                                              np.asarray(inp["attn_k_g"], np.float32)], axis=-1))
    m["w_o"] = f32(inp["attn_w_o"])
    return m


def _core_inputs(inp, shared, core):
    b0 = 2 * core
    m = dict(shared)
    x = np.asarray(inp["x"], dtype=np.float32)[b0:b0 + 2]
    ctx = np.asarray(inp["ctx"], dtype=np.float32)[b0:b0 + 2]
    m["xT"] = np.ascontiguousarray(x.transpose(0, 2, 1))
    m["ctxT"] = np.ascontiguousarray(ctx.transpose(0, 2, 1))
    cv = np.concatenate([np.asarray(inp["c"], np.float32)[b0:b0 + 2], np.asarray(inp["c_ctx"], np.float32)[None, :]], axis=0)
    m["cvec"] = np.ascontiguousarray(cv.T.reshape(KC, 128, 3).transpose(1, 0, 2))
    return m


def kernel(**inputs):
    key = "full"
    if key not in _PROG_CACHE:
        _PROG_CACHE[key] = build()[0]
    nc = _PROG_CACHE[key]
    shared = _shared_inputs(inputs)
    in_maps = [_core_inputs(inputs, shared, c) for c in range(8)]
    res = run_bass_kernel_spmd(nc, in_maps, core_ids=list(range(8)))
    outs = [np.asarray(r["outT"]).transpose(0, 2, 1) for r in res.results]
    return np.ascontiguousarray(np.concatenate(outs, axis=0).astype(np.float32))
```

```python
import math
import os
import numpy as np
import ml_dtypes
import concourse.bass as bass
import concourse.mybir as mybir
from concourse.bass_utils import run_bass_kernel_spmd

F32 = mybir.dt.float32
BF16 = mybir.dt.bfloat16
AF = mybir.ActivationFunctionType
ALU = mybir.AluOpType

EPOCH = 30000


class Buf:
    __slots__ = ("name", "w", "rs", "dsem", "dcnt")

    def __init__(self, name):
        self.name = name
        self.w = {}
        self.rs = {}
        self.dsem = None
        self.dcnt = 0


class Prog:
    ENGS = ("pe", "act", "dve", "pool", "sp")

    def __init__(self, nc):
        self.nc = nc
        self.ops = {e: [] for e in self.ENGS}
        self.cnt = {e: 0 for e in self.ENGS}
        self.esems = {e: [] for e in self.ENGS}
        self.seen = {e: {} for e in self.ENGS}
        self.semobjs = {}
        self.dma_bufs = []
        self.named = {}

    def dbuf(self, name):
        b = self.named.get(name)
        if b is None:
            b = Buf(name)
            self.named[name] = b
        return b

    def _newsem(self, name):
        cm = self.nc.semaphore(name)
        s = cm.__enter__()
        self.semobjs[name] = (s, cm)
        return name

    def _esem(self, eng, epoch):
        while len(self.esems[eng]) <= epoch:
            self.esems[eng].append(self._newsem(f"e_{eng}_{len(self.esems[eng])}"))
        return self.esems[eng][epoch]

    def _deps(self, eng, reads, writes):
        out = {}

        def add(key, val):
            if out.get(key, 0) < val:
                out[key] = val

        for b in reads:
            for key, (val, src) in b.w.items():
                if src != eng or eng in ("act", "dve", "pool", "__dma__"):
                    add(key, val)
        strict = eng in ("act", "dve", "pool", "__dma__")
        for b in writes:
            for key, (val, src) in b.w.items():
                if src != eng or strict:
                    add(key, val)
            for key, (val, src) in b.rs.items():
                if src != eng or strict:
                    add(key, val)
        return out

    def _filter_waits(self, eng, deps):
        seen = self.seen[eng]
        waits = []
        for key, val in deps.items():
            if seen.get(key, 0) >= val:
                continue
            seen[key] = val
            waits.append((key, val))
        return waits

    def _record(self, event, reads, writes):
        key, val, src = event
        for b in reads:
            b.rs[key] = (val, src)
        for b in writes:
            b.w[key] = (val, src)
            b.rs = {}

    def op(self, eng, fn, reads=(), writes=()):
        deps = self._deps(eng, reads, writes)
        waits = self._filter_waits(eng, deps)
        n = self.cnt[eng]
        epoch, idx = divmod(n, EPOCH)
        key = self._esem(eng, epoch)
        self.cnt[eng] = n + 1
        self.ops[eng].append((waits, fn, key, 1))
        self._record((key, idx + 1, eng), reads, writes)

    def dma(self, out_ap, in_ap, reads=(), writes=(), q="sp", **kw):
        deps = self._deps("__dma__", reads, writes)
        waits = self._filter_waits(q, deps)
        dst = writes[0]
        if dst.dsem is None:
            dst.dsem = self._newsem(f"d{len(self.dma_bufs)}_{dst.name}")
            self.dma_bufs.append(dst)
        dst.dcnt += 16

        def fn(e, out_ap=out_ap, in_ap=in_ap, kw=kw):
            return e.dma_start(out=out_ap, in_=in_ap, **kw)

        self.ops[q].append((waits, fn, dst.dsem, 16))
        self._record((dst.dsem, dst.dcnt, "__dma__"), reads, writes)

    def barrier(self):
        for eng in self.ENGS:
            deps = {}
            for e2 in self.ENGS:
                if e2 == eng or self.cnt[e2] == 0:
                    continue
                epoch, idx = divmod(self.cnt[e2] - 1, EPOCH)
                deps[self.esems[e2][epoch]] = idx + 1
            for b in self.dma_bufs:
                deps[b.dsem] = b.dcnt
            waits = self._filter_waits(eng, deps)
            if waits:
                self.ops[eng].append((waits, None, None, 0))

    def emit(self):
        nc = self.nc
        sem = {k: v[0] for k, v in self.semobjs.items()}

        def run(e, lst):
            for waits, fn, key, inc in lst:
                for (k, v) in waits:
                    e.wait_ge(sem[k], v)
                if fn is not None:
                    fn(e).then_inc(sem[key], inc)

        with nc.Block() as block:
            @block.sync
            def _(e):
                run(e, self.ops["sp"])

            @block.tensor
            def _(e):
                run(e, self.ops["pe"])

            @block.scalar
            def _(e):
                run(e, self.ops["act"])

            @block.vector
            def _(e):
                run(e, self.ops["dve"])

            @block.gpsimd
            def _(e):
                run(e, self.ops["pool"])


class Arena:
    def __init__(self, ap_f32, nbytes):
        self.ap = ap_f32
        self.nbytes = nbytes
        self.off = 0
        self.marks = []
        self.peak = 0

    def alloc(self, shape_free, dtype):
        esz = 2 if dtype == BF16 else 4
        n = 1
        for s in shape_free:
            n *= s
        nb = (n * esz + 63) // 64 * 64
        assert self.off + nb <= self.nbytes, f"arena overflow: {self.off}+{nb}>{self.nbytes}"
        a = self.ap[:, self.off // 4:(self.off + nb) // 4]
        self.off += nb
        self.peak = max(self.peak, self.off)
        if dtype == BF16:
            a = a.bitcast(BF16)
        a = a[:, 0:n]
        if len(shape_free) == 2:
            a = a.rearrange("p (a b) -> p a b", a=shape_free[0])
        elif len(shape_free) == 3:
            a = a.rearrange("p (a b c) -> p a b c", a=shape_free[0], b=shape_free[1])
        elif len(shape_free) == 4:
            a = a.rearrange("p (a b c d) -> p a b c d", a=shape_free[0], b=shape_free[1], c=shape_free[2])
        return a

    def mark(self):
        self.marks.append(self.off)

    def release(self):
        self.off = self.marks.pop()


D = 1024
KC = 8
NL = 2048
NCX = 256
T = NL + NCX
DFF = 2816
FCH = 22
DEPTH = 4
EPS = 1e-6
HL = 1
HC = NL + 3
HW = NL + NCX + 4
GRID_W = 64
HEAD_DIM = 128
MAGIC = 12582912.0
TWO_PI = 2.0 * math.pi
CB = 256
NCB = D // CB

_CONST_CACHE = {}


def _bf16(a):
    return np.ascontiguousarray(a.astype(ml_dtypes.bfloat16))


def _dft_tables(n):
    N = 2 * n
    nf = n // 128 + 1
    idx = np.arange(nf * 128, dtype=np.int64)
    prod = (idx[:, None] * idx[None, :]) % N
    ang = prod.astype(np.float64) * (2.0 * np.pi / N)
    valid = (idx[:, None] <= n) & (idx[None, :] <= n)
    c = np.where(valid, np.cos(ang), 0.0)
    s = np.where(valid, np.sin(ang), 0.0)
    c4 = c.reshape(nf, 128, nf, 128).transpose(2, 1, 0, 3)
    s4 = s.reshape(nf, 128, nf, 128).transpose(2, 1, 0, 3)
    return _bf16(c4), _bf16(s4)


def _consts():
    if _CONST_CACHE:
        return _CONST_CACHE
    C = {}
    for n, tag in ((NL, "L"), (NCX, "C")):
        tc, ts = _dft_tables(n)
        C["tcos" + tag] = tc
        C["tsin" + tag] = ts
        nt = n // 128
        nf = nt + 1
        t = np.linspace(0.0, 1.0, n, dtype=np.float32)
        w = (np.float32(2.0 * math.pi / n) * np.arange(n, dtype=np.float32))
        bands = np.linspace(1e-4, 15.0, 16, dtype=np.float32)
        ang = bands[None, :] * w[:, None]
        z = np.concatenate([t[:, None], np.cos(ang), -np.sin(ang)], axis=-1).astype(np.float32)
        C["zT" + tag] = np.ascontiguousarray(z.T)
        C["negt" + tag] = np.ascontiguousarray((-t).reshape(nt, 128).T)
        wf = np.full(nf * 128, 2.0, dtype=np.float64)
        wf[0] = 1.0
        wf[n] = 1.0
        wf[n + 1:] = 0.0
        C["wfn" + tag] = np.ascontiguousarray((wf / (2 * n)).astype(np.float32).reshape(nf, 128).T)
    min_decay = math.log(1e-2) / 1.5
    max_decay = math.log(1e-2) / 0.3
    C["delta"] = np.abs(np.linspace(min_decay, max_decay, D, dtype=np.float32)).astype(np.float32)
    rows = NL // GRID_W
    row = np.repeat(np.arange(rows), GRID_W).astype(np.float32)
    col = np.tile(np.arange(GRID_W), rows).astype(np.float32)
    half = HEAD_DIM // 2
    inv = (10000.0 ** (-np.arange(0, half, 2, dtype=np.float32) / half)).astype(np.float32)
    ang = np.concatenate([row[:, None] * inv, col[:, None] * inv], axis=-1)
    cos = np.cos(ang).astype(np.float32)
    sin = np.sin(ang).astype(np.float32)
    C["ropec"] = np.ascontiguousarray(np.concatenate([cos, cos], axis=1).T)
    C["ropes"] = np.ascontiguousarray(np.concatenate([sin, sin], axis=1).T)
    rp = np.zeros((128, 128), dtype=np.float32)
    for m in range(64):
        rp[m + 64, m] = -1.0
        rp[m, m + 64] = 1.0
    C["rperm"] = rp
    C["ident"] = np.eye(128, dtype=np.float32)
    sh = np.zeros((4, 128, 128), dtype=np.float32)
    for t in range(1, 128):
        sh[0, t - 1, t] = 1.0
    sh[1, 127, 0] = 1.0
    for t in range(0, 127):
        sh[2, t + 1, t] = 1.0
    sh[3, 0, 127] = 1.0
    C["shiftm"] = np.ascontiguousarray(sh.transpose(1, 0, 2))
    _CONST_CACHE.update(C)
    return C


def build(nlayers=DEPTH, nbatch=2, final_norm=True, layers=None):
    layers = list(range(nlayers)) if layers is None else list(layers)
    nc = bass.Bass("TRN2", target_bir_lowering=False)
    dr = {}

    def din(name, shape, dt=F32):
        dr[name] = nc.dram_tensor(name, list(shape), dt, kind="ExternalInput").ap()
        return dr[name]

    din("xT", [2, D, NL])
    din("ctxT", [2, D, NCX])
    din("cvec", [128, KC, 3])
    din("mod_w", [DEPTH, D, 6 * D])
    din("modb", [128, DEPTH, 6, KC])
    din("n1g", [128, DEPTH, KC])
    din("n2g", [128, DEPTH, KC])
    din("fing", [128, KC])
    din("w_gu", [DEPTH, D, 2 * DFF])
    din("w_down", [DEPTH, DFF, D])
    din("w_in", [2, D, 5 * D])
    din("ln_g", [2, D])
    din("ln_b", [2, D])
    din("w_sT", [2, 128, 8, 128])
    din("b_s", [2, 8 * 128])
    din("conv_w", [2, 3, 3 * D])
    din("conv_b", [2, 3 * D])
    din("f_w1", [2, 33, 64])
    din("f_pp", [2, 64, 4])
    din("f_w2", [2, 64, 64])
    din("f_w3", [2, 64, 4 * D])
    din("skip", [2, 2, D])
    din("w_out", [2, 2 * D, D])
    din("w_qkv", [2, D, 1536])
    din("qkg", [2, 128, 2])
    din("w_o", [2, D, D])
    for tag, n in (("L", NL), ("C", NCX)):
        nf = n // 128 + 1
        din("tcos" + tag, [nf, 128, nf, 128], BF16)
        din("tsin" + tag, [nf, 128, nf, 128], BF16)
        din("zT" + tag, [33, n])
        din("negt" + tag, [128, n // 128])
        din("wfn" + tag, [128, nf])
    din("delta", [D])
    din("ropec", [128, NL])
    din("ropes", [128, NL])
    din("rperm", [128, 128])
    din("ident", [128, 128])
    din("shiftm", [128, 4, 128])
    outT = nc.dram_tensor("outT", [2, D, NL], F32, kind="ExternalOutput").ap()
    Hs = {}
    for i in range(2):
        for tag, n in (("L", NL), ("C", NCX)):
            nf = n // 128 + 1
            Hs[(i, tag)] = nc.dram_tensor(f"Hs{i}{tag}", [2, nf, NCB, 128, 2, CB], F32, kind="Internal").ap()

    ARENA_BYTES = 211968
    cm_a = nc.sbuf_tensor("arena", [128, ARENA_BYTES // 4], F32)
    arena_t = cm_a.__enter__()
    cm_p = nc.psum_tensor("psum", [128, 8, 512], F32)
    ps = cm_p.__enter__()
    A = Arena(arena_t[:, :], ARENA_BYTES)
    P = Prog(nc)
    PB = [Buf(f"psb{i}") for i in range(8)]

    def MM(out, lhsT, rhs, start, stop, R, W):
        P.op("pe", lambda e: e.matmul(out, lhsT, rhs, start=start, stop=stop), R, W)

    def TR(out, in_, ident_ap, R, W):
        P.op("pe", lambda e: e.transpose(out, in_, ident_ap), R, W)

    def ACT(out, in_, func, R, W, bias=None, scale=None, accum=None):
        kw = {}
        if bias is not None:
            kw["bias"] = bias
        if scale is not None:
            kw["scale"] = scale
        if accum is not None:
            kw["accum_out"] = accum
        P.op("act", lambda e: e.activation(out=out, in_=in_, func=func, **kw), R, W)

    def TT(eng, out, a, b, op, R, W):
        P.op(eng, lambda e: e.tensor_tensor(out=out, in0=a, in1=b, op=op), R, W)

    def TS(out, a, s1, s2, op0, op1, R, W, eng="dve"):
        P.op(eng, lambda e: e.tensor_scalar(out, a, s1, s2, op0, op1), R, W)

    def STT(out, in0, scalar, in1, op0, op1, R, W):
        P.op("dve", lambda e: e.scalar_tensor_tensor(out=out, in0=in0, scalar=scalar, in1=in1, op0=op0, op1=op1), R, W)

    def CP(eng, out, in_, R, W):
        P.op(eng, lambda e: e.tensor_copy(out=out, in_=in_), R, W)

    def MSET(eng, out, val, W):
        P.op(eng, lambda e: e.memset(out, val), (), W)

    def RECIP(out, in_, R, W):
        P.op("dve", lambda e: e.reciprocal(out=out, in_=in_), R, W)

    def wview(w2d, k0, kc, c0, ncol):
        return w2d[k0:k0 + kc * 128, c0:c0 + ncol].rearrange("(kc p) c -> p kc c", p=128)

    ident = A.alloc([128], BF16); b_ident = Buf("ident")
    ones_bf = A.alloc([128], BF16); b_ones = Buf("ones")
    ones_f = A.alloc([128], F32)
    rperm = A.alloc([128], BF16); b_rperm = Buf("rperm")
    cbias = A.alloc([4], F32); b_cbias = Buf("cbias")
    modv = A.alloc([DEPTH, 3, 6, KC], F32); b_modv = Buf("modv")
    fing = A.alloc([KC], F32); b_fing = Buf("fing")
    n1g = A.alloc([DEPTH, KC], F32); n2g = A.alloc([DEPTH, KC], F32); b_n1g = Buf("n1g"); b_n2g = Buf("n2g")
    shiftm = A.alloc([4, 128], BF16); b_shiftm = Buf("shiftm")
    P.dma(shiftm, dr["shiftm"], writes=[b_shiftm], q="pool")
    P.dma(ident, dr["ident"], writes=[b_ident], q="pool")
    P.dma(rperm, dr["rperm"], writes=[b_rperm], q="pool")
    P.dma(n1g, dr["n1g"], writes=[b_n1g])
    P.dma(n2g, dr["n2g"], writes=[b_n2g])
    P.dma(fing, dr["fing"], writes=[b_fing])
    MSET("pool", ones_f, 1.0, [b_ones])
    MSET("pool", ones_bf, 1.0, [b_ones])
    MSET("pool", cbias[:, 0:1], D * EPS, [b_cbias])
    MSET("pool", cbias[:, 1:2], 128 * EPS, [b_cbias])
    MSET("pool", cbias[:, 2:3], 1e-5, [b_cbias])
    MSET("pool", cbias[:, 3:4], 0.0, [b_cbias])

    A.mark()
    scT = A.alloc([KC, 3], F32); b_sc = Buf("scT")
    modb = A.alloc([DEPTH, 6, KC], F32); b_modb = Buf("modb")
    P.dma(scT, dr["cvec"], writes=[b_sc])
    P.dma(modb, dr["modb"], writes=[b_modb])
    ACT(scT, scT, AF.Silu, [b_sc], [b_sc])
    scTb = A.alloc([KC, 3], BF16); b_scb = Buf("scTb")
    CP("dve", scTb, scT, [b_sc], [b_scb])
    mwt = [A.alloc([KC, 512], BF16) for _ in range(3)]
    b_mw = [P.dbuf(f"mw{i}") for i in range(3)]
    it = 0
    for l in layers:
        for j in range(12):
            s = it % 3
            P.dma(mwt[s], wview(dr["mod_w"][l], 0, KC, j * 512, 512), writes=[b_mw[s]], q="pool")
            six = j // 2
            for cc in range(4):
                chunk = (j % 2) * 4 + cc
                pb = PB[(it * 4 + cc) % 8]
                for k in range(KC):
                    MM(ps[:, (it * 4 + cc) % 8, 0:3], mwt[s][:, k, cc * 128:(cc + 1) * 128], scTb[:, k, :],
                       k == 0, k == KC - 1, [b_mw[s], b_scb], [pb])
                TS(modv[:, l, :, six, chunk], ps[:, (it * 4 + cc) % 8, 0:3], modb[:, l, six, chunk:chunk + 1], 0.0,
                   ALU.add, ALU.add, [pb, b_modb], [b_modv])
            it += 1
        for r in range(3):
            TS(modv[:, l, r, 1, :], modv[:, l, r, 1, :], 1.0, 32.0, ALU.add, ALU.mult, [b_modv], [b_modv])
            TT("dve", modv[:, l, r, 1, :], modv[:, l, r, 1, :], n1g[:, l, :], ALU.mult, [b_modv, b_n1g], [b_modv])
            TS(modv[:, l, r, 4, :], modv[:, l, r, 4, :], 1.0, 32.0, ALU.add, ALU.mult, [b_modv], [b_modv])
            TT("dve", modv[:, l, r, 4, :], modv[:, l, r, 4, :], n2g[:, l, :], ALU.mult, [b_modv, b_n2g], [b_modv])
    TS(fing, fing, 32.0, 0.0, ALU.mult, ALU.add, [b_fing], [b_fing])
    P.barrier()
    A.release()

    b_Hs = {k: Buf(f"Hs{k}") for k in Hs}
    even_idx = sorted({l // 2 for l in layers if l % 2 == 0})

    def filter_precompute(i, tag, n):
        nt = n // 128
        nf = nt + 1
        A.mark()
        zT = A.alloc([n], F32); b_z = P.dbuf("f_z")
        w1 = A.alloc([64], F32); b_w1 = P.dbuf("f_w1")
        w2 = A.alloc([64], F32); b_w2 = P.dbuf("f_w2")
        w3 = A.alloc([4 * D], F32); b_w3 = P.dbuf("f_w3")
        fpp = A.alloc([6], F32); b_fpp = P.dbuf("f_pp")
        negt = A.alloc([nt], F32); b_negt = P.dbuf("f_negt")
        wfn = A.alloc([nf], F32); b_wfn = P.dbuf("f_wfn")
        dl = A.alloc([D], F32); b_dl = P.dbuf("f_dl")
        hid1 = A.alloc([n], F32); b_h1 = Buf("hid1")
        hid2 = A.alloc([n], F32); b_h2 = Buf("hid2")
        P.dma(zT[0:33, :], dr["zT" + tag], writes=[b_z])
        P.dma(w1[0:33, :], dr["f_w1"][i], writes=[b_w1])
        P.dma(w2[0:64, :], dr["f_w2"][i], writes=[b_w2])
        P.dma(w3[0:64, :], dr["f_w3"][i], writes=[b_w3])
        P.dma(fpp[0:64, 0:4], dr["f_pp"][i], writes=[b_fpp])
        P.dma(negt, dr["negt" + tag], writes=[b_negt])
        P.dma(wfn, dr["wfn" + tag], writes=[b_wfn])
        P.dma(dl, dr["delta"].partition_broadcast(128), writes=[b_dl])
        TT("dve", fpp[0:64, 4:5], fpp[0:64, 0:1], fpp[0:64, 1:2], ALU.mult, [b_fpp], [b_fpp])
        TT("dve", fpp[0:64, 5:6], fpp[0:64, 2:3], fpp[0:64, 3:4], ALU.mult, [b_fpp], [b_fpp])
        tmpa = A.alloc([512], F32); b_ta = Buf("tmpa")
        tmpb = A.alloc([512], F32); b_tb = Buf("tmpb")
        bw = min(512, n)
        for stage in range(2):
            src, b_src, kk = (zT, b_z, 33) if stage == 0 else (hid1, b_h1, 64)
            wt, b_wt = (w1, b_w1) if stage == 0 else (w2, b_w2)
            dst, b_dst = (hid1, b_h1) if stage == 0 else (hid2, b_h2)
            fcol = 0 if stage == 0 else 2
            for blk in range(n // bw):
                cs = slice(blk * bw, (blk + 1) * bw)
                pb = PB[blk % 2]
                MM(ps[0:64, blk % 2, 0:bw], wt[0:kk, 0:64], src[0:kk, cs], True, True, [b_wt, b_src], [pb])
                TS(tmpa[0:64, 0:bw], ps[0:64, blk % 2, 0:bw], fpp[0:64, fcol:fcol + 1], fpp[0:64, 4 + stage:5 + stage],
                   ALU.mult, ALU.add, [pb, b_fpp], [b_ta])
                TS(tmpb[0:64, 0:bw], tmpa[0:64, 0:bw], 1.0 / TWO_PI, MAGIC, ALU.mult, ALU.add, [b_ta], [b_tb])
                TS(tmpb[0:64, 0:bw], tmpb[0:64, 0:bw], MAGIC, -TWO_PI, ALU.subtract, ALU.mult, [b_tb], [b_tb])
                TT("dve", tmpa[0:64, 0:bw], tmpa[0:64, 0:bw], tmpb[0:64, 0:bw], ALU.add, [b_ta, b_tb], [b_ta])
                ACT(dst[0:64, cs], tmpa[0:64, 0:bw], AF.Sin, [b_ta], [b_dst])
        hsum = A.alloc([nt, D], BF16); b_hs = Buf("hsum")
        hdif = A.alloc([nt, D], BF16); b_hd = Buf("hdif")
        hw_ = [A.alloc([2, D], F32) for _ in range(2)]; b_hw_ = [Buf("hw0"), Buf("hw1")]
        habs_ = [A.alloc([2, D], BF16) for _ in range(2)]; b_ha_ = [Buf("habs0"), Buf("habs1")]
        win_ = [A.alloc([D], F32) for _ in range(2)]; b_win_ = [Buf("win0"), Buf("win1")]
        rl1 = A.alloc([D], F32); b_rl1 = Buf("rl1")
        tct = [A.alloc([nf, 128], BF16) for _ in range(2)]
        tst = [A.alloc([nf, 128], BF16) for _ in range(2)]
        b_tc = [P.dbuf(f"tc{s}") for s in range(2)]
        b_ts = [P.dbuf(f"ts{s}") for s in range(2)]
        hst = [A.alloc([2, D], F32) for _ in range(2)]
        b_hst = [Buf(f"hst{s}") for s in range(2)]
        hsc = Hs[(i, tag)]
        git = 0
        for o in range(2):
            for tc in range(nt):
                hw, b_hw = hw_[tc % 2], b_hw_[tc % 2]
                habs, b_ha = habs_[tc % 2], b_ha_[tc % 2]
                win, b_win = win_[tc % 2], b_win_[tc % 2]
                ACT(win, dl, AF.Exp, [b_dl, b_negt], [b_win], scale=negt[:, tc:tc + 1])
                for dirn in range(2):
                    for ch in range(2):
                        col0 = dirn * 2 * D + o * D + ch * 512
                        bank = (2 if tc % 2 == 0 else 6) + (dirn * 2 + ch) % 2
                        MM(ps[:, bank, :], hid2[0:64, tc * 128:(tc + 1) * 128], w3[0:64, col0:col0 + 512], True, True,
                           [b_h2, b_w3], [PB[bank]])
                        TT("dve", hw[:, dirn, ch * 512:(ch + 1) * 512], ps[:, bank, :], win[:, ch * 512:(ch + 1) * 512],
                           ALU.mult, [PB[bank], b_win], [b_hw])
                if tc == 0:
                    MSET("dve", hw[0:1, 1, :], 0.0, [b_hw])
                TT("pool", hsum[:, tc, :], hw[:, 0, :], hw[:, 1, :], ALU.add, [b_hw], [b_hs])
                TT("pool", hdif[:, tc, :], hw[:, 1, :], hw[:, 0, :], ALU.subtract, [b_hw], [b_hd])
                ACT(habs, hw, AF.Abs, [b_hw], [b_ha])
                for ch in range(2):
                    for dirn in range(2):
                        MM(ps[:, ch, :], ones_bf, habs[:, dirn, ch * 512:(ch + 1) * 512],
                           tc == 0 and dirn == 0, tc == nt - 1 and dirn == 1, [b_ones, b_ha], [PB[ch]])
            for ch in range(2):
                RECIP(rl1[:, ch * 512:(ch + 1) * 512], ps[:, ch, :], [PB[ch]], [b_rl1])
            def load_tables(fc_, s_):
                P.dma(tct[s_], dr["tcos" + tag][fc_], writes=[b_tc[s_]])
                if fc_ != nf - 1:
                    P.dma(tst[s_], dr["tsin" + tag][fc_], writes=[b_ts[s_]])

            load_tables(0, git % 2)
            for fc in range(nf):
                s = git % 2
                git += 1
                nyq = fc == nf - 1
                if fc + 1 < nf:
                    load_tables(fc + 1, git % 2)
                mrow = 1 if nyq else 128
                mcol = 1 if nyq else 128
                if nyq:
                    MSET("dve", hst[s], 0.0, [b_hst[s]])
                for ch in range(2):
                    cs = slice(ch * 512, (ch + 1) * 512)
                    bank = 4 + ch * 2
                    for tc in range(nt):
                        MM(ps[0:mrow, bank, :], tct[s][:, tc, 0:mcol], hsum[:, tc, cs], tc == 0, tc == nt - 1,
                           [b_tc[s], b_hs], [PB[bank]])
                    STT(hst[s][0:mrow, 0, cs], ps[0:mrow, bank, :], wfn[0:mrow, fc:fc + 1], rl1[0:mrow, cs], ALU.mult, ALU.mult,
                        [PB[bank], b_wfn, b_rl1], [b_hst[s]])
                    if not nyq:
                        for tc in range(nt):
                            MM(ps[:, bank + 1, :], tst[s][:, tc, :], hdif[:, tc, cs], tc == 0, tc == nt - 1,
                               [b_ts[s], b_hd], [PB[bank + 1]])
                        STT(hst[s][:, 1, cs], ps[:, bank + 1, :], wfn[:, fc:fc + 1], rl1[:, cs], ALU.mult, ALU.mult,
                            [PB[bank + 1], b_wfn, b_rl1], [b_hst[s]])
                P.dma(hsc[o, fc].rearrange("cb p ri c -> p ri cb c"),
                      hst[s].rearrange("p ri (cb c) -> p ri cb c", cb=NCB),
                      reads=[b_hst[s]], writes=[P.dbuf(f"hsw{s}"), b_Hs[(i, tag)]])
        P.barrier()
        A.release()

    for i in even_idx:
        filter_precompute(i, "L", NL)
        if 2 * i < DEPTH - 1:
            filter_precompute(i, "C", NCX)

    xT = A.alloc([KC, T], F32)
    hT = A.alloc([KC, HW], BF16)
    TBLK = [(0, 512), (512, 512), (1024, 512), (1536, 512), (2048, 256)]
    b_x = [[Buf(f"x{c}_{j}") for j in range(5)] for c in range(KC)]
    b_h = [Buf(f"h{j}") for j in range(5)]

    def xbufs(c, x0, n):
        return [b_x[c][j] for j, (s, ln) in enumerate(TBLK) if s < x0 + n and x0 < s + ln]

    def hbufs(x0, n):
        return [b_h[j] for j, (s, ln) in enumerate(TBLK) if s < x0 + n and x0 < s + ln]

    def hcol(x0):
        return HL + x0 if x0 < NL else HC + (x0 - NL)

    MSET("pool", hT, 0.0, b_h)

    def norm_alloc():
        nb = {}
        nb["sq"] = [A.alloc([512], BF16) for _ in range(2)]; nb["b_sq"] = [Buf("sq0"), Buf("sq1")]
        nb["lnv"] = [A.alloc([512], F32) for _ in range(2)]; nb["b_ln"] = [Buf("lnv0"), Buf("lnv1")]
        nb["rstd"] = [A.alloc([512], F32) for _ in range(2)]; nb["b_rs"] = [Buf("rstd0"), Buf("rstd1")]
        nb["tmp"] = [A.alloc([512], F32) for _ in range(2)]; nb["b_tmp"] = [Buf("nt0"), Buf("nt1")]
        return nb

    def norm_stats(nb, l, which, b, j, bank):
        x0, n = TBLK[j]
        sq, b_sq = nb["sq"], nb["b_sq"]
        p_ = j % 2
        for c in range(KC):
            s = c % 2
            if c % 2 == 0:
                ACT(sq[s][:, 0:n], xT[:, c, x0:x0 + n], AF.Square, [b_x[c][j]], [b_sq[s]])
            else:
                TT("dve", sq[s][:, 0:n], xT[:, c, x0:x0 + n], xT[:, c, x0:x0 + n], ALU.mult, [b_x[c][j]], [b_sq[s]])
            MM(ps[:, bank, 0:n], ones_bf, sq[s][:, 0:n], c == 0, c == KC - 1, [b_ones, b_sq[s]], [PB[bank]])
        ACT(nb["lnv"][p_][:, 0:n], ps[:, bank, 0:n], AF.Ln, [PB[bank], b_cbias], [nb["b_ln"][p_]], bias=cbias[:, 0:1])
        ACT(nb["rstd"][p_][:, 0:n], nb["lnv"][p_][:, 0:n], AF.Exp, [nb["b_ln"][p_]], [nb["b_rs"][p_]], scale=-0.5)

    def norm_apply(nb, l, which, b, j):
        so = 0 if which == 1 else 3
        x0, n = TBLK[j]
        row = b if x0 < NL else 2
        p_ = j % 2
        tmp, b_tmp = nb["tmp"], nb["b_tmp"]
        h0 = hcol(x0)
        for c in range(KC):
            s = c % 2
            STT(tmp[s][:, 0:n], xT[:, c, x0:x0 + n], modv[:, l, row, so + 1, c:c + 1], nb["rstd"][p_][:, 0:n], ALU.mult, ALU.mult,
                [b_x[c][j], b_modv, nb["b_rs"][p_]], [b_tmp[s]])
            ACT(hT[:, c, h0:h0 + n], tmp[s][:, 0:n], AF.Identity, [b_tmp[s], b_modv], [b_h[j]],
                bias=modv[:, l, row, so, c:c + 1])

    def norm_block(nb, l, which, b, j, bank):
        norm_stats(nb, l, which, b, j, bank)
        norm_apply(nb, l, which, b, j)

    def norm_blocks(nb, l, which, b, js, banks=(0, 1)):
        js = list(js)
        pipeline(js, [lambda i_, j: norm_stats(nb, l, which, b, j, banks[j % 2]),
                      lambda i_, j: norm_apply(nb, l, which, b, j)])

    def norm_phase(l, which, b, with_ctx):
        A.mark()
        nb = norm_alloc()
        norm_blocks(nb, l, which, b, range(5 if with_ctx else 4))
        A.release()

    def ffn_phase(l, b, with_ctx):
        A.mark()
        nb = norm_alloc()
        norm_blocks(nb, l, 2, b, range(3))
        late_norms = [3, 4] if with_ctx else [3]
        subs = [(0, 384), (384, 384), (768, 384), (1152, 384), (1536, 384), (1920, 128)]
        if with_ctx:
            subs.append((2048, 256))
        blocks = [subs[0:3], subs[3:]]
        aT = A.alloc([FCH, 1152], BF16)
        b_a = [Buf(f"a{s}") for s in range(4)]
        wgu = [A.alloc([KC, 256], BF16) for _ in range(3)]
        b_wgu = [P.dbuf(f"wgu{s}") for s in range(3)]
        wdn = [A.alloc([FCH, 128], BF16) for _ in range(2)]
        b_wdn = [P.dbuf(f"wdn{s}") for s in range(2)]
        sg = [A.alloc([384], F32) for _ in range(2)]; b_sg = [Buf("sg0"), Buf("sg1")]
        wi = 0; di = 0; si = 0; pi = 0
        for blk in blocks:
            offs = []
            o = 0
            for (x0, n) in blk:
                offs.append(o); o += n
            for f in range(FCH):
                s = wi % 3; wi += 1
                P.dma(wgu[s], wview(dr["w_gu"][l], 0, KC, f * 256, 256), writes=[b_wgu[s]], q="pool")
                if f in (2, 5) and late_norms:
                    norm_block(nb, l, 2, b, late_norms.pop(0), 3)
                for sb, (x0, n) in enumerate(blk):
                    h0 = hcol(x0)
                    bg = pi % 2; bu = 2 + pi % 2; pi += 1
                    for k in range(KC):
                        MM(ps[:, bg, 0:n], wgu[s][:, k, 0:128], hT[:, k, h0:h0 + n], k == 0, k == KC - 1,
                           [b_wgu[s]] + hbufs(x0, n), [PB[bg]])
                    for k in range(KC):
                        MM(ps[:, bu, 0:n], wgu[s][:, k, 128:256], hT[:, k, h0:h0 + n], k == 0, k == KC - 1,
                           [b_wgu[s]] + hbufs(x0, n), [PB[bu]])
                    ss = si % 2; si += 1
                    ACT(sg[ss][:, 0:n], ps[:, bg, 0:n], AF.Silu, [PB[bg]], [b_sg[ss]])
                    TT("dve", aT[:, f, offs[sb]:offs[sb] + n], sg[ss][:, 0:n], ps[:, bu, 0:n], ALU.mult,
                       [b_sg[ss], PB[bu]], [b_a[sb]])
            for dc in range(KC):
                s = di % 2; di += 1
                P.dma(wdn[s], wview(dr["w_down"][l], 0, FCH, dc * 128, 128), writes=[b_wdn[s]], q="pool")
                for sb, (x0, n) in enumerate(blk):
                    row = b if x0 < NL else 2
                    bank = 4 + pi % 4; pi += 1
                    for f in range(FCH):
                        MM(ps[:, bank, 0:n], wdn[s][:, f, :], aT[:, f, offs[sb]:offs[sb] + n], f == 0, f == FCH - 1,
                           [b_wdn[s], b_a[sb]], [PB[bank]])
                    xb = xbufs(dc, x0, n)
                    STT(xT[:, dc, x0:x0 + n], ps[:, bank, 0:n], modv[:, l, row, 5, dc:dc + 1], xT[:, dc, x0:x0 + n],
                        ALU.mult, ALU.add, [PB[bank], b_modv] + xb, xb)
        A.release()

    def pipeline(items, stages):
        n = len(items)
        S = len(stages)
        for step in range(n + S - 1):
            for s_ in range(S):
                i = step - s_
                if 0 <= i < n and stages[s_] is not None:
                    stages[s_](i, items[i])

    def attn_phase(l, b, last):
        j = l // 2
        A.mark()
        ropec = A.alloc([NL], F32); ropes = A.alloc([NL], F32)
        b_ropec = P.dbuf("ropec"); b_ropes = P.dbuf("ropes")
        P.dma(ropec, dr["ropec"], writes=[b_ropec])
        P.dma(ropes, dr["ropes"], writes=[b_ropes])
        qkg = A.alloc([2], F32); b_qkg = P.dbuf("qkg")
        P.dma(qkg, dr["qkg"][j], writes=[b_qkg])
        TS(qkg, qkg, math.sqrt(128.0), 0.0, ALU.mult, ALU.add, [b_qkg], [b_qkg])
        qT = A.alloc([4, T], BF16); b_q = [[Buf(f"q{h}_{jb}") for jb in range(5)] for h in range(4)]
        kT = A.alloc([T], BF16); b_k = [Buf(f"k{jb}") for jb in range(5)]
        V = A.alloc([18, 128], BF16); b_v = [Buf(f"v{jb}") for jb in range(5)]
        oT = A.alloc([4, 512], BF16); b_o = [Buf(f"o{h}") for h in range(4)]
        wq = [A.alloc([KC, 128], BF16) for _ in range(2)]; b_wq = [P.dbuf(f"wq{s}") for s in range(2)]
        wv = A.alloc([KC, 128], BF16); b_wv = P.dbuf("wv")
        wo = A.alloc([4, D], BF16); b_wo = P.dbuf("wo")
        sqb = [A.alloc([512], BF16) for _ in range(3)]; b_sqb = [Buf(f"asq{s}") for s in range(3)]
        lnv = [A.alloc([512], F32) for _ in range(2)]; b_ln = [Buf(f"alnv{s}") for s in range(2)]
        rstd = [A.alloc([512], F32) for _ in range(2)]; b_rs = [Buf(f"arstd{s}") for s in range(2)]
        qn = [A.alloc([512], BF16) for _ in range(3)]; b_qn = [Buf(f"qn{s}") for s in range(3)]
        t1 = [A.alloc([512], F32) for _ in range(2)]; b_t1 = [Buf(f"t1{s}") for s in range(2)]
        t2 = [A.alloc([512], F32) for _ in range(2)]; b_t2 = [Buf(f"t2{s}") for s in range(2)]
        pT = [A.alloc([512], BF16) for _ in range(3)]; b_pT = [Buf(f"pT{s}") for s in range(3)]
        rden = [A.alloc([512], F32) for _ in range(2)]; b_rd = [Buf(f"rden{s}") for s in range(2)]
        blks = TBLK
        hcount = [0]

        for g in range(2):
            P.dma(wo, wview(dr["w_o"][j], g * 512, 4, 0, D), writes=[b_wo], q="pool")
            P.dma(wv, wview(dr["w_qkv"][j], 0, KC, 1280 + g * 128, 128), writes=[b_wv], q="pool")
            for tt in range(18):
                x0 = tt * 128
                jb = min(x0 // 512, 4)
                h0 = hcol(x0)
                bank = 6 + tt % 2
                for k in range(KC):
                    MM(ps[:, bank, 0:128], hT[:, k, h0:h0 + 128], wv[:, k, :], k == 0, k == KC - 1, [b_wv, b_h[jb]], [PB[bank]])
                ACT(V[:, tt, :], ps[:, bank, 0:128], AF.Copy, [PB[bank]], [b_v[jb]])
            qblist = list(range(4)) if last else list(range(5))
            heads = [(1024 + g * 128, 1, kT, b_k, list(range(5)))]
            for h in range(4):
                heads.append(((g * 4 + h) * 128, 0, qT[:, h, :], b_q[h], qblist))
            items = []
            for hi, hd in enumerate(heads):
                for bi, jb in enumerate(hd[4]):
                    items.append((hi, jb, bi == 0))

            def stA(i, it):
                hi, jb, firstblk = it
                col0 = heads[hi][0]
                s = (hcount[0] + hi) % 2
                if firstblk:
                    P.dma(wq[s], wview(dr["w_qkv"][j], 0, KC, col0, 128), writes=[b_wq[s]], q="pool")
                x0, n = blks[jb]
                h0 = hcol(x0)
                bank = i % 3
                for k in range(KC):
                    MM(ps[:, bank, 0:n], wq[s][:, k, :], hT[:, k, h0:h0 + n], k == 0, k == KC - 1, [b_wq[s], b_h[jb]], [PB[bank]])
                ACT(sqb[i % 3][:, 0:n], ps[:, bank, 0:n], AF.Square, [PB[bank]], [b_sqb[i % 3]])

            def stB(i, it):
                hi, jb, firstblk = it
                col0, gcol, dst, dstbufs, _ = heads[hi]
                x0, n = blks[jb]
                bank = i % 3
                sb = 3 if i % 2 == 0 else 6
                MM(ps[:, sb, 0:n], ones_bf, sqb[i % 3][:, 0:n], True, True, [b_ones, b_sqb[i % 3]], [PB[sb]])
                ACT(lnv[i % 2][:, 0:n], ps[:, sb, 0:n], AF.Ln, [PB[sb], b_cbias], [b_ln[i % 2]], bias=cbias[:, 1:2])
                ACT(rstd[i % 2][:, 0:n], lnv[i % 2][:, 0:n], AF.Exp, [b_ln[i % 2]], [b_rs[i % 2]], scale=-0.5)
                if x0 >= NL:
                    STT(dst[:, x0:x0 + n], ps[:, bank, 0:n], qkg[:, gcol:gcol + 1], rstd[i % 2][:, 0:n], ALU.mult, ALU.mult,
                        [PB[bank], b_qkg, b_rs[i % 2]], [dstbufs[jb]])
                else:
                    STT(qn[i % 3][:, 0:n], ps[:, bank, 0:n], qkg[:, gcol:gcol + 1], rstd[i % 2][:, 0:n], ALU.mult, ALU.mult,
                        [PB[bank], b_qkg, b_rs[i % 2]], [b_qn[i % 3]])

            def stC(i, it):
                hi, jb, firstblk = it
                col0, gcol, dst, dstbufs, _ = heads[hi]
                x0, n = blks[jb]
                if x0 >= NL:
                    return
                rb = 4 + i % 2
                MM(ps[:, rb, 0:n], rperm, qn[i % 3][:, 0:n], True, True, [b_rperm, b_qn[i % 3]], [PB[rb]])
                TT("pool", t1[i % 2][:, 0:n], qn[i % 3][:, 0:n], ropec[:, x0:x0 + n], ALU.mult, [b_qn[i % 3], b_ropec], [b_t1[i % 2]])
                TT("dve", t2[i % 2][:, 0:n], ps[:, rb, 0:n], ropes[:, x0:x0 + n], ALU.mult, [PB[rb], b_ropes], [b_t2[i % 2]])
                TT("pool", dst[:, x0:x0 + n], t1[i % 2][:, 0:n], t2[i % 2][:, 0:n], ALU.add, [b_t1[i % 2], b_t2[i % 2]], [dstbufs[jb]])

            pipeline(items, [stA, stB, stC])
            hcount[0] += len(heads)

            aitems = []
            for jb in qblist:
                x0_, n_ = blks[jb]
                ktiles = list(range(18)) if x0_ < NL else [16, 17]
                for h in range(4):
                    for ki, kt in enumerate(ktiles):
                        aitems.append((jb, h, ki, kt, len(ktiles)))
            pend = []

            def emit_wo(jb):
                x0, n = blks[jb]
                row = b if x0 < NL else 2
                for dc in range(KC):
                    bank = 2 + dc % 2
                    for h in range(4):
                        MM(ps[:, bank, 0:n], wo[:, h, dc * 128:(dc + 1) * 128], oT[:, h, 0:n], h == 0, h == 3, [b_wo, b_o[h]], [PB[bank]])
                    STT(xT[:, dc, x0:x0 + n], ps[:, bank, 0:n], modv[:, l, row, 2, dc:dc + 1], xT[:, dc, x0:x0 + n],
                        ALU.mult, ALU.add, [PB[bank], b_modv, b_x[dc][jb]], [b_x[dc][jb]])

            def stS(i, it):
                jb, h, ki, kt, nk = it
                x0, n = blks[jb]
                kjb = min(kt // 4, 4)
                MM(ps[:, i % 2, 0:n], kT[:, kt * 128:(kt + 1) * 128], qT[:, h, x0:x0 + n], True, True,
                   [b_k[kjb], b_q[h][jb]], [PB[i % 2]])
                ACT(pT[i % 3][:, 0:n], ps[:, i % 2, 0:n], AF.Exp, [PB[i % 2]], [b_pT[i % 3]], scale=1.0 / math.sqrt(128.0))

            def stPV(i, it):
                jb, h, ki, kt, nk = it
                x0, n = blks[jb]
                kjb = min(kt // 4, 4)
                ob = 4 + h % 2
                db = 6 + h % 2
                MM(ps[:, ob, 0:n], V[:, kt, :], pT[i % 3][:, 0:n], ki == 0, ki == nk - 1, [b_v[kjb], b_pT[i % 3]], [PB[ob]])
                MM(ps[:, db, 0:n], ones_bf, pT[i % 3][:, 0:n], ki == 0, ki == nk - 1, [b_ones, b_pT[i % 3]], [PB[db]])
                if ki == nk - 1:
                    while pend and pend[0][1] != jb:
                        emit_wo(pend.pop(0)[1])
                    RECIP(rden[h % 2][:, 0:n], ps[:, db, 0:n], [PB[db]], [b_rd[h % 2]])
                    TT("dve", oT[:, h, 0:n], ps[:, ob, 0:n], rden[h % 2][:, 0:n], ALU.mult, [PB[ob], b_rd[h % 2]], [b_o[h]])
                    if h == 3:
                        pend.append((i + 3, jb))
                while pend and pend[0][0] <= i:
                    emit_wo(pend.pop(0)[1])

            pipeline(aitems, [stS, None, stPV])
            while pend:
                emit_wo(pend.pop(0)[1])
        A.release()

    def gmlp_phase(l, b, with_ctx):
        i = l // 2
        A.mark()
        wvv = [A.alloc([KC, 512], BF16) for _ in range(2)]; b_wvv = [P.dbuf(f"gwv{s}") for s in range(2)]
        wuo = [A.alloc([KC, 256], BF16) for _ in range(4)]; b_wuo = [P.dbuf(f"gwuo{s}") for s in range(4)]
        lng = A.alloc([D], F32); lnb = A.alloc([D], F32); b_lng = P.dbuf("lng"); b_lnb = P.dbuf("lnb")
        wsT = A.alloc([8, 128], BF16); b_ws = P.dbuf("wsT")
        bsb = A.alloc([8, 128], F32); b_bs = P.dbuf("bsb")
        P.dma(wvv[0], wview(dr["w_in"][i], 0, KC, 1024, 512), writes=[b_wvv[0]], q="pool")
        P.dma(wvv[1], wview(dr["w_in"][i], 0, KC, 1536, 512), writes=[b_wvv[1]], q="pool")
        P.dma(lng, dr["ln_g"][i].partition_broadcast(128), writes=[b_lng])
        P.dma(lnb, dr["ln_b"][i].partition_broadcast(128), writes=[b_lnb])
        P.dma(wsT, dr["w_sT"][i], writes=[b_ws], q="pool")
        P.dma(bsb.rearrange("p g q -> p (g q)"), dr["b_s"][i].partition_broadcast(128), writes=[b_bs])
        vg = [A.alloc([D], F32) for _ in range(4)]; b_vg = [Buf(f"vg{c}") for c in range(4)]
        junk = A.alloc([D], BF16); b_junk = Buf("junk")
        st = A.alloc([8, 4], F32); b_st = Buf("st")
        vnb = [A.alloc([D], BF16) for _ in range(2)]; b_vnb = [Buf("vnb0"), Buf("vnb1")]
        svt = [A.alloc([4, 128], F32) for _ in range(2)]; b_svt = [Buf("svt0"), Buf("svt1")]
        ug = A.alloc([8, 512], BF16); b_ug = Buf("ug")
        ya = A.alloc([8, 512], BF16); b_ya = [Buf(f"ya{c}") for c in range(4)]
        blks = TBLK if with_ctx else TBLK[:4]
        sched = []
        for jb in range(len(blks)):
            for t_ in range(4):
                sched.append((dr["w_in"][i], t_ * 256))
            for t_ in range(4):
                sched.append((dr["w_out"][i], t_ * 256))
        issued = [0]

        def need(idx):
            while issued[0] <= min(idx + 2, len(sched) - 1):
                src, c0 = sched[issued[0]]
                s_ = issued[0] % 4
                P.dma(wuo[s_], wview(src, 0, KC, c0, 256), writes=[b_wuo[s_]], q="pool")
                issued[0] += 1
            return idx % 4

        ti = 0
        for jb, (x0, n) in enumerate(blks):
            row = b if x0 < NL else 2
            nch = n // 128
            h0 = hcol(x0)
            MSET("dve", st[:, 0:2, :], 0.0, [b_st])
            for ci in range(nch):
                p_ = ci % 2
                hh = hcol(x0 + ci * 128)
                for hf in range(2):
                    for k in range(KC):
                        MM(ps[:, 2 * p_ + hf, :], hT[:, k, hh:hh + 128], wvv[hf][:, k, :], k == 0, k == KC - 1,
                           [b_wvv[hf], b_h[jb]], [PB[2 * p_ + hf]])
                ACT(vg[ci].rearrange("p (a b) -> p a b", a=2), ps[:, 2 * p_:2 * p_ + 2, :], AF.Gelu,
                    [PB[2 * p_], PB[2 * p_ + 1]], [b_vg[ci], b_st], accum=st[:, 0, ci:ci + 1])
                ACT(junk, vg[ci], AF.Square, [b_vg[ci]], [b_junk, b_st], accum=st[:, 1, ci:ci + 1])
            for gp in range(4):
                s_ = need(ti); ti += 1
                for gg in range(2):
                    g = gp * 2 + gg
                    bank = 6 + g % 2
                    for k in range(KC):
                        MM(ps[:, bank, 0:n], wuo[s_][:, k, gg * 128:(gg + 1) * 128], hT[:, k, h0:h0 + n], k == 0, k == KC - 1,
                           [b_wuo[s_], b_h[jb]], [PB[bank]])
                    ACT(ug[:, g, 0:n], ps[:, bank, 0:n], AF.Gelu, [PB[bank]], [b_ug])
            TS(st[:, 2, :], st[:, 0, :], 1.0 / D, 0.0, ALU.mult, ALU.add, [b_st], [b_st])
            TT("dve", st[:, 3, :], st[:, 2, :], st[:, 2, :], ALU.mult, [b_st], [b_st])
            STT(st[:, 4, :], st[:, 1, :], 1.0 / D, st[:, 3, :], ALU.mult, ALU.subtract, [b_st], [b_st])
            ACT(st[:, 5, :], st[:, 4, :], AF.Ln, [b_st, b_cbias], [b_st], bias=cbias[:, 2:3])
            ACT(st[:, 6, :], st[:, 5, :], AF.Exp, [b_st], [b_st], scale=-0.5)
            STT(st[:, 7, :], st[:, 2, :], -1.0, st[:, 6, :], ALU.mult, ALU.mult, [b_st], [b_st])
            for ci in range(nch):
                p_ = ci % 2
                ACT(vg[ci], vg[ci], AF.Identity, [b_vg[ci], b_st], [b_vg[ci]], bias=st[:, 7, ci:ci + 1], scale=st[:, 6, ci:ci + 1])
                TT("dve", vg[ci], vg[ci], lng, ALU.mult, [b_vg[ci], b_lng], [b_vg[ci]])
                TT("dve", vnb[p_], vg[ci], lnb, ALU.add, [b_vg[ci], b_lnb], [b_vnb[p_]])
                for g in range(8):
                    bank = 4 + (g // 4)
                    MM(ps[:, bank, (g % 4) * 128:(g % 4 + 1) * 128], vnb[p_][:, g * 128:(g + 1) * 128], wsT[:, g, :], True, True,
                       [b_vnb[p_], b_ws], [PB[bank]])
                for hf in range(2):
                    TT("dve", svt[hf], ps[:, 4 + hf, :].rearrange("p (g q) -> p g q", g=4), bsb[:, hf * 4:(hf + 1) * 4, :], ALU.add,
                       [PB[4 + hf], b_bs], [b_svt[hf]])
                    TT("pool", ya[:, hf * 4:(hf + 1) * 4, ci * 128:(ci + 1) * 128], ug[:, hf * 4:(hf + 1) * 4, ci * 128:(ci + 1) * 128],
                       svt[hf], ALU.mult, [b_ug, b_svt[hf]], [b_ya[ci]])
            for dp in range(4):
                s_ = need(ti); ti += 1
                for dd in range(2):
                    dc = dp * 2 + dd
                    bank = 6 + dc % 2
                    for g in range(8):
                        MM(ps[:, bank, 0:n], wuo[s_][:, g, dd * 128:(dd + 1) * 128], ya[:, g, 0:n], g == 0, g == 7,
                           [b_wuo[s_]] + b_ya[0:nch], [PB[bank]])
                    STT(xT[:, dc, x0:x0 + n], ps[:, bank, 0:n], modv[:, l, row, 2, dc:dc + 1], xT[:, dc, x0:x0 + n],
                        ALU.mult, ALU.add, [PB[bank], b_modv, b_x[dc][jb]], [b_x[dc][jb]])
        A.release()

    def hyena_phase(l, b, tag):
        i = l // 2
        n = NL if tag == "L" else NCX
        xbase = 0 if tag == "L" else NL
        row = b if tag == "L" else 2
        nt = n // 128
        nf = nt + 1
        hsc = Hs[(i, tag)]
        b_hsc = b_Hs[(i, tag)]
        A.mark()
        wB = [A.alloc([KC, CB], BF16) for _ in range(2)]; b_wB = [P.dbuf(f"hwB{s}") for s in range(2)]
        Q0 = [A.alloc([CB], BF16) for _ in range(3)]; b_Q0 = [Buf(f"Q0_{s}") for s in range(3)]
        Q2 = [A.alloc([CB], BF16) for _ in range(3)]; b_Q2 = [Buf(f"Q2_{s}") for s in range(3)]
        C1 = [A.alloc([CB], F32) for _ in range(3)]; b_C1 = [Buf(f"C1_{s}") for s in range(3)]
        taps = [A.alloc([4, CB], F32) for _ in range(2)]; b_taps = [P.dbuf(f"taps{s}") for s in range(2)]
        b_tapsb = [P.dbuf(f"tapsb{s}") for s in range(2)]
        skp = A.alloc([2, CB], F32); b_skp = P.dbuf("skp")
        zb = A.alloc([nt, CB], BF16); b_zb = [Buf(f"zb{t}") for t in range(nt)]
        Yr = A.alloc([nf, CB], BF16); Yi = A.alloc([nf, CB], BF16)
        b_Y = [Buf(f"Y{f}") for f in range(nf)]
        zT = A.alloc([CB // 128, n], BF16); b_zT = [Buf(f"zT{j}") for j in range((n + 511) // 512)]
        tct = [A.alloc([nf, 128], BF16) for _ in range(2)]; b_tc = [P.dbuf(f"tc{s}") for s in range(2)]
        tst = [A.alloc([nf, 128], BF16) for _ in range(2)]; b_ts = [P.dbuf(f"ts{s}") for s in range(2)]
        Ht = [A.alloc([2, CB], F32) for _ in range(2)]; b_Ht = [P.dbuf(f"Ht{s}") for s in range(2)]
        wo = A.alloc([CB // 128, D], BF16); b_wo = P.dbuf("hwo")
        e1 = A.alloc([CB], F32); b_e1 = Buf("e1")
        e2 = A.alloc([CB], F32); b_e2 = Buf("e2")
        e3 = A.alloc([CB], F32); b_e3 = Buf("e3")
        e4 = A.alloc([CB], F32); b_e4 = Buf("e4")
        z2 = [A.alloc([CB], BF16) for _ in range(2)]; b_z2 = [Buf("z2a"), Buf("z2b")]
        cnt = {"w": 0, "t": 0, "h": 0, "o": 0, "z": 0}

        def prep_weights(cb, third):
            s = cnt["w"] % 2; cnt["w"] += 1
            col0 = third * D + cb * CB
            P.dma(wB[s], wview(dr["w_in"][i], 0, KC, 2 * D + col0, CB), writes=[b_wB[s]], q="pool")
            P.dma(taps[s][:, 0:3, :], dr["conv_w"][i][:, col0:col0 + CB].partition_broadcast(128), writes=[b_taps[s]])
            P.dma(taps[s][:, 3, :], dr["conv_b"][i][col0:col0 + CB].partition_broadcast(128), writes=[b_tapsb[s]])
            return s

        def proj_stage(s, tc):
            x0 = xbase + tc * 128
            h0 = hcol(x0)
            jb = min(x0 // 512, 4)
            pbank = tc % 2
            sl = tc % 3
            for k in range(KC):
                MM(ps[:, pbank, 0:CB], hT[:, k, h0:h0 + 128], wB[s][:, k, :], k == 0, k == KC - 1, [b_wB[s], b_h[jb]], [PB[pbank]])
            TT("dve", Q0[sl], ps[:, pbank, 0:CB], taps[s][:, 0, :], ALU.mult, [PB[pbank], b_taps[s]], [b_Q0[sl]])
            TT("dve", Q2[sl], ps[:, pbank, 0:CB], taps[s][:, 2, :], ALU.mult, [PB[pbank], b_taps[s]], [b_Q2[sl]])
            TT("dve", C1[sl], ps[:, pbank, 0:CB], taps[s][:, 1, :], ALU.mult, [PB[pbank], b_taps[s]], [b_C1[sl]])
            TT("pool", C1[sl], C1[sl], taps[s][:, 3, :], ALU.add, [b_C1[sl], b_tapsb[s]], [b_C1[sl]])

        def shift_stage(tc, bank):
            sl = tc % 3
            ops = [(0, Q0[sl], b_Q0[sl]), (2, Q2[sl], b_Q2[sl])]
            if tc > 0:
                ops.append((1, Q0[(tc - 1) % 3], b_Q0[(tc - 1) % 3]))
            if tc < nt - 1:
                ops.append((3, Q2[(tc + 1) % 3], b_Q2[(tc + 1) % 3]))
            for oi, (m, q, bq) in enumerate(ops):
                MM(ps[:, bank, 0:CB], shiftm[:, m, :], q, oi == 0, oi == len(ops) - 1, [b_shiftm, bq], [PB[bank]])

        def fwd_dft(o, cb, deferred=()):
            deferred = list(deferred)
            for fc in range(nf):
                for _ in range(3):
                    if deferred:
                        deferred.pop(0)()
                nyq = fc == nf - 1
                s = cnt["t"] % 2; cnt["t"] += 1
                P.dma(tct[s], dr["tcos" + tag][fc], writes=[b_tc[s]])
                if not nyq:
                    P.dma(tst[s], dr["tsin" + tag][fc], writes=[b_ts[s]])
                hs_ = cnt["h"] % 2; cnt["h"] += 1
                P.dma(Ht[hs_], hsc[o, fc, cb], reads=[b_hsc], writes=[b_Ht[hs_]])
                m = 1 if nyq else 128
                bank = 2 * (fc % 2)
                for tc in range(nt):
                    MM(ps[0:m, bank, 0:CB], tct[s][:, tc, 0:m], zb[:, tc, :], tc == 0, tc == nt - 1, [b_tc[s], b_zb[tc]], [PB[bank]])
                if nyq:
                    MSET("pool", Yr[:, fc, :], 0.0, [b_Y[fc]])
                    MSET("pool", Yi[:, fc, :], 0.0, [b_Y[fc]])
                    TT("dve", Yr[0:1, fc, :], ps[0:1, bank, 0:CB], Ht[hs_][0:1, 0, :], ALU.mult, [PB[bank], b_Ht[hs_]], [b_Y[fc]])
                    continue
                for tc in range(nt):
                    MM(ps[:, bank + 1, 0:CB], tst[s][:, tc, :], zb[:, tc, :], tc == 0, tc == nt - 1, [b_ts[s], b_zb[tc]], [PB[bank + 1]])
                TT("dve", e1, ps[:, bank, 0:CB], Ht[hs_][:, 0, :], ALU.mult, [PB[bank], b_Ht[hs_]], [b_e1])
                TT("dve", e2, ps[:, bank + 1, 0:CB], Ht[hs_][:, 1, :], ALU.mult, [PB[bank + 1], b_Ht[hs_]], [b_e2])
                TT("pool", Yr[:, fc, :], e1, e2, ALU.add, [b_e1, b_e2], [b_Y[fc]])
                TT("dve", e3, ps[:, bank + 1, 0:CB], Ht[hs_][:, 0, :], ALU.mult, [PB[bank + 1], b_Ht[hs_]], [b_e3])
                TT("dve", e4, ps[:, bank, 0:CB], Ht[hs_][:, 1, :], ALU.mult, [PB[bank], b_Ht[hs_]], [b_e4])
                TT("pool", Yi[:, fc, :], e3, e4, ALU.subtract, [b_e3, b_e4], [b_Y[fc]])
            while deferred:
                deferred.pop(0)()

        def inv_dft_gate(o, cb, sx, final):
            pending = []
            proj_stage(sx, 0)
            for tc in range(nt):
                if tc + 1 < nt:
                    proj_stage(sx, tc + 1)
                s = cnt["t"] % 2; cnt["t"] += 1
                P.dma(tct[s], dr["tcos" + tag][tc], writes=[b_tc[s]])
                P.dma(tst[s], dr["tsin" + tag][tc], writes=[b_ts[s]])
                bank = 4 + 2 * (tc % 2)
                for fc in range(nf):
                    kk = 1 if fc == nf - 1 else 128
                    MM(ps[:, bank, 0:CB], tct[s][0:kk, fc, :], Yr[0:kk, fc, :], fc == 0, False, [b_tc[s], b_Y[fc]], [PB[bank]])
                for fc in range(nf - 1):
                    MM(ps[:, bank, 0:CB], tst[s][:, fc, :], Yi[:, fc, :], False, fc == nf - 2, [b_ts[s], b_Y[fc]], [PB[bank]])
                shift_stage(tc, bank + 1)
                TT("pool", e1, zb[:, tc, :], skp[:, o, :], ALU.mult, [b_zb[tc], b_skp], [b_e1])
                TT("dve", e2, ps[:, bank, 0:CB], e1, ALU.add, [PB[bank], b_e1], [b_e2])
                TT("dve", e3, ps[:, bank + 1, 0:CB], C1[tc % 3], ALU.add, [PB[bank + 1], b_C1[tc % 3]], [b_e3])
                if not final:
                    TT("pool", zb[:, tc, :], e2, e3, ALU.mult, [b_e2, b_e3], [b_zb[tc]])
                else:
                    zs = cnt["z"] % 2; cnt["z"] += 1
                    TT("pool", z2[zs], e2, e3, ALU.mult, [b_e2, b_e3], [b_z2[zs]])

                    def trans(zs=zs, tc=tc):
                        psb = ps[:, 2 + zs, :].bitcast(BF16)
                        for cc in range(CB // 128):
                            TR(psb[:, cc * 128:(cc + 1) * 128], z2[zs][:, cc * 128:(cc + 1) * 128], ident, [b_z2[zs], b_ident], [PB[2 + zs]])
                        ACT(zT[:, :, tc * 128:(tc + 1) * 128], psb[:, 0:CB].rearrange("p (c t) -> p c t", c=CB // 128), AF.Copy,
                            [PB[2 + zs]], [b_zT[tc // 4]])
                    pending.append(trans)
                if final and len(pending) > 1:
                    pending.pop(0)()
            while pending:
                pending.pop(0)()

        for cb in range(NCB):
            P.dma(skp, dr["skip"][i][:, cb * CB:(cb + 1) * CB].partition_broadcast(128), writes=[b_skp])
            P.dma(wo, wview(dr["w_out"][i], D + cb * CB, CB // 128, 0, D), writes=[b_wo], q="pool")
            sv_ = prep_weights(cb, 0)
            sx1 = prep_weights(cb, 1)
            proj_stage(sv_, 0)
            for tc in range(nt):
                if tc + 1 < nt:
                    proj_stage(sv_, tc + 1)
                bank = 6 + tc % 2
                shift_stage(tc, bank)
                TT("dve", zb[:, tc, :], ps[:, bank, 0:CB], C1[tc % 3], ALU.add, [PB[bank], b_C1[tc % 3]], [b_zb[tc]])
            fwd_dft(0, cb)
            inv_dft_gate(0, cb, sx1, False)
            sx2 = prep_weights(cb, 2)
            fwd_dft(1, cb)
            inv_dft_gate(1, cb, sx2, True)
            for dc in range(KC):
                for jb in range((n + 511) // 512):
                    t0 = jb * 512
                    nn = min(512, n - t0)
                    x0 = xbase + t0
                    xjb = min(x0 // 512, 4)
                    bank = (dc * 4 + jb) % 2
                    for cc in range(CB // 128):
                        MM(ps[:, bank, 0:nn], wo[:, cc, dc * 128:(dc + 1) * 128], zT[:, cc, t0:t0 + nn], cc == 0, cc == CB // 128 - 1,
                           [b_wo, b_zT[jb]], [PB[bank]])
                    STT(xT[:, dc, x0:x0 + nn], ps[:, bank, 0:nn], modv[:, l, row, 2, dc:dc + 1], xT[:, dc, x0:x0 + nn],
                        ALU.mult, ALU.add, [PB[bank], b_modv, b_x[dc][xjb]], [b_x[dc][xjb]])
        A.release()

    def final_phase(b):
        A.mark()
        sq = [A.alloc([512], BF16) for _ in range(2)]; b_sq = [Buf("fsq0"), Buf("fsq1")]
        lnv = A.alloc([512], F32); b_ln = Buf("flnv")
        rstd = A.alloc([512], F32); b_rs = Buf("frstd")
        ot = [A.alloc([512], F32) for _ in range(4)]; b_ot = [Buf(f"fo{s}") for s in range(4)]
        it = 0
        for j, (x0, n) in enumerate(TBLK[:4]):
            bank = j % 2
            for c in range(KC):
                s = it % 2; it += 1
                if final_norm:
                    ACT(sq[s][:, 0:n], xT[:, c, x0:x0 + n], AF.Square, [b_x[c][j]], [b_sq[s]])
                    MM(ps[:, bank, 0:n], ones_bf, sq[s][:, 0:n], c == 0, c == KC - 1, [b_ones, b_sq[s]], [PB[bank]])
            if final_norm:
                ACT(lnv[:, 0:n], ps[:, bank, 0:n], AF.Ln, [PB[bank], b_cbias], [b_ln], bias=cbias[:, 0:1])
                ACT(rstd[:, 0:n], lnv[:, 0:n], AF.Exp, [b_ln], [b_rs], scale=-0.5)
            for c in range(KC):
                s = it % 4; it += 1
                if final_norm:
                    STT(ot[s][:, 0:n], xT[:, c, x0:x0 + n], fing[:, c:c + 1], rstd[:, 0:n], ALU.mult, ALU.mult,
                        [b_x[c][j], b_fing, b_rs], [b_ot[s]])
                    P.dma(outT[b, c * 128:(c + 1) * 128, x0:x0 + n], ot[s][:, 0:n], reads=[b_ot[s]], writes=[P.dbuf(f"out{s}")])
                else:
                    P.dma(outT[b, c * 128:(c + 1) * 128, x0:x0 + n], xT[:, c, x0:x0 + n], reads=[b_x[c][j]], writes=[P.dbuf(f"out{s}")])
        A.release()

    for b in range(nbatch):
        for c in range(KC):
            P.dma(xT[:, c, 0:NL], dr["xT"][b, c * 128:(c + 1) * 128, :], writes=[P.dbuf("xld")] + b_x[c][0:4])
            P.dma(xT[:, c, NL:T], dr["ctxT"][b, c * 128:(c + 1) * 128, :], writes=[P.dbuf("xld2"), b_x[c][4]])
        for l in layers:
            last = l == DEPTH - 1
            even = l % 2 == 0
            with_ctx = not last
            norm_phase(l, 1, b, True if not even else with_ctx)
            P.barrier()
            if even:
                gmlp_phase(l, b, with_ctx)
                P.barrier()
                hyena_phase(l, b, "L")
                P.barrier()
                if with_ctx:
                    hyena_phase(l, b, "C")
                    P.barrier()
            else:
                attn_phase(l, b, last)
                P.barrier()
            ffn_phase(l, b, with_ctx)
            P.barrier()
        final_phase(b)
        P.barrier()
    P.emit()
    return nc, A.peak


_PROG_CACHE = {}


def _pp(v):
    v = np.asarray(v, dtype=np.float32)
    lead = v.shape[:-1]
    r = v.reshape(lead + (KC, 128))
    r = np.moveaxis(r, -1, 0)
    return np.ascontiguousarray(r)


def _shared_inputs(inp):
    C = _consts()
    m = dict(C)
    f32 = lambda a: np.ascontiguousarray(np.asarray(a, dtype=np.float32))
    m["mod_w"] = f32(inp["mod_w"])
    mb = np.asarray(inp["mod_b"], dtype=np.float32).reshape(DEPTH, 6, KC, 128)
    m["modb"] = np.ascontiguousarray(mb.transpose(3, 0, 1, 2))
    m["n1g"] = _pp(inp["norm1_g"])
    m["n2g"] = _pp(inp["norm2_g"])
    m["fing"] = _pp(inp["final_g"])
    wgu = np.asarray(inp["ffn_w_gu"], dtype=np.float32)
    g = wgu[:, :, :DFF].reshape(DEPTH, D, FCH, 1, 128)
    u = wgu[:, :, DFF:].reshape(DEPTH, D, FCH, 1, 128)
    m["w_gu"] = np.ascontiguousarray(np.concatenate([g, u], axis=3).reshape(DEPTH, D, 2 * DFF))
    m["w_down"] = f32(inp["ffn_w_down"])
    m["w_in"] = f32(inp["even_w_in"])
    m["ln_g"] = f32(inp["gmlp_ln_g"])
    m["ln_b"] = f32(inp["gmlp_ln_b"])
    ws = np.asarray(inp["gmlp_w_s"], dtype=np.float32)
    m["w_sT"] = np.ascontiguousarray(ws.transpose(0, 3, 1, 2))
    m["b_s"] = f32(np.asarray(inp["gmlp_b_s"], dtype=np.float32).reshape(2, 8 * 128))
    m["conv_w"] = f32(inp["hyena_conv_w"])
    m["conv_b"] = f32(inp["hyena_conv_b"])
    m["f_w1"] = f32(inp["hyena_f_w1"])
    fr = np.asarray(inp["hyena_freq"], dtype=np.float32)
    m["f_pp"] = np.ascontiguousarray(np.stack([fr[:, 0], np.asarray(inp["hyena_f_b1"], np.float32),
                                               fr[:, 1], np.asarray(inp["hyena_f_b2"], np.float32)], axis=-1))
    m["f_w2"] = f32(inp["hyena_f_w2"])
    m["f_w3"] = f32(inp["hyena_f_w3"])
    m["skip"] = f32(inp["hyena_skip"])
    m["w_out"] = f32(inp["even_w_out"])
    m["w_qkv"] = f32(inp["attn_w_qkv"])
    m["qkg"] = np.ascontiguousarray(np.stack([np.asarray(inp["attn_q_g"], np.float32),
                                              np.asarray(inp["attn_k_g"], np.float32)], axis=-1))
    m["w_o"] = f32(inp["attn_w_o"])
    return m


def _core_inputs(inp, shared, core):
    b0 = 2 * core
    m = dict(shared)
    x = np.asarray(inp["x"], dtype=np.float32)[b0:b0 + 2]
    ctx = np.asarray(inp["ctx"], dtype=np.float32)[b0:b0 + 2]
    m["xT"] = np.ascontiguousarray(x.transpose(0, 2, 1))
    m["ctxT"] = np.ascontiguousarray(ctx.transpose(0, 2, 1))
    cv = np.concatenate([np.asarray(inp["c"], np.float32)[b0:b0 + 2], np.asarray(inp["c_ctx"], np.float32)[None, :]], axis=0)
    m["cvec"] = np.ascontiguousarray(cv.T.reshape(KC, 128, 3).transpose(1, 0, 2))
    return m


def kernel(**inputs):
    key = "full"
    if key not in _PROG_CACHE:
        _PROG_CACHE[key] = build()[0]
    nc = _PROG_CACHE[key]
    shared = _shared_inputs(inputs)
    in_maps = [_core_inputs(inputs, shared, c) for c in range(8)]
    res = run_bass_kernel_spmd(nc, in_maps, core_ids=list(range(8)))
    outs = [np.asarray(r["outT"]).transpose(0, 2, 1) for r in res.results]
    return np.ascontiguousarray(np.concatenate(outs, axis=0).astype(np.float32))
```

```python
import math
import os
import numpy as np
import ml_dtypes
import concourse.bass as bass
import concourse.mybir as mybir
from concourse.bass_utils import run_bass_kernel_spmd

F32 = mybir.dt.float32
BF16 = mybir.dt.bfloat16
AF = mybir.ActivationFunctionType
ALU = mybir.AluOpType

EPOCH = 30000


class Buf:
    __slots__ = ("name", "w", "rs", "dsem", "dcnt")

    def __init__(self, name):
        self.name = name
        self.w = {}
        self.rs = {}
        self.dsem = None
        self.dcnt = 0


class Prog:
    ENGS = ("pe", "act", "dve", "pool", "sp")

    def __init__(self, nc):
        self.nc = nc
        self.ops = {e: [] for e in self.ENGS}
        self.cnt = {e: 0 for e in self.ENGS}
        self.esems = {e: [] for e in self.ENGS}
        self.seen = {e: {} for e in self.ENGS}
        self.semobjs = {}
        self.dma_bufs = []
        self.named = {}

    def dbuf(self, name):
        b = self.named.get(name)
        if b is None:
            b = Buf(name)
            self.named[name] = b
        return b

    def _newsem(self, name):
        cm = self.nc.semaphore(name)
        s = cm.__enter__()
        self.semobjs[name] = (s, cm)
        return name

    def _esem(self, eng, epoch):
        while len(self.esems[eng]) <= epoch:
            self.esems[eng].append(self._newsem(f"e_{eng}_{len(self.esems[eng])}"))
        return self.esems[eng][epoch]

    def _deps(self, eng, reads, writes):
        out = {}

        def add(key, val):
            if out.get(key, 0) < val:
                out[key] = val

        for b in reads:
            for key, (val, src) in b.w.items():
                if src != eng or eng in ("act", "dve", "pool", "__dma__"):
                    add(key, val)
        strict = eng in ("act", "dve", "pool", "__dma__")
        for b in writes:
            for key, (val, src) in b.w.items():
                if src != eng or strict:
                    add(key, val)
            for key, (val, src) in b.rs.items():
                if src != eng or strict:
                    add(key, val)
        return out

    def _filter_waits(self, eng, deps):
        seen = self.seen[eng]
        waits = []
        for key, val in deps.items():
            if seen.get(key, 0) >= val:
                continue
            seen[key] = val
            waits.append((key, val))
        return waits

    def _record(self, event, reads, writes):
        key, val, src = event
        for b in reads:
            b.rs[key] = (val, src)
        for b in writes:
            b.w[key] = (val, src)
            b.rs = {}

    def op(self, eng, fn, reads=(), writes=()):
        deps = self._deps(eng, reads, writes)
        waits = self._filter_waits(eng, deps)
        n = self.cnt[eng]
        epoch, idx = divmod(n, EPOCH)
        key = self._esem(eng, epoch)
        self.cnt[eng] = n + 1
        self.ops[eng].append((waits, fn, key, 1))
        self._record((key, idx + 1, eng), reads, writes)

    def dma(self, out_ap, in_ap, reads=(), writes=(), q="sp", **kw):
        deps = self._deps("__dma__", reads, writes)
        waits = self._filter_waits(q, deps)
        dst = writes[0]
        if dst.dsem is None:
            dst.dsem = self._newsem(f"d{len(self.dma_bufs)}_{dst.name}")
            self.dma_bufs.append(dst)
        dst.dcnt += 16

        def fn(e, out_ap=out_ap, in_ap=in_ap, kw=kw):
            return e.dma_start(out=out_ap, in_=in_ap, **kw)

        self.ops[q].append((waits, fn, dst.dsem, 16))
        self._record((dst.dsem, dst.dcnt, "__dma__"), reads, writes)

    def barrier(self):
        for eng in self.ENGS:
            deps = {}
            for e2 in self.ENGS:
                if e2 == eng or self.cnt[e2] == 0:
                    continue
                epoch, idx = divmod(self.cnt[e2] - 1, EPOCH)
                deps[self.esems[e2][epoch]] = idx + 1
            for b in self.dma_bufs:
                deps[b.dsem] = b.dcnt
            waits = self._filter_waits(eng, deps)
            if waits:
                self.ops[eng].append((waits, None, None, 0))

    def emit(self):
        nc = self.nc
        sem = {k: v[0] for k, v in self.semobjs.items()}

        def run(e, lst):
            for waits, fn, key, inc in lst:
                for (k, v) in waits:
                    e.wait_ge(sem[k], v)
                if fn is not None:
                    fn(e).then_inc(sem[key], inc)

        with nc.Block() as block:
            @block.sync
            def _(e):
                run(e, self.ops["sp"])

            @block.tensor
            def _(e):
                run(e, self.ops["pe"])

            @block.scalar
            def _(e):
                run(e, self.ops["act"])

            @block.vector
            def _(e):
                run(e, self.ops["dve"])

            @block.gpsimd
            def _(e):
                run(e, self.ops["pool"])


class Arena:
    def __init__(self, ap_f32, nbytes):
        self.ap = ap_f32
        self.nbytes = nbytes
        self.off = 0
        self.marks = []
        self.peak = 0

    def alloc(self, shape_free, dtype):
        esz = 2 if dtype == BF16 else 4
        n = 1
        for s in shape_free:
            n *= s
        nb = (n * esz + 63) // 64 * 64
        assert self.off + nb <= self.nbytes, f"arena overflow: {self.off}+{nb}>{self.nbytes}"
        a = self.ap[:, self.off // 4:(self.off + nb) // 4]
        self.off += nb
        self.peak = max(self.peak, self.off)
        if dtype == BF16:
            a = a.bitcast(BF16)
        a = a[:, 0:n]
        if len(shape_free) == 2:
            a = a.rearrange("p (a b) -> p a b", a=shape_free[0])
        elif len(shape_free) == 3:
            a = a.rearrange("p (a b c) -> p a b c", a=shape_free[0], b=shape_free[1])
        elif len(shape_free) == 4:
            a = a.rearrange("p (a b c d) -> p a b c d", a=shape_free[0], b=shape_free[1], c=shape_free[2])
        return a

    def mark(self):
        self.marks.append(self.off)

    def release(self):
        self.off = self.marks.pop()


D = 1024
KC = 8
NL = 2048
NCX = 256
T = NL + NCX
DFF = 2816
FCH = 22
DEPTH = 4
EPS = 1e-6
HL = 1
HC = NL + 3
HW = NL + NCX + 4
GRID_W = 64
HEAD_DIM = 128
MAGIC = 12582912.0
TWO_PI = 2.0 * math.pi
CB = 256
NCB = D // CB

_CONST_CACHE = {}


def _bf16(a):
    return np.ascontiguousarray(a.astype(ml_dtypes.bfloat16))


def _dft_tables(n):
    N = 2 * n
    nf = n // 128 + 1
    idx = np.arange(nf * 128, dtype=np.int64)
    prod = (idx[:, None] * idx[None, :]) % N
    ang = prod.astype(np.float64) * (2.0 * np.pi / N)
    valid = (idx[:, None] <= n) & (idx[None, :] <= n)
    c = np.where(valid, np.cos(ang), 0.0)
    s = np.where(valid, np.sin(ang), 0.0)
    c4 = c.reshape(nf, 128, nf, 128).transpose(2, 1, 0, 3)
    s4 = s.reshape(nf, 128, nf, 128).transpose(2, 1, 0, 3)
    return _bf16(c4), _bf16(s4)


def _consts():
    if _CONST_CACHE:
        return _CONST_CACHE
    C = {}
    for n, tag in ((NL, "L"), (NCX, "C")):
        tc, ts = _dft_tables(n)
        C["tcos" + tag] = tc
        C["tsin" + tag] = ts
        nt = n // 128
        nf = nt + 1
        t = np.linspace(0.0, 1.0, n, dtype=np.float32)
        w = (np.float32(2.0 * math.pi / n) * np.arange(n, dtype=np.float32))
        bands = np.linspace(1e-4, 15.0, 16, dtype=np.float32)
        ang = bands[None, :] * w[:, None]
        z = np.concatenate([t[:, None], np.cos(ang), -np.sin(ang)], axis=-1).astype(np.float32)
        C["zT" + tag] = np.ascontiguousarray(z.T)
        C["negt" + tag] = np.ascontiguousarray((-t).reshape(nt, 128).T)
        wf = np.full(nf * 128, 2.0, dtype=np.float64)
        wf[0] = 1.0
        wf[n] = 1.0
        wf[n + 1:] = 0.0
        C["wfn" + tag] = np.ascontiguousarray((wf / (2 * n)).astype(np.float32).reshape(nf, 128).T)
    min_decay = math.log(1e-2) / 1.5
    max_decay = math.log(1e-2) / 0.3
    C["delta"] = np.abs(np.linspace(min_decay, max_decay, D, dtype=np.float32)).astype(np.float32)
    rows = NL // GRID_W
    row = np.repeat(np.arange(rows), GRID_W).astype(np.float32)
    col = np.tile(np.arange(GRID_W), rows).astype(np.float32)
    half = HEAD_DIM // 2
    inv = (10000.0 ** (-np.arange(0, half, 2, dtype=np.float32) / half)).astype(np.float32)
    ang = np.concatenate([row[:, None] * inv, col[:, None] * inv], axis=-1)
    cos = np.cos(ang).astype(np.float32)
    sin = np.sin(ang).astype(np.float32)
    C["ropec"] = np.ascontiguousarray(np.concatenate([cos, cos], axis=1).T)
    C["ropes"] = np.ascontiguousarray(np.concatenate([sin, sin], axis=1).T)
    rp = np.zeros((128, 128), dtype=np.float32)
    for m in range(64):
        rp[m + 64, m] = -1.0
        rp[m, m + 64] = 1.0
    C["rperm"] = rp
    C["ident"] = np.eye(128, dtype=np.float32)
    sh = np.zeros((4, 128, 128), dtype=np.float32)
    for t in range(1, 128):
        sh[0, t - 1, t] = 1.0
    sh[1, 127, 0] = 1.0
    for t in range(0, 127):
        sh[2, t + 1, t] = 1.0
    sh[3, 0, 127] = 1.0
    C["shiftm"] = np.ascontiguousarray(sh.transpose(1, 0, 2))
    _CONST_CACHE.update(C)
    return C


def build(nlayers=DEPTH, nbatch=2, final_norm=True, layers=None):
    layers = list(range(nlayers)) if layers is None else list(layers)
    nc = bass.Bass("TRN2", target_bir_lowering=False)
    dr = {}

    def din(name, shape, dt=F32):
        dr[name] = nc.dram_tensor(name, list(shape), dt, kind="ExternalInput").ap()
        return dr[name]

    din("xT", [2, D, NL])
    din("ctxT", [2, D, NCX])
    din("cvec", [128, KC, 3])
    din("mod_w", [DEPTH, D, 6 * D])
    din("modb", [128, DEPTH, 6, KC])
    din("n1g", [128, DEPTH, KC])
    din("n2g", [128, DEPTH, KC])
    din("fing", [128, KC])
    din("w_gu", [DEPTH, D, 2 * DFF])
    din("w_down", [DEPTH, DFF, D])
    din("w_in", [2, D, 5 * D])
    din("ln_g", [2, D])
    din("ln_b", [2, D])
    din("w_sT", [2, 128, 8, 128])
    din("b_s", [2, 8 * 128])
    din("conv_w", [2, 3, 3 * D])
    din("conv_b", [2, 3 * D])
    din("f_w1", [2, 33, 64])
    din("f_pp", [2, 64, 4])
    din("f_w2", [2, 64, 64])
    din("f_w3", [2, 64, 4 * D])
    din("skip", [2, 2, D])
    din("w_out", [2, 2 * D, D])
    din("w_qkv", [2, D, 1536])
    din("qkg", [2, 128, 2])
    din("w_o", [2, D, D])
    for tag, n in (("L", NL), ("C", NCX)):
        nf = n // 128 + 1
        din("tcos" + tag, [nf, 128, nf, 128], BF16)
        din("tsin" + tag, [nf, 128, nf, 128], BF16)
        din("zT" + tag, [33, n])
        din("negt" + tag, [128, n // 128])
        din("wfn" + tag, [128, nf])
    din("delta", [D])
    din("ropec", [128, NL])
    din("ropes", [128, NL])
    din("rperm", [128, 128])
    din("ident", [128, 128])
    din("shiftm", [128, 4, 128])
    outT = nc.dram_tensor("outT", [2, D, NL], F32, kind="ExternalOutput").ap()
    Hs = {}
    for i in range(2):
        for tag, n in (("L", NL), ("C", NCX)):
            nf = n // 128 + 1
            Hs[(i, tag)] = nc.dram_tensor(f"Hs{i}{tag}", [2, nf, NCB, 128, 2, CB], F32, kind="Internal").ap()

    ARENA_BYTES = 211968
    cm_a = nc.sbuf_tensor("arena", [128, ARENA_BYTES // 4], F32)
    arena_t = cm_a.__enter__()
    cm_p = nc.psum_tensor("psum", [128, 8, 512], F32)
    ps = cm_p.__enter__()
    A = Arena(arena_t[:, :], ARENA_BYTES)
    P = Prog(nc)
    PB = [Buf(f"psb{i}") for i in range(8)]

    def MM(out, lhsT, rhs, start, stop, R, W):
        P.op("pe", lambda e: e.matmul(out, lhsT, rhs, start=start, stop=stop), R, W)

    def TR(out, in_, ident_ap, R, W):
        P.op("pe", lambda e: e.transpose(out, in_, ident_ap), R, W)

    def ACT(out, in_, func, R, W, bias=None, scale=None, accum=None):
        kw = {}
        if bias is not None:
            kw["bias"] = bias
        if scale is not None:
            kw["scale"] = scale
        if accum is not None:
            kw["accum_out"] = accum
        P.op("act", lambda e: e.activation(out=out, in_=in_, func=func, **kw), R, W)

    def TT(eng, out, a, b, op, R, W):
        P.op(eng, lambda e: e.tensor_tensor(out=out, in0=a, in1=b, op=op), R, W)

    def TS(out, a, s1, s2, op0, op1, R, W, eng="dve"):
        P.op(eng, lambda e: e.tensor_scalar(out, a, s1, s2, op0, op1), R, W)

    def STT(out, in0, scalar, in1, op0, op1, R, W):
        P.op("dve", lambda e: e.scalar_tensor_tensor(out=out, in0=in0, scalar=scalar, in1=in1, op0=op0, op1=op1), R, W)

    def CP(eng, out, in_, R, W):
        P.op(eng, lambda e: e.tensor_copy(out=out, in_=in_), R, W)

    def MSET(eng, out, val, W):
        P.op(eng, lambda e: e.memset(out, val), (), W)

    def RECIP(out, in_, R, W):
        P.op("dve", lambda e: e.reciprocal(out=out, in_=in_), R, W)

    def wview(w2d, k0, kc, c0, ncol):
        return w2d[k0:k0 + kc * 128, c0:c0 + ncol].rearrange("(kc p) c -> p kc c", p=128)

    ident = A.alloc([128], BF16); b_ident = Buf("ident")
    ones_bf = A.alloc([128], BF16); b_ones = Buf("ones")
    ones_f = A.alloc([128], F32)
    rperm = A.alloc([128], BF16); b_rperm = Buf("rperm")
    cbias = A.alloc([4], F32); b_cbias = Buf("cbias")
    modv = A.alloc([DEPTH, 3, 6, KC], F32); b_modv = Buf("modv")
    fing = A.alloc([KC], F32); b_fing = Buf("fing")
    n1g = A.alloc([DEPTH, KC], F32); n2g = A.alloc([DEPTH, KC], F32); b_n1g = Buf("n1g"); b_n2g = Buf("n2g")
    shiftm = A.alloc([4, 128], BF16); b_shiftm = Buf("shiftm")
    P.dma(shiftm, dr["shiftm"], writes=[b_shiftm], q="pool")
    P.dma(ident, dr["ident"], writes=[b_ident], q="pool")
    P.dma(rperm, dr["rperm"], writes=[b_rperm], q="pool")
    P.dma(n1g, dr["n1g"], writes=[b_n1g])
    P.dma(n2g, dr["n2g"], writes=[b_n2g])
    P.dma(fing, dr["fing"], writes=[b_fing])
    MSET("pool", ones_f, 1.0, [b_ones])
    MSET("pool", ones_bf, 1.0, [b_ones])
    MSET("pool", cbias[:, 0:1], D * EPS, [b_cbias])
    MSET("pool", cbias[:, 1:2], 128 * EPS, [b_cbias])
    MSET("pool", cbias[:, 2:3], 1e-5, [b_cbias])
    MSET("pool", cbias[:, 3:4], 0.0, [b_cbias])

    A.mark()
    scT = A.alloc([KC, 3], F32); b_sc = Buf("scT")
    modb = A.alloc([DEPTH, 6, KC], F32); b_modb = Buf("modb")
    P.dma(scT, dr["cvec"], writes=[b_sc])
    P.dma(modb, dr["modb"], writes=[b_modb])
    ACT(scT, scT, AF.Silu, [b_sc], [b_sc])
    scTb = A.alloc([KC, 3], BF16); b_scb = Buf("scTb")
    CP("dve", scTb, scT, [b_sc], [b_scb])
    mwt = [A.alloc([KC, 512], BF16) for _ in range(3)]
    b_mw = [P.dbuf(f"mw{i}") for i in range(3)]
    it = 0
    for l in layers:
        for j in range(12):
            s = it % 3
            P.dma(mwt[s], wview(dr["mod_w"][l], 0, KC, j * 512, 512), writes=[b_mw[s]], q="pool")
            six = j // 2
            for cc in range(4):
                chunk = (j % 2) * 4 + cc
                pb = PB[(it * 4 + cc) % 8]
                for k in range(KC):
                    MM(ps[:, (it * 4 + cc) % 8, 0:3], mwt[s][:, k, cc * 128:(cc + 1) * 128], scTb[:, k, :],
                       k == 0, k == KC - 1, [b_mw[s], b_scb], [pb])
                TS(modv[:, l, :, six, chunk], ps[:, (it * 4 + cc) % 8, 0:3], modb[:, l, six, chunk:chunk + 1], 0.0,
                   ALU.add, ALU.add, [pb, b_modb], [b_modv])
            it += 1
        for r in range(3):
            TS(modv[:, l, r, 1, :], modv[:, l, r, 1, :], 1.0, 32.0, ALU.add, ALU.mult, [b_modv], [b_modv])
            TT("dve", modv[:, l, r, 1, :], modv[:, l, r, 1, :], n1g[:, l, :], ALU.mult, [b_modv, b_n1g], [b_modv])
            TS(modv[:, l, r, 4, :], modv[:, l, r, 4, :], 1.0, 32.0, ALU.add, ALU.mult, [b_modv], [b_modv])
            TT("dve", modv[:, l, r, 4, :], modv[:, l, r, 4, :], n2g[:, l, :], ALU.mult, [b_modv, b_n2g], [b_modv])
    TS(fing, fing, 32.0, 0.0, ALU.mult, ALU.add, [b_fing], [b_fing])
    P.barrier()
    A.release()

    b_Hs = {k: Buf(f"Hs{k}") for k in Hs}
    even_idx = sorted({l // 2 for l in layers if l % 2 == 0})

    def filter_precompute(i, tag, n):
        nt = n // 128
        nf = nt + 1
        A.mark()
        zT = A.alloc([n], F32); b_z = P.dbuf("f_z")
        w1 = A.alloc([64], F32); b_w1 = P.dbuf("f_w1")
        w2 = A.alloc([64], F32); b_w2 = P.dbuf("f_w2")
        w3 = A.alloc([4 * D], BF16); b_w3 = P.dbuf("f_w3")
        fpp = A.alloc([6], F32); b_fpp = P.dbuf("f_pp")
        negt = A.alloc([nt], F32); b_negt = P.dbuf("f_negt")
        wfn = A.alloc([nf], F32); b_wfn = P.dbuf("f_wfn")
        dl = A.alloc([D], F32); b_dl = P.dbuf("f_dl")
        hid1 = A.alloc([n], F32); b_h1 = Buf("hid1")
        hid2 = A.alloc([n], BF16); b_h2 = Buf("hid2")
        P.dma(zT[0:33, :], dr["zT" + tag], writes=[b_z])
        P.dma(w1[0:33, :], dr["f_w1"][i], writes=[b_w1])
        P.dma(w2[0:64, :], dr["f_w2"][i], writes=[b_w2])
        P.dma(w3[0:64, :], dr["f_w3"][i], writes=[b_w3], q="pool")
        P.dma(fpp[0:64, 0:4], dr["f_pp"][i], writes=[b_fpp])
        P.dma(negt, dr["negt" + tag], writes=[b_negt])
        P.dma(wfn, dr["wfn" + tag], writes=[b_wfn])
        P.dma(dl, dr["delta"].partition_broadcast(128), writes=[b_dl])
        TT("dve", fpp[0:64, 4:5], fpp[0:64, 0:1], fpp[0:64, 1:2], ALU.mult, [b_fpp], [b_fpp])
        TT("dve", fpp[0:64, 5:6], fpp[0:64, 2:3], fpp[0:64, 3:4], ALU.mult, [b_fpp], [b_fpp])
        tmpa = A.alloc([512], F32); b_ta = Buf("tmpa")
        tmpb = A.alloc([512], F32); b_tb = Buf("tmpb")
        bw = min(512, n)
        for stage in range(2):
            src, b_src, kk = (zT, b_z, 33) if stage == 0 else (hid1, b_h1, 64)
            wt, b_wt = (w1, b_w1) if stage == 0 else (w2, b_w2)
            dst, b_dst = (hid1, b_h1) if stage == 0 else (hid2, b_h2)
            fcol = 0 if stage == 0 else 2
            for blk in range(n // bw):
                cs = slice(blk * bw, (blk + 1) * bw)
                pb = PB[blk % 2]
                MM(ps[0:64, blk % 2, 0:bw], wt[0:kk, 0:64], src[0:kk, cs], True, True, [b_wt, b_src], [pb])
                TS(tmpa[0:64, 0:bw], ps[0:64, blk % 2, 0:bw], fpp[0:64, fcol:fcol + 1], fpp[0:64, 4 + stage:5 + stage],
                   ALU.mult, ALU.add, [pb, b_fpp], [b_ta])
                TS(tmpb[0:64, 0:bw], tmpa[0:64, 0:bw], 1.0 / TWO_PI, MAGIC, ALU.mult, ALU.add, [b_ta], [b_tb])
                TS(tmpb[0:64, 0:bw], tmpb[0:64, 0:bw], MAGIC, -TWO_PI, ALU.subtract, ALU.mult, [b_tb], [b_tb])
                TT("dve", tmpa[0:64, 0:bw], tmpa[0:64, 0:bw], tmpb[0:64, 0:bw], ALU.add, [b_ta, b_tb], [b_ta])
                ACT(dst[0:64, cs], tmpa[0:64, 0:bw], AF.Sin, [b_ta], [b_dst])
        hsum = A.alloc([nt, D], BF16); b_hs = Buf("hsum")
        hdif = A.alloc([nt, D], BF16); b_hd = Buf("hdif")
        hw_ = [A.alloc([2, D], F32) for _ in range(2)]; b_hw_ = [Buf("hw0"), Buf("hw1")]
        habs_ = [A.alloc([2, D], BF16) for _ in range(2)]; b_ha_ = [Buf("habs0"), Buf("habs1")]
        win_ = [A.alloc([D], F32) for _ in range(2)]; b_win_ = [Buf("win0"), Buf("win1")]
        rl1 = A.alloc([D], F32); b_rl1 = Buf("rl1")
        tct = [A.alloc([nf, 128], BF16) for _ in range(2)]
        tst = [A.alloc([nf, 128], BF16) for _ in range(2)]
        b_tc = [P.dbuf(f"tc{s}") for s in range(2)]
        b_ts = [P.dbuf(f"ts{s}") for s in range(2)]
        hst = [A.alloc([2, D], F32) for _ in range(2)]
        b_hst = [Buf(f"hst{s}") for s in range(2)]
        hsc = Hs[(i, tag)]
        git = 0
        for o in range(2):
            for tc in range(nt):
                hw, b_hw = hw_[tc % 2], b_hw_[tc % 2]
                habs, b_ha = habs_[tc % 2], b_ha_[tc % 2]
                win, b_win = win_[tc % 2], b_win_[tc % 2]
                ACT(win, dl, AF.Exp, [b_dl, b_negt], [b_win], scale=negt[:, tc:tc + 1])
                for dirn in range(2):
                    for ch in range(2):
                        col0 = dirn * 2 * D + o * D + ch * 512
                        bank = (2 if tc % 2 == 0 else 6) + (dirn * 2 + ch) % 2
                        MM(ps[:, bank, :], hid2[0:64, tc * 128:(tc + 1) * 128], w3[0:64, col0:col0 + 512], True, True,
                           [b_h2, b_w3], [PB[bank]])
                        TT("dve", hw[:, dirn, ch * 512:(ch + 1) * 512], ps[:, bank, :], win[:, ch * 512:(ch + 1) * 512],
                           ALU.mult, [PB[bank], b_win], [b_hw])
                if tc == 0:
                    MSET("dve", hw[0:1, 1, :], 0.0, [b_hw])
                TT("dve", hsum[:, tc, :], hw[:, 0, :], hw[:, 1, :], ALU.add, [b_hw], [b_hs])
                TT("pool", hdif[:, tc, :], hw[:, 1, :], hw[:, 0, :], ALU.subtract, [b_hw], [b_hd])
                ACT(habs, hw, AF.Abs, [b_hw], [b_ha])
                for ch in range(2):
                    for dirn in range(2):
                        MM(ps[:, ch, :], ones_bf, habs[:, dirn, ch * 512:(ch + 1) * 512],
                           tc == 0 and dirn == 0, tc == nt - 1 and dirn == 1, [b_ones, b_ha], [PB[ch]])
            for ch in range(2):
                RECIP(rl1[:, ch * 512:(ch + 1) * 512], ps[:, ch, :], [PB[ch]], [b_rl1])
            def load_tables(fc_, s_):
                P.dma(tct[s_], dr["tcos" + tag][fc_], writes=[b_tc[s_]])
                if fc_ != nf - 1:
                    P.dma(tst[s_], dr["tsin" + tag][fc_], writes=[b_ts[s_]])

            load_tables(0, git % 2)
            for fc in range(nf):
                s = git % 2
                git += 1
                nyq = fc == nf - 1
                if fc + 1 < nf:
                    load_tables(fc + 1, git % 2)
                mrow = 1 if nyq else 128
                mcol = 1 if nyq else 128
                if nyq:
                    MSET("dve", hst[s], 0.0, [b_hst[s]])
                for ch in range(2):
                    cs = slice(ch * 512, (ch + 1) * 512)
                    bank = 4 + ch * 2
                    for tc in range(nt):
                        MM(ps[0:mrow, bank, :], tct[s][:, tc, 0:mcol], hsum[:, tc, cs], tc == 0, tc == nt - 1,
                           [b_tc[s], b_hs], [PB[bank]])
                    STT(hst[s][0:mrow, 0, cs], ps[0:mrow, bank, :], wfn[0:mrow, fc:fc + 1], rl1[0:mrow, cs], ALU.mult, ALU.mult,
                        [PB[bank], b_wfn, b_rl1], [b_hst[s]])
                    if not nyq:
                        for tc in range(nt):
                            MM(ps[:, bank + 1, :], tst[s][:, tc, :], hdif[:, tc, cs], tc == 0, tc == nt - 1,
                               [b_ts[s], b_hd], [PB[bank + 1]])
                        STT(hst[s][:, 1, cs], ps[:, bank + 1, :], wfn[:, fc:fc + 1], rl1[:, cs], ALU.mult, ALU.mult,
                            [PB[bank + 1], b_wfn, b_rl1], [b_hst[s]])
                P.dma(hsc[o, fc].rearrange("cb p ri c -> p ri cb c"),
                      hst[s].rearrange("p ri (cb c) -> p ri cb c", cb=NCB),
                      reads=[b_hst[s]], writes=[P.dbuf(f"hsw{s}"), b_Hs[(i, tag)]])
        P.barrier()
        A.release()

    for i in even_idx:
        filter_precompute(i, "L", NL)
        if 2 * i < DEPTH - 1:
            filter_precompute(i, "C", NCX)

    xT = A.alloc([KC, T], F32)
    hT = A.alloc([KC, HW], BF16)
    TBLK = [(0, 512), (512, 512), (1024, 512), (1536, 512), (2048, 256)]
    b_x = [[Buf(f"x{c}_{j}") for j in range(5)] for c in range(KC)]
    b_h = [Buf(f"h{j}") for j in range(5)]

    def xbufs(c, x0, n):
        return [b_x[c][j] for j, (s, ln) in enumerate(TBLK) if s < x0 + n and x0 < s + ln]

    def hbufs(x0, n):
        return [b_h[j] for j, (s, ln) in enumerate(TBLK) if s < x0 + n and x0 < s + ln]

    def hcol(x0):
        return HL + x0 if x0 < NL else HC + (x0 - NL)

    MSET("pool", hT, 0.0, b_h)

    def norm_alloc():
        nb = {}
        nb["sq"] = [A.alloc([512], BF16) for _ in range(2)]; nb["b_sq"] = [Buf("sq0"), Buf("sq1")]
        nb["lnv"] = [A.alloc([512], F32) for _ in range(2)]; nb["b_ln"] = [Buf("lnv0"), Buf("lnv1")]
        nb["rstd"] = [A.alloc([512], F32) for _ in range(2)]; nb["b_rs"] = [Buf("rstd0"), Buf("rstd1")]
        nb["tmp"] = [A.alloc([512], F32) for _ in range(2)]; nb["b_tmp"] = [Buf("nt0"), Buf("nt1")]
        return nb

    def norm_stats(nb, l, which, b, j, bank):
        x0, n = TBLK[j]
        sq, b_sq = nb["sq"], nb["b_sq"]
        p_ = j % 2
        for c in range(KC):
            s = c % 2
            if c % 2 == 0:
                ACT(sq[s][:, 0:n], xT[:, c, x0:x0 + n], AF.Square, [b_x[c][j]], [b_sq[s]])
            else:
                TT("dve", sq[s][:, 0:n], xT[:, c, x0:x0 + n], xT[:, c, x0:x0 + n], ALU.mult, [b_x[c][j]], [b_sq[s]])
            MM(ps[:, bank, 0:n], ones_bf, sq[s][:, 0:n], c == 0, c == KC - 1, [b_ones, b_sq[s]], [PB[bank]])
        ACT(nb["lnv"][p_][:, 0:n], ps[:, bank, 0:n], AF.Ln, [PB[bank], b_cbias], [nb["b_ln"][p_]], bias=cbias[:, 0:1])
        ACT(nb["rstd"][p_][:, 0:n], nb["lnv"][p_][:, 0:n], AF.Exp, [nb["b_ln"][p_]], [nb["b_rs"][p_]], scale=-0.5)

    def norm_apply(nb, l, which, b, j):
        so = 0 if which == 1 else 3
        x0, n = TBLK[j]
        row = b if x0 < NL else 2
        p_ = j % 2
        tmp, b_tmp = nb["tmp"], nb["b_tmp"]
        h0 = hcol(x0)
        for c in range(KC):
            s = c % 2
            STT(tmp[s][:, 0:n], xT[:, c, x0:x0 + n], modv[:, l, row, so + 1, c:c + 1], nb["rstd"][p_][:, 0:n], ALU.mult, ALU.mult,
                [b_x[c][j], b_modv, nb["b_rs"][p_]], [b_tmp[s]])
            ACT(hT[:, c, h0:h0 + n], tmp[s][:, 0:n], AF.Identity, [b_tmp[s], b_modv], [b_h[j]],
                bias=modv[:, l, row, so, c:c + 1])

    def norm_block(nb, l, which, b, j, bank):
        norm_stats(nb, l, which, b, j, bank)
        norm_apply(nb, l, which, b, j)

    def norm_blocks(nb, l, which, b, js, banks=(0, 1)):
        js = list(js)
        pipeline(js, [lambda i_, j: norm_stats(nb, l, which, b, j, banks[j % 2]),
                      lambda i_, j: norm_apply(nb, l, which, b, j)])

    def norm_phase(l, which, b, with_ctx):
        A.mark()
        nb = norm_alloc()
        norm_blocks(nb, l, which, b, range(5 if with_ctx else 4))
        A.release()

    def ffn_phase(l, b, with_ctx):
        A.mark()
        nb = norm_alloc()
        norm_blocks(nb, l, 2, b, range(3))
        late_norms = [3, 4] if with_ctx else [3]
        subs = [(0, 384), (384, 384), (768, 384), (1152, 384), (1536, 384), (1920, 128)]
        if with_ctx:
            subs.append((2048, 256))
        blocks = [subs[0:3], subs[3:]]
        aT = A.alloc([FCH, 1152], BF16)
        b_a = [Buf(f"a{s}") for s in range(4)]
        wgu = [A.alloc([KC, 256], BF16) for _ in range(3)]
        b_wgu = [P.dbuf(f"wgu{s}") for s in range(3)]
        wdn = [A.alloc([FCH, 128], BF16) for _ in range(2)]
        b_wdn = [P.dbuf(f"wdn{s}") for s in range(2)]
        sg = [A.alloc([384], F32) for _ in range(2)]; b_sg = [Buf("sg0"), Buf("sg1")]
        wi = 0; di = 0; si = 0; pi = 0
        for blk in blocks:
            offs = []
            o = 0
            for (x0, n) in blk:
                offs.append(o); o += n
            for f in range(FCH):
                s = wi % 3; wi += 1
                P.dma(wgu[s], wview(dr["w_gu"][l], 0, KC, f * 256, 256), writes=[b_wgu[s]], q="pool")
                if f in (2, 5) and late_norms:
                    norm_block(nb, l, 2, b, late_norms.pop(0), 3)
                for sb, (x0, n) in enumerate(blk):
                    h0 = hcol(x0)
                    bg = pi % 2; bu = 2 + pi % 2; pi += 1
                    for k in range(KC):
                        MM(ps[:, bg, 0:n], wgu[s][:, k, 0:128], hT[:, k, h0:h0 + n], k == 0, k == KC - 1,
                           [b_wgu[s]] + hbufs(x0, n), [PB[bg]])
                    for k in range(KC):
                        MM(ps[:, bu, 0:n], wgu[s][:, k, 128:256], hT[:, k, h0:h0 + n], k == 0, k == KC - 1,
                           [b_wgu[s]] + hbufs(x0, n), [PB[bu]])
                    ss = si % 2; si += 1
                    ACT(sg[ss][:, 0:n], ps[:, bg, 0:n], AF.Silu, [PB[bg]], [b_sg[ss]])
                    TT("dve", aT[:, f, offs[sb]:offs[sb] + n], sg[ss][:, 0:n], ps[:, bu, 0:n], ALU.mult,
                       [b_sg[ss], PB[bu]], [b_a[sb]])
            for dc in range(KC):
                s = di % 2; di += 1
                P.dma(wdn[s], wview(dr["w_down"][l], 0, FCH, dc * 128, 128), writes=[b_wdn[s]], q="pool")
                for sb, (x0, n) in enumerate(blk):
                    row = b if x0 < NL else 2
                    bank = 4 + pi % 4; pi += 1
                    for f in range(FCH):
                        MM(ps[:, bank, 0:n], wdn[s][:, f, :], aT[:, f, offs[sb]:offs[sb] + n], f == 0, f == FCH - 1,
                           [b_wdn[s], b_a[sb]], [PB[bank]])
                    xb = xbufs(dc, x0, n)
                    STT(xT[:, dc, x0:x0 + n], ps[:, bank, 0:n], modv[:, l, row, 5, dc:dc + 1], xT[:, dc, x0:x0 + n],
                        ALU.mult, ALU.add, [PB[bank], b_modv] + xb, xb)
        A.release()

    def pipeline(items, stages):
        n = len(items)
        S = len(stages)
        for step in range(n + S - 1):
            for s_ in range(S):
                i = step - s_
                if 0 <= i < n and stages[s_] is not None:
                    stages[s_](i, items[i])

    def attn_phase(l, b, last):
        j = l // 2
        A.mark()
        ropec = A.alloc([NL], F32); ropes = A.alloc([NL], F32)
        b_ropec = P.dbuf("ropec"); b_ropes = P.dbuf("ropes")
        P.dma(ropec, dr["ropec"], writes=[b_ropec])
        P.dma(ropes, dr["ropes"], writes=[b_ropes])
        qkg = A.alloc([2], F32); b_qkg = P.dbuf("qkg")
        P.dma(qkg, dr["qkg"][j], writes=[b_qkg])
        TS(qkg, qkg, math.sqrt(128.0), 0.0, ALU.mult, ALU.add, [b_qkg], [b_qkg])
        qT = A.alloc([4, T], BF16); b_q = [[Buf(f"q{h}_{jb}") for jb in range(5)] for h in range(4)]
        kT = A.alloc([T], BF16); b_k = [Buf(f"k{jb}") for jb in range(5)]
        V = A.alloc([18, 128], BF16); b_v = [Buf(f"v{jb}") for jb in range(5)]
        oT = A.alloc([4, 512], BF16); b_o = [Buf(f"o{h}") for h in range(4)]
        wq = [A.alloc([KC, 128], BF16) for _ in range(2)]; b_wq = [P.dbuf(f"wq{s}") for s in range(2)]
        wv = A.alloc([KC, 128], BF16); b_wv = P.dbuf("wv")
        wo = A.alloc([4, D], BF16); b_wo = P.dbuf("wo")
        sqb = [A.alloc([512], BF16) for _ in range(3)]; b_sqb = [Buf(f"asq{s}") for s in range(3)]
        lnv = [A.alloc([512], F32) for _ in range(2)]; b_ln = [Buf(f"alnv{s}") for s in range(2)]
        rstd = [A.alloc([512], F32) for _ in range(2)]; b_rs = [Buf(f"arstd{s}") for s in range(2)]
        qn = [A.alloc([512], BF16) for _ in range(3)]; b_qn = [Buf(f"qn{s}") for s in range(3)]
        t1 = [A.alloc([512], F32) for _ in range(2)]; b_t1 = [Buf(f"t1{s}") for s in range(2)]
        t2 = [A.alloc([512], F32) for _ in range(2)]; b_t2 = [Buf(f"t2{s}") for s in range(2)]
        pT = [A.alloc([512], BF16) for _ in range(3)]; b_pT = [Buf(f"pT{s}") for s in range(3)]
        rden = [A.alloc([512], F32) for _ in range(2)]; b_rd = [Buf(f"rden{s}") for s in range(2)]
        blks = TBLK
        hcount = [0]

        for g in range(2):
            P.dma(wo, wview(dr["w_o"][j], g * 512, 4, 0, D), writes=[b_wo], q="pool")
            P.dma(wv, wview(dr["w_qkv"][j], 0, KC, 1280 + g * 128, 128), writes=[b_wv], q="pool")
            for tt in range(18):
                x0 = tt * 128
                jb = min(x0 // 512, 4)
                h0 = hcol(x0)
                bank = 6 + tt % 2
                for k in range(KC):
                    MM(ps[:, bank, 0:128], hT[:, k, h0:h0 + 128], wv[:, k, :], k == 0, k == KC - 1, [b_wv, b_h[jb]], [PB[bank]])
                ACT(V[:, tt, :], ps[:, bank, 0:128], AF.Copy, [PB[bank]], [b_v[jb]])
            qblist = list(range(4)) if last else list(range(5))
            heads = [(1024 + g * 128, 1, kT, b_k, list(range(5)))]
            for h in range(4):
                heads.append(((g * 4 + h) * 128, 0, qT[:, h, :], b_q[h], qblist))
            items = []
            for hi, hd in enumerate(heads):
                for bi, jb in enumerate(hd[4]):
                    items.append((hi, jb, bi == 0))

            def stA(i, it):
                hi, jb, firstblk = it
                col0 = heads[hi][0]
                s = (hcount[0] + hi) % 2
                if firstblk:
                    P.dma(wq[s], wview(dr["w_qkv"][j], 0, KC, col0, 128), writes=[b_wq[s]], q="pool")
                x0, n = blks[jb]
                h0 = hcol(x0)
                bank = i % 3
                for k in range(KC):
                    MM(ps[:, bank, 0:n], wq[s][:, k, :], hT[:, k, h0:h0 + n], k == 0, k == KC - 1, [b_wq[s], b_h[jb]], [PB[bank]])
                ACT(sqb[i % 3][:, 0:n], ps[:, bank, 0:n], AF.Square, [PB[bank]], [b_sqb[i % 3]])

            def stB(i, it):
                hi, jb, firstblk = it
                col0, gcol, dst, dstbufs, _ = heads[hi]
                x0, n = blks[jb]
                bank = i % 3
                sb = 3 if i % 2 == 0 else 6
                MM(ps[:, sb, 0:n], ones_bf, sqb[i % 3][:, 0:n], True, True, [b_ones, b_sqb[i % 3]], [PB[sb]])
                ACT(lnv[i % 2][:, 0:n], ps[:, sb, 0:n], AF.Ln, [PB[sb], b_cbias], [b_ln[i % 2]], bias=cbias[:, 1:2])
                ACT(rstd[i % 2][:, 0:n], lnv[i % 2][:, 0:n], AF.Exp, [b_ln[i % 2]], [b_rs[i % 2]], scale=-0.5)
                if x0 >= NL:
                    STT(dst[:, x0:x0 + n], ps[:, bank, 0:n], qkg[:, gcol:gcol + 1], rstd[i % 2][:, 0:n], ALU.mult, ALU.mult,
                        [PB[bank], b_qkg, b_rs[i % 2]], [dstbufs[jb]])
                else:
                    STT(qn[i % 3][:, 0:n], ps[:, bank, 0:n], qkg[:, gcol:gcol + 1], rstd[i % 2][:, 0:n], ALU.mult, ALU.mult,
                        [PB[bank], b_qkg, b_rs[i % 2]], [b_qn[i % 3]])

            def stC(i, it):
                hi, jb, firstblk = it
                col0, gcol, dst, dstbufs, _ = heads[hi]
                x0, n = blks[jb]
                if x0 >= NL:
                    return
                rb = 4 + i % 2
                MM(ps[:, rb, 0:n], rperm, qn[i % 3][:, 0:n], True, True, [b_rperm, b_qn[i % 3]], [PB[rb]])
                TT("pool", t1[i % 2][:, 0:n], qn[i % 3][:, 0:n], ropec[:, x0:x0 + n], ALU.mult, [b_qn[i % 3], b_ropec], [b_t1[i % 2]])
                TT("dve", t2[i % 2][:, 0:n], ps[:, rb, 0:n], ropes[:, x0:x0 + n], ALU.mult, [PB[rb], b_ropes], [b_t2[i % 2]])
                TT("pool", dst[:, x0:x0 + n], t1[i % 2][:, 0:n], t2[i % 2][:, 0:n], ALU.add, [b_t1[i % 2], b_t2[i % 2]], [dstbufs[jb]])

            pipeline(items, [stA, stB, stC])
            hcount[0] += len(heads)

            aitems = []
            for jb in qblist:
                x0_, n_ = blks[jb]
                ktiles = list(range(18)) if x0_ < NL else [16, 17]
                for h in range(4):
                    for ki, kt in enumerate(ktiles):
                        aitems.append((jb, h, ki, kt, len(ktiles)))
            pend = []

            def emit_wo(jb):
                x0, n = blks[jb]
                row = b if x0 < NL else 2
                for dc in range(KC):
                    bank = 2 + dc % 2
                    for h in range(4):
                        MM(ps[:, bank, 0:n], wo[:, h, dc * 128:(dc + 1) * 128], oT[:, h, 0:n], h == 0, h == 3, [b_wo, b_o[h]], [PB[bank]])
                    STT(xT[:, dc, x0:x0 + n], ps[:, bank, 0:n], modv[:, l, row, 2, dc:dc + 1], xT[:, dc, x0:x0 + n],
                        ALU.mult, ALU.add, [PB[bank], b_modv, b_x[dc][jb]], [b_x[dc][jb]])

            def stS(i, it):
                jb, h, ki, kt, nk = it
                x0, n = blks[jb]
                kjb = min(kt // 4, 4)
                MM(ps[:, i % 2, 0:n], kT[:, kt * 128:(kt + 1) * 128], qT[:, h, x0:x0 + n], True, True,
                   [b_k[kjb], b_q[h][jb]], [PB[i % 2]])
                ACT(pT[i % 3][:, 0:n], ps[:, i % 2, 0:n], AF.Exp, [PB[i % 2]], [b_pT[i % 3]], scale=1.0 / math.sqrt(128.0))

            def stPV(i, it):
                jb, h, ki, kt, nk = it
                x0, n = blks[jb]
                kjb = min(kt // 4, 4)
                ob = 4 + h % 2
                db = 6 + h % 2
                MM(ps[:, ob, 0:n], V[:, kt, :], pT[i % 3][:, 0:n], ki == 0, ki == nk - 1, [b_v[kjb], b_pT[i % 3]], [PB[ob]])
                MM(ps[:, db, 0:n], ones_bf, pT[i % 3][:, 0:n], ki == 0, ki == nk - 1, [b_ones, b_pT[i % 3]], [PB[db]])
                if ki == nk - 1:
                    while pend and pend[0][1] != jb:
                        emit_wo(pend.pop(0)[1])
                    RECIP(rden[h % 2][:, 0:n], ps[:, db, 0:n], [PB[db]], [b_rd[h % 2]])
                    TT("dve", oT[:, h, 0:n], ps[:, ob, 0:n], rden[h % 2][:, 0:n], ALU.mult, [PB[ob], b_rd[h % 2]], [b_o[h]])
                    if h == 3:
                        pend.append((i + 3, jb))
                while pend and pend[0][0] <= i:
                    emit_wo(pend.pop(0)[1])

            pipeline(aitems, [stS, None, stPV])
            while pend:
                emit_wo(pend.pop(0)[1])
        A.release()

    def gmlp_phase(l, b, with_ctx):
        i = l // 2
        A.mark()
        wvv = [A.alloc([KC, 512], BF16) for _ in range(2)]; b_wvv = [P.dbuf(f"gwv{s}") for s in range(2)]
        wuo = [A.alloc([KC, 256], BF16) for _ in range(4)]; b_wuo = [P.dbuf(f"gwuo{s}") for s in range(4)]
        lng = A.alloc([D], F32); lnb = A.alloc([D], F32); b_lng = P.dbuf("lng"); b_lnb = P.dbuf("lnb")
        wsT = A.alloc([8, 128], BF16); b_ws = P.dbuf("wsT")
        bsb = A.alloc([8, 128], F32); b_bs = P.dbuf("bsb")
        P.dma(wvv[0], wview(dr["w_in"][i], 0, KC, 1024, 512), writes=[b_wvv[0]], q="pool")
        P.dma(wvv[1], wview(dr["w_in"][i], 0, KC, 1536, 512), writes=[b_wvv[1]], q="pool")
        P.dma(lng, dr["ln_g"][i].partition_broadcast(128), writes=[b_lng])
        P.dma(lnb, dr["ln_b"][i].partition_broadcast(128), writes=[b_lnb])
        P.dma(wsT, dr["w_sT"][i], writes=[b_ws], q="pool")
        P.dma(bsb.rearrange("p g q -> p (g q)"), dr["b_s"][i].partition_broadcast(128), writes=[b_bs])
        vg = [A.alloc([D], F32) for _ in range(4)]; b_vg = [Buf(f"vg{c}") for c in range(4)]
        junk = A.alloc([D], BF16); b_junk = Buf("junk")
        st = A.alloc([8, 4], F32); b_st = Buf("st")
        vnb = [A.alloc([D], BF16) for _ in range(2)]; b_vnb = [Buf("vnb0"), Buf("vnb1")]
        svt = [A.alloc([4, 128], F32) for _ in range(2)]; b_svt = [Buf("svt0"), Buf("svt1")]
        ug = A.alloc([8, 512], BF16); b_ug = Buf("ug")
        ya = A.alloc([8, 512], BF16); b_ya = [Buf(f"ya{c}") for c in range(4)]
        blks = TBLK if with_ctx else TBLK[:4]
        sched = []
        for jb in range(len(blks)):
            for t_ in range(4):
                sched.append((dr["w_in"][i], t_ * 256))
            for t_ in range(4):
                sched.append((dr["w_out"][i], t_ * 256))
        issued = [0]

        def need(idx):
            while issued[0] <= min(idx + 2, len(sched) - 1):
                src, c0 = sched[issued[0]]
                s_ = issued[0] % 4
                P.dma(wuo[s_], wview(src, 0, KC, c0, 256), writes=[b_wuo[s_]], q="pool")
                issued[0] += 1
            return idx % 4

        ti = 0
        for jb, (x0, n) in enumerate(blks):
            row = b if x0 < NL else 2
            nch = n // 128
            h0 = hcol(x0)
            MSET("dve", st[:, 0:2, :], 0.0, [b_st])
            for ci in range(nch):
                p_ = ci % 2
                hh = hcol(x0 + ci * 128)
                for hf in range(2):
                    for k in range(KC):
                        MM(ps[:, 2 * p_ + hf, :], hT[:, k, hh:hh + 128], wvv[hf][:, k, :], k == 0, k == KC - 1,
                           [b_wvv[hf], b_h[jb]], [PB[2 * p_ + hf]])
                ACT(vg[ci].rearrange("p (a b) -> p a b", a=2), ps[:, 2 * p_:2 * p_ + 2, :], AF.Gelu,
                    [PB[2 * p_], PB[2 * p_ + 1]], [b_vg[ci], b_st], accum=st[:, 0, ci:ci + 1])
                ACT(junk, vg[ci], AF.Square, [b_vg[ci]], [b_junk, b_st], accum=st[:, 1, ci:ci + 1])
            for gp in range(4):
                s_ = need(ti); ti += 1
                for gg in range(2):
                    g = gp * 2 + gg
                    bank = 6 + g % 2
                    for k in range(KC):
                        MM(ps[:, bank, 0:n], wuo[s_][:, k, gg * 128:(gg + 1) * 128], hT[:, k, h0:h0 + n], k == 0, k == KC - 1,
                           [b_wuo[s_], b_h[jb]], [PB[bank]])
                    ACT(ug[:, g, 0:n], ps[:, bank, 0:n], AF.Gelu, [PB[bank]], [b_ug])
            TS(st[:, 2, :], st[:, 0, :], 1.0 / D, 0.0, ALU.mult, ALU.add, [b_st], [b_st])
            TT("dve", st[:, 3, :], st[:, 2, :], st[:, 2, :], ALU.mult, [b_st], [b_st])
            STT(st[:, 4, :], st[:, 1, :], 1.0 / D, st[:, 3, :], ALU.mult, ALU.subtract, [b_st], [b_st])
            ACT(st[:, 5, :], st[:, 4, :], AF.Ln, [b_st, b_cbias], [b_st], bias=cbias[:, 2:3])
            ACT(st[:, 6, :], st[:, 5, :], AF.Exp, [b_st], [b_st], scale=-0.5)
            STT(st[:, 7, :], st[:, 2, :], -1.0, st[:, 6, :], ALU.mult, ALU.mult, [b_st], [b_st])
            for ci in range(nch):
                p_ = ci % 2
                ACT(vg[ci], vg[ci], AF.Identity, [b_vg[ci], b_st], [b_vg[ci]], bias=st[:, 7, ci:ci + 1], scale=st[:, 6, ci:ci + 1])
                TT("dve", vg[ci], vg[ci], lng, ALU.mult, [b_vg[ci], b_lng], [b_vg[ci]])
                TT("dve", vnb[p_], vg[ci], lnb, ALU.add, [b_vg[ci], b_lnb], [b_vnb[p_]])
                for g in range(8):
                    bank = 4 + (g // 4)
                    MM(ps[:, bank, (g % 4) * 128:(g % 4 + 1) * 128], vnb[p_][:, g * 128:(g + 1) * 128], wsT[:, g, :], True, True,
                       [b_vnb[p_], b_ws], [PB[bank]])
                for hf in range(2):
                    TT("dve", svt[hf], ps[:, 4 + hf, :].rearrange("p (g q) -> p g q", g=4), bsb[:, hf * 4:(hf + 1) * 4, :], ALU.add,
                       [PB[4 + hf], b_bs], [b_svt[hf]])
                    TT("pool", ya[:, hf * 4:(hf + 1) * 4, ci * 128:(ci + 1) * 128], ug[:, hf * 4:(hf + 1) * 4, ci * 128:(ci + 1) * 128],
                       svt[hf], ALU.mult, [b_ug, b_svt[hf]], [b_ya[ci]])
            for dp in range(4):
                s_ = need(ti); ti += 1
                for dd in range(2):
                    dc = dp * 2 + dd
                    bank = 6 + dc % 2
                    for g in range(8):
                        MM(ps[:, bank, 0:n], wuo[s_][:, g, dd * 128:(dd + 1) * 128], ya[:, g, 0:n], g == 0, g == 7,
                           [b_wuo[s_]] + b_ya[0:nch], [PB[bank]])
                    STT(xT[:, dc, x0:x0 + n], ps[:, bank, 0:n], modv[:, l, row, 2, dc:dc + 1], xT[:, dc, x0:x0 + n],
                        ALU.mult, ALU.add, [PB[bank], b_modv, b_x[dc][jb]], [b_x[dc][jb]])
        A.release()

    def hyena_phase(l, b, tag):
        i = l // 2
        n = NL if tag == "L" else NCX
        xbase = 0 if tag == "L" else NL
        row = b if tag == "L" else 2
        nt = n // 128
        nf = nt + 1
        hsc = Hs[(i, tag)]
        b_hsc = b_Hs[(i, tag)]
        A.mark()
        wB = [A.alloc([KC, CB], BF16) for _ in range(2)]; b_wB = [P.dbuf(f"hwB{s}") for s in range(2)]
        Q0 = [A.alloc([CB], BF16) for _ in range(3)]; b_Q0 = [Buf(f"Q0_{s}") for s in range(3)]
        Q2 = [A.alloc([CB], BF16) for _ in range(3)]; b_Q2 = [Buf(f"Q2_{s}") for s in range(3)]
        C1 = [A.alloc([CB], F32) for _ in range(3)]; b_C1 = [Buf(f"C1_{s}") for s in range(3)]
        taps = [A.alloc([4, CB], F32) for _ in range(2)]; b_taps = [P.dbuf(f"taps{s}") for s in range(2)]
        b_tapsb = [P.dbuf(f"tapsb{s}") for s in range(2)]
        skp = A.alloc([2, CB], F32); b_skp = P.dbuf("skp")
        zb = A.alloc([nt, CB], BF16); b_zb = [Buf(f"zb{t}") for t in range(nt)]
        Yr = A.alloc([nf, CB], BF16); Yi = A.alloc([nf, CB], BF16)
        b_Y = [Buf(f"Y{f}") for f in range(nf)]
        zT = A.alloc([CB // 128, n], BF16); b_zT = [Buf(f"zT{j}") for j in range((n + 511) // 512)]
        tct = [A.alloc([nf, 128], BF16) for _ in range(2)]; b_tc = [P.dbuf(f"tc{s}") for s in range(2)]
        tst = [A.alloc([nf, 128], BF16) for _ in range(2)]; b_ts = [P.dbuf(f"ts{s}") for s in range(2)]
        Ht = [A.alloc([2, CB], F32) for _ in range(2)]; b_Ht = [P.dbuf(f"Ht{s}") for s in range(2)]
        wo = A.alloc([CB // 128, D], BF16); b_wo = P.dbuf("hwo")
        e1 = A.alloc([CB], F32); b_e1 = Buf("e1")
        e2 = A.alloc([CB], F32); b_e2 = Buf("e2")
        e3 = A.alloc([CB], F32); b_e3 = Buf("e3")
        e4 = A.alloc([CB], F32); b_e4 = Buf("e4")
        z2 = [A.alloc([CB], BF16) for _ in range(2)]; b_z2 = [Buf("z2a"), Buf("z2b")]
        cnt = {"w": 0, "t": 0, "h": 0, "o": 0, "z": 0}

        def prep_weights(cb, third):
            s = cnt["w"] % 2; cnt["w"] += 1
            col0 = third * D + cb * CB
            P.dma(wB[s], wview(dr["w_in"][i], 0, KC, 2 * D + col0, CB), writes=[b_wB[s]], q="pool")
            P.dma(taps[s][:, 0:3, :], dr["conv_w"][i][:, col0:col0 + CB].partition_broadcast(128), writes=[b_taps[s]])
            P.dma(taps[s][:, 3, :], dr["conv_b"][i][col0:col0 + CB].partition_broadcast(128), writes=[b_tapsb[s]])
            return s

        def proj_stage(s, tc):
            x0 = xbase + tc * 128
            h0 = hcol(x0)
            jb = min(x0 // 512, 4)
            pbank = tc % 2
            sl = tc % 3
            for k in range(KC):
                MM(ps[:, pbank, 0:CB], hT[:, k, h0:h0 + 128], wB[s][:, k, :], k == 0, k == KC - 1, [b_wB[s], b_h[jb]], [PB[pbank]])
            TT("dve", Q0[sl], ps[:, pbank, 0:CB], taps[s][:, 0, :], ALU.mult, [PB[pbank], b_taps[s]], [b_Q0[sl]])
            TT("dve", Q2[sl], ps[:, pbank, 0:CB], taps[s][:, 2, :], ALU.mult, [PB[pbank], b_taps[s]], [b_Q2[sl]])
            TT("dve", C1[sl], ps[:, pbank, 0:CB], taps[s][:, 1, :], ALU.mult, [PB[pbank], b_taps[s]], [b_C1[sl]])
            TT("pool", C1[sl], C1[sl], taps[s][:, 3, :], ALU.add, [b_C1[sl], b_tapsb[s]], [b_C1[sl]])

        def shift_stage(tc, bank):
            sl = tc % 3
            ops = [(0, Q0[sl], b_Q0[sl]), (2, Q2[sl], b_Q2[sl])]
            if tc > 0:
                ops.append((1, Q0[(tc - 1) % 3], b_Q0[(tc - 1) % 3]))
            if tc < nt - 1:
                ops.append((3, Q2[(tc + 1) % 3], b_Q2[(tc + 1) % 3]))
            for oi, (m, q, bq) in enumerate(ops):
                MM(ps[:, bank, 0:CB], shiftm[:, m, :], q, oi == 0, oi == len(ops) - 1, [b_shiftm, bq], [PB[bank]])

        def fwd_dft(o, cb, deferred=()):
            deferred = list(deferred)
            for fc in range(nf):
                for _ in range(3):
                    if deferred:
                        deferred.pop(0)()
                nyq = fc == nf - 1
                s = cnt["t"] % 2; cnt["t"] += 1
                P.dma(tct[s], dr["tcos" + tag][fc], writes=[b_tc[s]])
                if not nyq:
                    P.dma(tst[s], dr["tsin" + tag][fc], writes=[b_ts[s]])
                hs_ = cnt["h"] % 2; cnt["h"] += 1
                P.dma(Ht[hs_], hsc[o, fc, cb], reads=[b_hsc], writes=[b_Ht[hs_]])
                m = 1 if nyq else 128
                bank = 2 * (fc % 2)
                for tc in range(nt):
                    MM(ps[0:m, bank, 0:CB], tct[s][:, tc, 0:m], zb[:, tc, :], tc == 0, tc == nt - 1, [b_tc[s], b_zb[tc]], [PB[bank]])
                if nyq:
                    MSET("pool", Yr[:, fc, :], 0.0, [b_Y[fc]])
                    MSET("pool", Yi[:, fc, :], 0.0, [b_Y[fc]])
                    TT("dve", Yr[0:1, fc, :], ps[0:1, bank, 0:CB], Ht[hs_][0:1, 0, :], ALU.mult, [PB[bank], b_Ht[hs_]], [b_Y[fc]])
                    continue
                for tc in range(nt):
                    MM(ps[:, bank + 1, 0:CB], tst[s][:, tc, :], zb[:, tc, :], tc == 0, tc == nt - 1, [b_ts[s], b_zb[tc]], [PB[bank + 1]])
                TT("dve", e1, ps[:, bank, 0:CB], Ht[hs_][:, 0, :], ALU.mult, [PB[bank], b_Ht[hs_]], [b_e1])
                TT("dve", e2, ps[:, bank + 1, 0:CB], Ht[hs_][:, 1, :], ALU.mult, [PB[bank + 1], b_Ht[hs_]], [b_e2])
                TT("pool", Yr[:, fc, :], e1, e2, ALU.add, [b_e1, b_e2], [b_Y[fc]])
                TT("dve", e3, ps[:, bank + 1, 0:CB], Ht[hs_][:, 0, :], ALU.mult, [PB[bank + 1], b_Ht[hs_]], [b_e3])
                TT("dve", e4, ps[:, bank, 0:CB], Ht[hs_][:, 1, :], ALU.mult, [PB[bank], b_Ht[hs_]], [b_e4])
                TT("pool", Yi[:, fc, :], e3, e4, ALU.subtract, [b_e3, b_e4], [b_Y[fc]])
            while deferred:
                deferred.pop(0)()

        def inv_dft_gate(o, cb, sx, final):
            pending = []
            proj_stage(sx, 0)
            for tc in range(nt):
                if tc + 1 < nt:
                    proj_stage(sx, tc + 1)
                s = cnt["t"] % 2; cnt["t"] += 1
                P.dma(tct[s], dr["tcos" + tag][tc], writes=[b_tc[s]])
                P.dma(tst[s], dr["tsin" + tag][tc], writes=[b_ts[s]])
                bank = 4 + 2 * (tc % 2)
                for fc in range(nf):
                    kk = 1 if fc == nf - 1 else 128
                    MM(ps[:, bank, 0:CB], tct[s][0:kk, fc, :], Yr[0:kk, fc, :], fc == 0, False, [b_tc[s], b_Y[fc]], [PB[bank]])
                for fc in range(nf - 1):
                    MM(ps[:, bank, 0:CB], tst[s][:, fc, :], Yi[:, fc, :], False, fc == nf - 2, [b_ts[s], b_Y[fc]], [PB[bank]])
                shift_stage(tc, bank + 1)
                TT("pool", e1, zb[:, tc, :], skp[:, o, :], ALU.mult, [b_zb[tc], b_skp], [b_e1])
                TT("dve", e2, ps[:, bank, 0:CB], e1, ALU.add, [PB[bank], b_e1], [b_e2])
                TT("dve", e3, ps[:, bank + 1, 0:CB], C1[tc % 3], ALU.add, [PB[bank + 1], b_C1[tc % 3]], [b_e3])
                if not final:
                    TT("pool", zb[:, tc, :], e2, e3, ALU.mult, [b_e2, b_e3], [b_zb[tc]])
                else:
                    zs = cnt["z"] % 2; cnt["z"] += 1
                    TT("pool", z2[zs], e2, e3, ALU.mult, [b_e2, b_e3], [b_z2[zs]])

                    def trans(zs=zs, tc=tc):
                        psb = ps[:, 2 + zs, :].bitcast(BF16)
                        for cc in range(CB // 128):
                            TR(psb[:, cc * 128:(cc + 1) * 128], z2[zs][:, cc * 128:(cc + 1) * 128], ident, [b_z2[zs], b_ident], [PB[2 + zs]])
                        ACT(zT[:, :, tc * 128:(tc + 1) * 128], psb[:, 0:CB].rearrange("p (c t) -> p c t", c=CB // 128), AF.Copy,
                            [PB[2 + zs]], [b_zT[tc // 4]])
                    pending.append(trans)
                if final and len(pending) > 1:
                    pending.pop(0)()
            while pending:
                pending.pop(0)()

        for cb in range(NCB):
            P.dma(skp, dr["skip"][i][:, cb * CB:(cb + 1) * CB].partition_broadcast(128), writes=[b_skp])
            P.dma(wo, wview(dr["w_out"][i], D + cb * CB, CB // 128, 0, D), writes=[b_wo], q="pool")
            sv_ = prep_weights(cb, 0)
            sx1 = prep_weights(cb, 1)
            proj_stage(sv_, 0)
            for tc in range(nt):
                if tc + 1 < nt:
                    proj_stage(sv_, tc + 1)
                bank = 6 + tc % 2
                shift_stage(tc, bank)
                TT("dve", zb[:, tc, :], ps[:, bank, 0:CB], C1[tc % 3], ALU.add, [PB[bank], b_C1[tc % 3]], [b_zb[tc]])
            fwd_dft(0, cb)
            inv_dft_gate(0, cb, sx1, False)
            sx2 = prep_weights(cb, 2)
            fwd_dft(1, cb)
            inv_dft_gate(1, cb, sx2, True)
            for dc in range(KC):
                for jb in range((n + 511) // 512):
                    t0 = jb * 512
                    nn = min(512, n - t0)
                    x0 = xbase + t0
                    xjb = min(x0 // 512, 4)
                    bank = (dc * 4 + jb) % 2
                    for cc in range(CB // 128):
                        MM(ps[:, bank, 0:nn], wo[:, cc, dc * 128:(dc + 1) * 128], zT[:, cc, t0:t0 + nn], cc == 0, cc == CB // 128 - 1,
                           [b_wo, b_zT[jb]], [PB[bank]])
                    STT(xT[:, dc, x0:x0 + nn], ps[:, bank, 0:nn], modv[:, l, row, 2, dc:dc + 1], xT[:, dc, x0:x0 + nn],
                        ALU.mult, ALU.add, [PB[bank], b_modv, b_x[dc][xjb]], [b_x[dc][xjb]])
        A.release()

    def final_phase(b):
        A.mark()
        sq = [A.alloc([512], BF16) for _ in range(2)]; b_sq = [Buf("fsq0"), Buf("fsq1")]
        lnv = A.alloc([512], F32); b_ln = Buf("flnv")
        rstd = A.alloc([512], F32); b_rs = Buf("frstd")
        ot = [A.alloc([512], F32) for _ in range(4)]; b_ot = [Buf(f"fo{s}") for s in range(4)]
        it = 0
        for j, (x0, n) in enumerate(TBLK[:4]):
            bank = j % 2
            for c in range(KC):
                s = it % 2; it += 1
                if final_norm:
                    ACT(sq[s][:, 0:n], xT[:, c, x0:x0 + n], AF.Square, [b_x[c][j]], [b_sq[s]])
                    MM(ps[:, bank, 0:n], ones_bf, sq[s][:, 0:n], c == 0, c == KC - 1, [b_ones, b_sq[s]], [PB[bank]])
            if final_norm:
                ACT(lnv[:, 0:n], ps[:, bank, 0:n], AF.Ln, [PB[bank], b_cbias], [b_ln], bias=cbias[:, 0:1])
                ACT(rstd[:, 0:n], lnv[:, 0:n], AF.Exp, [b_ln], [b_rs], scale=-0.5)
            for c in range(KC):
                s = it % 4; it += 1
                if final_norm:
                    STT(ot[s][:, 0:n], xT[:, c, x0:x0 + n], fing[:, c:c + 1], rstd[:, 0:n], ALU.mult, ALU.mult,
                        [b_x[c][j], b_fing, b_rs], [b_ot[s]])
                    P.dma(outT[b, c * 128:(c + 1) * 128, x0:x0 + n], ot[s][:, 0:n], reads=[b_ot[s]], writes=[P.dbuf(f"out{s}")])
                else:
                    P.dma(outT[b, c * 128:(c + 1) * 128, x0:x0 + n], xT[:, c, x0:x0 + n], reads=[b_x[c][j]], writes=[P.dbuf(f"out{s}")])
        A.release()

    for b in range(nbatch):
        for c in range(KC):
            P.dma(xT[:, c, 0:NL], dr["xT"][b, c * 128:(c + 1) * 128, :], writes=[P.dbuf("xld")] + b_x[c][0:4])
            P.dma(xT[:, c, NL:T], dr["ctxT"][b, c * 128:(c + 1) * 128, :], writes=[P.dbuf("xld2"), b_x[c][4]])
        for l in layers:
            last = l == DEPTH - 1
            even = l % 2 == 0
            with_ctx = not last
            norm_phase(l, 1, b, True if not even else with_ctx)
            P.barrier()
            if even:
                gmlp_phase(l, b, with_ctx)
                P.barrier()
                hyena_phase(l, b, "L")
                P.barrier()
                if with_ctx:
                    hyena_phase(l, b, "C")
                    P.barrier()
            else:
                attn_phase(l, b, last)
                P.barrier()
            ffn_phase(l, b, with_ctx)
            P.barrier()
        final_phase(b)
        P.barrier()
    P.emit()
    return nc, A.peak


_PROG_CACHE = {}


def _pp(v):
    v = np.asarray(v, dtype=np.float32)
    lead = v.shape[:-1]
    r = v.reshape(lead + (KC, 128))
    r = np.moveaxis(r, -1, 0)
    return np.ascontiguousarray(r)


def _shared_inputs(inp):
    C = _consts()
    m = dict(C)
    f32 = lambda a: np.ascontiguousarray(np.asarray(a, dtype=np.float32))
    m["mod_w"] = f32(inp["mod_w"])
    mb = np.asarray(inp["mod_b"], dtype=np.float32).reshape(DEPTH, 6, KC, 128)
    m["modb"] = np.ascontiguousarray(mb.transpose(3, 0, 1, 2))
    m["n1g"] = _pp(inp["norm1_g"])
    m["n2g"] = _pp(inp["norm2_g"])
    m["fing"] = _pp(inp["final_g"])
    wgu = np.asarray(inp["ffn_w_gu"], dtype=np.float32)
    g = wgu[:, :, :DFF].reshape(DEPTH, D, FCH, 1, 128)
    u = wgu[:, :, DFF:].reshape(DEPTH, D, FCH, 1, 128)
    m["w_gu"] = np.ascontiguousarray(np.concatenate([g, u], axis=3).reshape(DEPTH, D, 2 * DFF))
    m["w_down"] = f32(inp["ffn_w_down"])
    m["w_in"] = f32(inp["even_w_in"])
    m["ln_g"] = f32(inp["gmlp_ln_g"])
    m["ln_b"] = f32(inp["gmlp_ln_b"])
    ws = np.asarray(inp["gmlp_w_s"], dtype=np.float32)
    m["w_sT"] = np.ascontiguousarray(ws.transpose(0, 3, 1, 2))
    m["b_s"] = f32(np.asarray(inp["gmlp_b_s"], dtype=np.float32).reshape(2, 8 * 128))
    m["conv_w"] = f32(inp["hyena_conv_w"])
    m["conv_b"] = f32(inp["hyena_conv_b"])
    m["f_w1"] = f32(inp["hyena_f_w1"])
    fr = np.asarray(inp["hyena_freq"], dtype=np.float32)
    m["f_pp"] = np.ascontiguousarray(np.stack([fr[:, 0], np.asarray(inp["hyena_f_b1"], np.float32),
                                               fr[:, 1], np.asarray(inp["hyena_f_b2"], np.float32)], axis=-1))
    m["f_w2"] = f32(inp["hyena_f_w2"])
    m["f_w3"] = f32(inp["hyena_f_w3"])
    m["skip"] = f32(inp["hyena_skip"])
    m["w_out"] = f32(inp["even_w_out"])
    m["w_qkv"] = f32(inp["attn_w_qkv"])
    m["qkg"] = np.ascontiguousarray(np.stack([np.asarray(inp["attn_q_g"], np.float32),
                                              np.asarray(inp["attn_k_g"], np.float32)], axis=-1))
    m["w_o"] = f32(inp["attn_w_o"])
    return m


def _core_inputs(inp, shared, core):
    b0 = 2 * core
    m = dict(shared)
    x = np.asarray(inp["x"], dtype=np.float32)[b0:b0 + 2]
    ctx = np.asarray(inp["ctx"], dtype=np.float32)[b0:b0 + 2]
    m["xT"] = np.ascontiguousarray(x.transpose(0, 2, 1))
    m["ctxT"] = np.ascontiguousarray(ctx.transpose(0, 2, 1))
    cv = np.concatenate([np.asarray(inp["c"], np.float32)[b0:b0 + 2], np.asarray(inp["c_ctx"], np.float32)[None, :]], axis=0)
    m["cvec"] = np.ascontiguousarray(cv.T.reshape(KC, 128, 3).transpose(1, 0, 2))
    return m


def kernel(**inputs):
    key = "full"
    if key not in _PROG_CACHE:
        _PROG_CACHE[key] = build()[0]
    nc = _PROG_CACHE[key]
    shared = _shared_inputs(inputs)
    in_maps = [_core_inputs(inputs, shared, c) for c in range(8)]
    res = run_bass_kernel_spmd(nc, in_maps, core_ids=list(range(8)))
    outs = [np.asarray(r["outT"]).transpose(0, 2, 1) for r in res.results]
    return np.ascontiguousarray(np.concatenate(outs, axis=0).astype(np.float32))
```
